# Optimizing a Trainium2 kernel written in Bass

```python
import math
import jax, jax.numpy as jnp
from jax import lax
import numpy as np


D_MODEL = 1024
BATCH = 2
SEQ = 16384
DEPTH = 2

D_MIX = D_MODEL
HEAD_DIM = 64
W_A = D_MIX // 4
W_B = D_MIX // 4
W_C = D_MIX - W_A - W_B
N_HEADS_A = W_A // HEAD_DIM
N_Q_HEADS = W_C // HEAD_DIM
GQA_GROUP = 4
N_KV_HEADS = N_Q_HEADS // GQA_GROUP
KV_W = N_KV_HEADS * HEAD_DIM
CHUNK = 128
CONV_WIDTH = 31
CONV_PAD = CONV_WIDTH // 2
WINDOW = 128
BLOCK = 128
N_BUCKETS = 32
MAX_DISTANCE = 128
LN_EPS = 1e-5
NEG_INF = -1e30
DEEPNORM_ALPHA = (2 * DEPTH) ** 0.25
DEEPNORM_BETA = (8 * DEPTH) ** -0.25
SPLITS = (W_A, W_A, W_A,
          W_B, W_B, W_B,
          W_C, KV_W, KV_W, W_C)
D_IN = 3 * W_A + 3 * W_B + 2 * W_C + 2 * KV_W

kernel_name = 'hybrid_gmlp_conformer_swa_deepnorm'


def layer_norm(x, g, b):
    xf = x.astype(jnp.float32)
    mu = jnp.mean(xf, axis=-1, keepdims=True)
    var = jnp.mean(jnp.square(xf - mu), axis=-1, keepdims=True)
    y = (xf - mu) * lax.rsqrt(var + LN_EPS)
    return (y * g.astype(jnp.float32) + b.astype(jnp.float32)).astype(x.dtype)


def t5_bucket(rel):
    nb = N_BUCKETS // 2
    max_exact = nb // 2
    ret = jnp.where(rel > 0, nb, 0)
    n = jnp.abs(rel)
    nf = jnp.maximum(n, 1).astype(jnp.float32)
    large = max_exact + (jnp.log(nf / max_exact) / math.log(MAX_DISTANCE / max_exact)
                         * (nb - max_exact)).astype(jnp.int32)
    large = jnp.minimum(large, nb - 1)
    return ret + jnp.where(n < max_exact, n, large)


def band_geometry(seq):
    nblk = seq // BLOCK
    qq = jnp.arange(BLOCK)[:, None]
    kk = jnp.arange(3 * BLOCK)[None, :]
    rel = kk - BLOCK - qq
    in_window = jnp.abs(rel) <= WINDOW
    key_pos = jnp.arange(nblk)[:, None] * BLOCK - BLOCK + jnp.arange(3 * BLOCK)[None, :]
    valid = (key_pos >= 0) & (key_pos < seq)
    mask = in_window[None] & valid[:, None, :]
    return rel, mask


def mixer_spatial_gating(u, v, gate, ln_g, ln_b, w_s, b_s):
    bsz, seq, _ = u.shape
    u = jax.nn.gelu(u)
    v = layer_norm(jax.nn.gelu(v), ln_g, ln_b)
    v = v.reshape(bsz, seq // CHUNK, CHUNK, N_HEADS_A, HEAD_DIM)
    v = jnp.einsum('hij,bcjhd->bcihd', w_s, v) + b_s.T[None, None, :, :, None]
    return u * v.reshape(bsz, seq, W_A) * jax.nn.silu(gate)


def mixer_conformer_conv(a, b, gate, conv_w, conv_b, ln_g, ln_b):
    y = a * jax.nn.sigmoid(b)
    y = lax.conv_general_dilated(
        y, conv_w[:, None, :], window_strides=(1,), padding=[(CONV_PAD, CONV_PAD)],
        dimension_numbers=('NWC', 'WIO', 'NWC'), feature_group_count=W_B) + conv_b
    y = layer_norm(y, ln_g, ln_b)
    return jax.nn.silu(y) * jax.nn.silu(gate)


def mixer_window_attention(q, k, v, gate, sink, bias, mask):
    bsz, seq, _ = q.shape
    nblk = seq // BLOCK
    q = q.reshape(bsz, nblk, BLOCK, N_KV_HEADS, GQA_GROUP, HEAD_DIM)

    def band(t):
        t = t.reshape(bsz, seq, N_KV_HEADS, HEAD_DIM)
        t = jnp.pad(t, ((0, 0), (BLOCK, BLOCK), (0, 0), (0, 0)))
        t = t.reshape(bsz, nblk + 2, BLOCK, N_KV_HEADS, HEAD_DIM)
        return jnp.concatenate([t[:, :nblk], t[:, 1:nblk + 1], t[:, 2:]], axis=2)

    kb, vb = band(k), band(v)
    s = jnp.einsum('bnqkgd,bnskd->bnkgqs', q, kb).astype(jnp.float32) * (HEAD_DIM ** -0.5)
    s = jnp.where(mask[None, :, None, None], s + bias, NEG_INF)
    sk = sink.astype(jnp.float32).reshape(1, 1, N_KV_HEADS, GQA_GROUP, 1, 1)
    m = jnp.maximum(jnp.max(s, axis=-1, keepdims=True), sk)
    p = jnp.exp(s - m)
    p = p / (jnp.sum(p, axis=-1, keepdims=True) + jnp.exp(sk - m))
    o = jnp.einsum('bnkgqs,bnskd->bnqkgd', p.astype(vb.dtype), vb)
    return o.reshape(bsz, seq, W_C) * jax.nn.silu(gate)


def setup_inputs(seed: int = 0) -> dict:
    key = jax.random.key(seed)
    ks = jax.random.split(key, 20)
    f32 = jnp.float32
    nrm = lambda k, shape: jax.random.normal(k, shape, f32)
    return {
        'x': nrm(ks[0], (BATCH, SEQ, D_MODEL)),
        'ln_in_g': 1.0 + 0.05 * nrm(ks[1], (D_MODEL,)),
        'ln_in_b': 0.02 * nrm(ks[2], (D_MODEL,)),
        'w_in': nrm(ks[3], (DEPTH, D_MODEL, D_IN)) * D_MODEL ** -0.5,
        'gmlp_ln_g': 1.0 + 0.05 * nrm(ks[4], (DEPTH, W_A)),
        'gmlp_ln_b': 0.02 * nrm(ks[5], (DEPTH, W_A)),
        'w_spatial': nrm(ks[6], (DEPTH, N_HEADS_A, CHUNK, CHUNK)) * CHUNK ** -0.5,
        'b_spatial': 1.0 + 0.1 * nrm(ks[7], (DEPTH, N_HEADS_A, CHUNK)),
        'conv_w': nrm(ks[8], (DEPTH, CONV_WIDTH, W_B)) * CONV_WIDTH ** -0.5,
        'conv_b': 0.02 * nrm(ks[9], (DEPTH, W_B)),
        'conv_ln_g': 1.0 + 0.05 * nrm(ks[10], (DEPTH, W_B)),
        'conv_ln_b': 0.02 * nrm(ks[11], (DEPTH, W_B)),
        'attn_sink': 0.5 * nrm(ks[12], (DEPTH, N_Q_HEADS)),
        'rel_bias': 0.5 * nrm(ks[13], (N_BUCKETS, N_Q_HEADS)),
        'w_out': nrm(ks[14], (DEPTH, D_MIX, D_MODEL)) * (D_MIX ** -0.5) * DEEPNORM_BETA,
        'post_ln_g': 1.0 + 0.05 * nrm(ks[15], (DEPTH, D_MODEL)),
        'post_ln_b': 0.02 * nrm(ks[16], (DEPTH, D_MODEL)),
    }


def reference(x, ln_in_g, ln_in_b, w_in, gmlp_ln_g, gmlp_ln_b, w_spatial, b_spatial,
              conv_w, conv_b, conv_ln_g, conv_ln_b, attn_sink, rel_bias, w_out,
              post_ln_g, post_ln_b):
    seq = x.shape[1]
    split_points = np.cumsum(SPLITS)[:-1].tolist()
    rel, mask = band_geometry(seq)
    bias = rel_bias.astype(jnp.float32)[t5_bucket(rel)]
    bias = jnp.transpose(bias, (2, 0, 1)).reshape(N_KV_HEADS, GQA_GROUP, BLOCK, 3 * BLOCK)

    x = layer_norm(x, ln_in_g, ln_in_b)
    for l in range(DEPTH):
        h = jnp.einsum('bsd,de->bse', x, w_in[l])
        au, av, ag, ba, bb, bg, cq, ck, cv, cg = jnp.split(h, split_points, axis=-1)
        ya = mixer_spatial_gating(au, av, ag, gmlp_ln_g[l], gmlp_ln_b[l],
                                  w_spatial[l], b_spatial[l])
        yb = mixer_conformer_conv(ba, bb, bg, conv_w[l], conv_b[l],
                                  conv_ln_g[l], conv_ln_b[l])
        yc = mixer_window_attention(cq, ck, cv, cg, attn_sink[l], bias, mask)
        y = jnp.einsum('bse,ed->bsd', jnp.concatenate([ya, yb, yc], axis=-1), w_out[l])
        x = layer_norm(DEEPNORM_ALPHA * x + y, post_ln_g[l], post_ln_b[l])
    return x
```

```python
import math
from contextlib import ExitStack

import numpy as np
import jax
import jax.numpy as jnp
import concourse.bass as bass
import concourse.mybir as mybir
from concourse.bass_utils import run_bass_kernel_spmd

F32 = mybir.dt.float32
BF16 = mybir.dt.bfloat16
AF = mybir.ActivationFunctionType
ALU = mybir.AluOpType

D = 1024
SEQ = 16384
NCORE = 8
TOK_CORE = 4096
NE = 36
DIN = 2816
ALPHA = 4 ** 0.25
EPS = 1e-5
GA = 0.044715
GC = 0.7978845608028654
DEPTH = 2
import os
SS = os.environ.get("KSS", "act,dve,pool").split(",")
ORDER = os.environ.get("KORDER", "mimimiml")
NSLOW = int(os.environ.get("KNSLOW", "3"))
NXR = 5
STEP_ORDER = os.environ.get("KSTEP", "m.sc0 m.sc1 i.tr m.sc2 m.gconv d.0 m.conv1 i.j4 d.1 m.conv2 m.pv d.2 m.conv3 m.gmlp i.j2 d.3 i.j3 i.j1pe m.mixtr i.j5 m.outproj i.j1ev i.av i.j6")

_o = dict(au=0, av=256, ag=512, ba=768, bb=1024, bg=1280, cq=1536, ck=2048, cv=2176, cg=2304)
_perm = []
_perm += list(range(_o['au'], _o['au'] + 256))
_perm += list(range(_o['ag'], _o['ag'] + 256))
_perm += list(range(_o['ba'], _o['ba'] + 256))
_perm += list(range(_o['bb'], _o['bb'] + 256))
for _c in range(4):
    _perm += list(range(_o['cq'] + _c * 64, _o['cq'] + _c * 64 + 64))
    _perm += list(range(_o['cq'] + (_c + 4) * 64, _o['cq'] + (_c + 4) * 64 + 64))
_perm += list(range(_o['ck'], _o['ck'] + 128))
T_AV = len(_perm); _perm += list(range(_o['av'], _o['av'] + 256))
T_BG = len(_perm); _perm += list(range(_o['bg'], _o['bg'] + 256))
T_CV = len(_perm); _perm += list(range(_o['cv'], _o['cv'] + 128))
T_CG = len(_perm); _perm += list(range(_o['cg'], _o['cg'] + 512))
PERM = np.array(_perm)
assert len(PERM) == DIN and len(set(_perm)) == DIN


class Eng:
    def __init__(self, name, h, sem, inc=1, selfsync=False):
        self.name, self.h, self.sem, self.inc, self.selfsync = name, h, sem, inc, selfsync
        self.count = 0
        self.waited = {}


class Buf:
    __slots__ = ("w", "r")

    def __init__(self):
        self.w = None
        self.r = {}


class Tracker:
    def __init__(self):
        self.bufs = {}

    def buf(self, name):
        b = self.bufs.get(name)
        if b is None:
            b = self.bufs[name] = Buf()
        return b

    def _deps(self, reads, writes):
        need = {}
        for n in reads:
            ev = self.buf(n).w
            if ev is not None and need.get(ev[0], 0) < ev[1]:
                need[ev[0]] = ev[1]
        for n in writes:
            b = self.buf(n)
            if b.w is not None and need.get(b.w[0], 0) < b.w[1]:
                need[b.w[0]] = b.w[1]
            for e, t in b.r.items():
                if need.get(e, 0) < t:
                    need[e] = t
        return need

    def _wait(self, eng, need):
        for e, t in need.items():
            if e is eng and not eng.selfsync:
                continue
            if eng.waited.get(e, 0) >= t:
                continue
            assert t <= e.count, f"{eng.name} needs unissued tick {t} of {e.name} ({e.count})"
            eng.h.wait_ge(e.sem, t * e.inc)
            eng.waited[e] = t
            if e.inc == 16 and t > getattr(e, "max_wait", 0):
                e.max_wait = t

    def _record(self, ev_eng, tick, reads, writes):
        for n in reads:
            b = self.buf(n)
            if b.r.get(ev_eng, 0) < tick:
                b.r[ev_eng] = tick
        for n in writes:
            b = self.buf(n)
            b.w = (ev_eng, tick)
            b.r = {}

    def op(self, eng, reads, writes, fn):
        self._wait(eng, self._deps(reads, writes))
        inst = fn()
        eng.count += 1
        inst.then_inc(eng.sem, 1)
        self._record(eng, eng.count, reads, writes)

    def dma(self, q, stream, out, in_, reads, writes, **kw):
        self._wait(q, self._deps(reads, writes))
        mw = getattr(stream, "max_wait", 0)
        if mw > q.waited.get(stream, 0):
            q.h.wait_ge(stream.sem, mw * stream.inc)
            q.waited[stream] = mw
        inst = q.h.dma_start(out=out, in_=in_, **kw)
        stream.count += 1
        inst.then_inc(stream.sem, 16)
        self._record(stream, stream.count, reads, writes)

    def set_writer_latest(self, names, stream):
        for n in names:
            self.buf(n).w = (stream, stream.count)

    def wait_all(self, eng, streams):
        for s in streams:
            if s.count > 0 and eng.waited.get(s, 0) < s.count:
                eng.h.wait_ge(s.sem, s.count * s.inc)
                eng.waited[s] = s.count


def build_program(nlayers=DEPTH, debug=False):
    nc = bass.Bass("TRN2", target_bir_lowering=False)
    es = ExitStack()

    def sb(name, shape, dt=F32):
        return es.enter_context(nc.sbuf_tensor(name, shape, dt))

    def sem(name):
        return es.enter_context(nc.semaphore(name))

    def din(name, shape):
        return nc.dram_tensor(name, shape, F32, kind="ExternalInput")

    T = Tracker()
    PE = Eng("pe", nc.tensor, sem("s_pe"))
    ACT = Eng("act", nc.scalar, sem("s_act"), selfsync=("act" in SS))
    DVE = Eng("dve", nc.vector, sem("s_dve"), selfsync=("dve" in SS))
    POOL = Eng("pool", nc.gpsimd, sem("s_pool"), selfsync=("pool" in SS))
    SP = Eng("sp", nc.sync, sem("s_sp"))
    LD = [Eng(f"ld{i}", None, sem(f"s_ld{i}"), inc=16) for i in range(NXR)]
    ST = [Eng(f"st{i}", None, sem(f"s_st{i}"), inc=16) for i in range(2)]
    LDW = Eng("ldw", None, sem("s_ldw"), inc=16)
    LDS = [Eng(f"lds{i}", None, sem(f"s_lds{i}"), inc=16) for i in range(2)]
    LDC = Eng("ldc", None, sem("s_ldc"), inc=16)
    LDT = Eng("ldt", None, sem("s_ldt"), inc=16)

    x_d = din("x", [NE * 128, D])
    valid_d = din("valid", [1, NE])
    w_in_d = din("w_in", [DEPTH, D, DIN])
    w_out_d = din("w_out", [DEPTH, D, D])
    ln_in_g_d = din("ln_in_g", [1, D]); ln_in_b_d = din("ln_in_b", [1, D])
    post_g_d = din("post_g", [DEPTH, D]); post_b_d = din("post_b", [DEPTH, D])
    gm_g_d = din("gm_g", [DEPTH, 128, 2]); gm_b_d = din("gm_b", [DEPTH, 256])
    wsT_d = din("wsT", [DEPTH, 128, 4, 128]); b_sp_d = din("b_sp", [DEPTH, 4, 128])
    conv_wT_d = din("conv_wT", [DEPTH, 128, 2, 31]); conv_b_d = din("conv_b", [DEPTH, 256])
    conv_g_d = din("conv_g", [DEPTH, 256]); conv_bb_d = din("conv_bb", [DEPTH, 256])
    sink_d = din("sink", [DEPTH, 8]); relb_d = din("rel_bias", [32, 8]); oh_d = din("oh", [33, 512])
    ident_d = din("ident", [128, 128]); jflip_d = din("jflip", [128, 128])
    out_d = nc.dram_tensor("out", [TOK_CORE, D], F32, kind="ExternalOutput")
    x1s_d = nc.dram_tensor("x1s", [NE * 128, D], F32, kind="ExternalOutput") if debug else nc.dram_tensor("x1s", [NE * 128, D], F32)
    fv_d = nc.dram_tensor("fvs", [8, 512], F32)

    win = sb("win", [128, 8, DIN], BF16)
    wout = sb("wout", [128, 8, D], BF16)
    dg = sb("dg", [128, 2, 31, 128], BF16)
    XR = [sb(f"xr{i}", [128, D]) for i in range(NXR)]
    Z = [sb(f"z{i}", [128, D]) for i in range(2)]
    xT = sb("xT", [128, D], BF16)
    mixT = sb("mixT", [128, D], BF16)
    Gin = sb("Gin", [128, D]); Bin = sb("Bin", [128, D]); Gp = sb("Gp", [128, D]); Bp = sb("Bp", [128, D])
    expb = sb("expb", [128, 3, 8, 128], BF16)
    PT = sb("PT", [128, 3, 8, 128], BF16)
    GUS = [sb(f"gus{i}", [128, 256]) for i in range(3)]
    VH = [sb(f"vh{i}", [128, 256], BF16) for i in range(3)]
    SG2B = [sb(f"sg2b{i}", [128, 256]) for i in range(3)]
    QT = [sb(f"qt{i}", [128, 512], BF16) for i in range(3)]
    SG2C = [sb(f"sg2c{i}", [128, 512]) for i in range(3)]
    KT = [sb(f"kt{i}", [128, 128], BF16) for i in range(4)]
    VA = [sb(f"va{i}", [128, 2, 65], BF16) for i in range(4)]
    YT = sb("YT", [128, 2, 4 * 128 + 30], BF16)
    tA1 = sb("tA1", [128, 256]); tA2 = sb("tA2", [128, 256]); tA3 = sb("tA3", [128, 256]); tA4 = sb("tA4", [128, 256])
    tB1 = sb("tB1", [128, 256]); tB2 = sb("tB2", [128, 256]); tC1 = sb("tC1", [128, 512])
    tSG = sb("tSG", [128, 256]); tCN = sb("tCN", [128, 256]); tCT = sb("tCT", [128, 256])
    On = sb("On", [128, 512]); YB = sb("YB", [128, 256], BF16); YC = sb("YC", [128, 512], BF16)
    ident_f = sb("ident_f", [128, 128]); ident_b = sb("ident_b", [128, 128], BF16); jflip = sb("jflip_sb", [128, 128])
    cwT = sb("cwT", [128, 2, 31]); wsT = sb("wsT_sb", [128, 4, 128], BF16); LBb = sb("LBb", [128, 256], BF16)
    bsT = sb("bsT", [128, 2, 128]); Cst = sb("Cst", [128, 2, 128]); gmg = sb("gmg", [128, 2])
    Gc = sb("Gc", [128, 256]); Bc = sb("Bc", [128, 256]); cbrow = sb("cbrow", [1, 256], BF16); ones_b = sb("ones_b", [1, 128], BF16)
    esink = sb("esink", [128, 8]); vcol = sb("vcol", [128, NE]); onescol = sb("onescol", [128, 2, 1])
    RB = sb("RB", [33, 8]); OHt = sb("OHt", [33, 512]); fvs = sb("fvs_sb", [8, 512]); hank = sb("hank", [128, 8, 128])
    mhalf = sb("mhalf", [128, 1])
    stg = [sb(f"stg{i}", [128, 1408]) for i in range(2)]
    den = sb("den", [128, 8]); rc = sb("rc", [128, 8])
    lnt = {}
    for tag in ("lnin", "lngm", "lncv", "lnpo"):
        lnt[tag] = dict(st=sb(tag + "_st", [128, 12]), mv=sb(tag + "_mv", [128, 2]), ve=sb(tag + "_ve", [128, 1]),
                        rs=sb(tag + "_rs", [128, 1]), nm=sb(tag + "_nm", [128, 1]))

    banks = [es.enter_context(nc.psum_tensor(f"bank{i}", [128, 512], F32)) for i in range(8)]
    bstate = [0]

    sstate = [0]

    def newbank(slow=False):
        if slow:
            i = sstate[0] % NSLOW
            sstate[0] += 1
        else:
            i = NSLOW + bstate[0] % (8 - NSLOW)
            bstate[0] += 1
        return banks[i], f"bank{i}"

    V, G, A_, P_ = nc.vector, nc.gpsimd, nc.scalar, nc.tensor

    def bc_rows(dram, row, n):
        return bass.AP(dram, row * n, [[0, 128], [1, n]])

    T.dma(SP, LDC, ident_f[:], ident_d.ap()[:, :], [], ["ident_f"])
    T.dma(SP, LDC, vcol[:], bc_rows(valid_d, 0, NE), [], ["vcol"])
    T.dma(SP, LDC, Gin[:], bc_rows(ln_in_g_d, 0, D), [], ["Gin"])
    T.dma(SP, LDC, Bin[:], bc_rows(ln_in_b_d, 0, D), [], ["Bin"])
    T.set_writer_latest(["ident_f", "vcol", "Gin", "Bin"], LDC)
    T.op(DVE, ["ident_f"], ["ident_b"], lambda: V.tensor_copy(out=ident_b[:], in_=ident_f[:]))
    T.op(DVE, [], ["mhalf"], lambda: V.memset(mhalf[:], -0.5))
    T.op(DVE, [], ["onescol"], lambda: V.memset(onescol[:], 1.0))
    T.op(DVE, [], ["ones_b"], lambda: V.memset(ones_b[:], 1.0))
    T.op(POOL, [], ["YT"], lambda: G.memset(YT[:], 0.0))

    def setup_bias():
        T.dma(SP, LDC, jflip[:], jflip_d.ap()[:, :], [], ["jflip"])
        T.dma(SP, LDC, OHt[:], oh_d.ap()[:, :], [], ["OHt"])
        T.op(DVE, [], ["RB"], lambda: V.memset(RB[:], 1.0))
        T.dma(SP, LDC, RB[0:32, :], relb_d.ap()[:, :], [], ["RB"])
        T.set_writer_latest(["jflip", "OHt", "RB"], LDC)
        bk, bn = newbank()
        T.op(PE, ["RB", "OHt"], [bn], lambda: P_.matmul(bk[0:8, :], lhsT=RB[0:33, 0:8], rhs=OHt[0:33, :], start=True, stop=True))
        T.op(DVE, [bn], ["fvs"], lambda: V.tensor_copy(out=fvs[:], in_=bk[0:8, :]))
        T.dma(POOL, LDT, fv_d.ap()[:, :], fvs[:], ["fvs"], ["fv_d"])
        for kc in range(3):
            T.dma(POOL, LDT, hank[:], bass.AP(fv_d, 256 - kc * 128, [[1, 128], [512, 8], [1, 128]]), ["fv_d"], ["hank"])
            for hh in range(2):
                bk, bn = newbank()
                T.op(PE, ["hank", "jflip"], [bn],
                     lambda: P_.matmul(bk[:, :], lhsT=jflip[:], rhs=hank[:, hh * 4:(hh + 1) * 4, :], start=True, stop=True))
                T.op(ACT, [bn], ["expb"],
                     lambda: A_.activation(out=expb[:, kc, hh * 4:(hh + 1) * 4, :].rearrange("p h q -> p (h q)"), in_=bk[:, :], func=AF.Copy, scale=8.0))


    WIN_NAMES = [f"win{k}" for k in range(16)]
    WOUT_NAMES = [f"wout{k}" for k in range(8)]
    stg_i = [0]
    cast_mode = ["dve"]

    def staged_cast(dst_ap, src_ap, dstname, n):
        i = stg_i[0] % 2
        stg_i[0] += 1
        T.dma(SP, LDS[i], stg[i][:, 0:n], src_ap, [], [f"stg{i}"])
        if i == 0 and cast_mode[0] == "alt":
            T.op(ACT, [f"stg{i}"], [dstname], lambda: A_.activation(out=dst_ap, in_=stg[i][:, 0:n], func=AF.Copy))
        else:
            T.op(DVE, [f"stg{i}"], [dstname], lambda: V.tensor_copy(out=dst_ap, in_=stg[i][:, 0:n]))

    pending = []

    def pump(n=1):
        for _ in range(n):
            if pending:
                pending.pop(0)()

    def queue_win(l):
        for kc in range(8):
            for hf in range(2):
                pending.append(lambda kc=kc, hf=hf: staged_cast(
                    win[:, kc, hf * 1408:(hf + 1) * 1408], w_in_d.ap()[l, kc * 128:(kc + 1) * 128, hf * 1408:(hf + 1) * 1408],
                    WIN_NAMES[kc * 2 + hf], 1408))

    def queue_wout(l):
        for kc in range(8):
            pending.append(lambda kc=kc: staged_cast(wout[:, kc, :], w_out_d.ap()[l, kc * 128:(kc + 1) * 128, :], WOUT_NAMES[kc], 1024))

    def load_layer(l):
        T.dma(POOL, LDW, wsT[:], wsT_d.ap()[l, :, :, :], [], ["wsT"])
        T.dma(POOL, LDW, LBb[:], bc_rows(gm_b_d, l, 256), [], ["LBb"])
        T.dma(POOL, LDW, cbrow[:], conv_b_d.ap()[l:l + 1, :], [], ["cbrow"])
        T.set_writer_latest(["wsT", "LBb", "cbrow"], LDW)
        T.dma(SP, LDC, cwT[:], conv_wT_d.ap()[l, :, :, :], [], ["cwT"])
        T.dma(SP, LDC, gmg[:], gm_g_d.ap()[l, :, :], [], ["gmg"])
        for h in range(4):
            T.dma(SP, LDC, bsT[(h % 2) * 64:(h % 2 + 1) * 64, h // 2, :],
                  bass.AP(b_sp_d, (l * 4 + h) * 128, [[0, 64], [1, 128]]), [], ["bsT"])
        T.dma(SP, LDC, Gc[:], bc_rows(conv_g_d, l, 256), [], ["Gc"])
        T.dma(SP, LDC, Bc[:], bc_rows(conv_bb_d, l, 256), [], ["Bc"])
        T.dma(SP, LDC, esink[:], bc_rows(sink_d, l, 8), [], ["esink"])
        T.dma(SP, LDC, Gp[:], bc_rows(post_g_d, l, D), [], ["Gp"])
        T.dma(SP, LDC, Bp[:], bc_rows(post_b_d, l, D), [], ["Bp"])
        T.set_writer_latest(["cwT", "gmg", "bsT", "Gc", "Bc", "esink", "Gp", "Bp"], LDC)
        T.op(ACT, ["esink"], ["esink"], lambda: A_.activation(out=esink[:], in_=esink[:], func=AF.Exp))
        T.op(DVE, ["gmg"], ["gmg"], lambda: V.tensor_scalar(out=gmg[:], in0=gmg[:], scalar1=0.25, scalar2=None, op0=ALU.mult))
        def dg_piece(ch, k0, k1):
            for k in range(k0, k1):
                T.op(POOL, ["ident_f", "cwT"], ["dg"],
                     lambda: G.tensor_scalar(out=dg[:, ch, k, :], in0=ident_f[:], scalar1=cwT[:, ch, k:k + 1], scalar2=0.5,
                                             op0=ALU.mult, op1=ALU.mult))
        for ch in range(2):
            for k0 in range(0, 31, 8):
                pending.append(lambda ch=ch, k0=k0: dg_piece(ch, k0, min(k0 + 8, 31)))
        bk, bn = newbank()

        def f():
            for h in range(4):
                ins = P_.matmul(bk[(h % 2) * 64:(h % 2 + 1) * 64, (h // 2) * 128:(h // 2 + 1) * 128],
                                lhsT=LBb[:, h * 64:(h + 1) * 64], rhs=wsT[:, h, :], start=True, stop=True)
            return ins
        T.op(PE, ["LBb", "wsT"], [bn], f)
        T.op(DVE, [bn, "bsT"], ["Cst"], lambda: V.tensor_tensor(out=Cst[:].rearrange("p c i -> p (c i)"), in0=bk[:, 0:256],
                                                               in1=bsT[:].rearrange("p c i -> p (c i)"), op=ALU.add))
        T.op(DVE, ["Cst"], ["Cst"], lambda: V.tensor_scalar(out=Cst[:], in0=Cst[:], scalar1=0.25, scalar2=None, op0=ALU.mult))

    def ln_a(tag, aps, srcbufs):
        t = lnt[tag]
        for i, ap in enumerate(aps):
            T.op(DVE, srcbufs, [tag + "st"], lambda: V.bn_stats(out=t["st"][:, 6 * i:6 * i + 6], in_=ap))
        k = len(aps)
        T.op(DVE, [tag + "st"], [tag + "mv"], lambda: V.bn_aggr(out=t["mv"][:, 0:2], in_=t["st"][:, 0:6 * k]))

    def ln_b(tag, eps):
        t = lnt[tag]
        T.op(POOL, [tag + "mv"], [tag + "ve"], lambda: G.tensor_scalar(out=t["ve"][:], in0=t["mv"][:, 1:2], scalar1=eps, scalar2=None, op0=ALU.add))
        T.op(POOL, [tag + "ve", "mhalf"], [tag + "rs"], lambda: G.tensor_tensor(out=t["rs"][:], in0=t["ve"][:], in1=mhalf[:], op=ALU.pow))
        T.op(POOL, [tag + "mv", tag + "rs"], [tag + "nm"],
             lambda: G.tensor_scalar(out=t["nm"][:], in0=t["mv"][:, 0:1], scalar1=t["rs"][:], scalar2=-1.0, op0=ALU.mult, op1=ALU.mult))
        return t["rs"], t["nm"], [tag + "rs", tag + "nm"]

    def issue_load(l, b):
        X = XR[b % NXR]
        if l == 0:
            T.dma(SP, LD[b % NXR], X[:], x_d.ap()[b * 128:(b + 1) * 128, :], [], [f"xr{b % NXR}"])
        else:
            T.dma(SP, LD[b % NXR], X[:], x1s_d.ap()[b * 128:(b + 1) * 128, :], [f"x1s{b}"], [f"xr{b % NXR}"])

    def lnin_pieces(l, b):
        X = XR[b % NXR]; xb = f"xr{b % NXR}"
        st = {}

        def a():
            ln_a("lnin", [X[:, 0:512], X[:, 512:1024]], [xb])

        def bb():
            rs, nm, lb = ln_b("lnin", EPS)
            T.op(ACT, [xb] + lb, [xb], lambda: A_.activation(out=X[:], in_=X[:], func=AF.Identity, scale=rs[:], bias=nm[:]))

        def c():
            T.op(DVE, [xb, "Gin"], [xb], lambda: V.tensor_tensor(out=X[:], in0=X[:], in1=Gin[:], op=ALU.mult))

        def d():
            T.op(POOL, [xb, "Bin"], [xb], lambda: G.tensor_tensor(out=X[:], in0=X[:], in1=Bin[:], op=ALU.add))
        if l != 0:
            return [lambda: None] * 4
        return [a, bb, c, d]

    def lnin_stage(l, b):
        for fn in lnin_pieces(l, b):
            fn()

    def input_pieces(l, b):
        X = XR[b % NXR]; xb = f"xr{b % NXR}"
        s2 = b % 3; r4 = b % 4
        xr = WIN_NAMES + ["xT0", "xT1"]
        st = {}

        def fm(bk, col, ci):
            for kc in range(8):
                ins = P_.matmul(bk[:, col:col + 128], lhsT=win[:, kc, ci * 128:(ci + 1) * 128], rhs=xT[:, kc * 128:(kc + 1) * 128],
                                start=(kc == 0), stop=(kc == 7))
            return ins

        def tm(bk, col, w0, n):
            for kc in range(8):
                ins = P_.matmul(bk[:, col:col + n], lhsT=xT[:, kc * 128:(kc + 1) * 128], rhs=win[:, kc, w0:w0 + n],
                                start=(kc == 0), stop=(kc == 7))
            return ins

        def p_tr():
            for half in range(2):
                bk, bn = newbank()

                def f():
                    for i in range(4):
                        kc = half * 4 + i
                        ins = P_.transpose(out=bk[:, i * 128:(i + 1) * 128], in_=X[:, kc * 128:(kc + 1) * 128], identity=ident_f[:])
                    return ins
                T.op(PE, [xb, "ident_f"], [bn], f)
                T.op(ACT, [bn], [f"xT{half}"], lambda: A_.activation(out=xT[:, half * 512:(half + 1) * 512], in_=bk[:, :], func=AF.Copy))

        def p_j4():
            b4, n4 = newbank(slow=True)
            st["b4"], st["n4"] = b4, n4

            def f4():
                fm(b4, 0, 12)
                tm(b4, 128, T_CV, 128)
                return tm(b4, 256, T_AV, 256)
            T.op(PE, xr, [n4], f4)
            T.op(DVE, [n4], [f"kt{r4}"], lambda: V.tensor_copy(out=KT[r4][:], in_=b4[:, 0:128]))
            T.op(DVE, [n4, "vcol"], [f"va{r4}"],
                 lambda: V.tensor_scalar(out=VA[r4][:, :, 0:64], in0=b4[:, 128:256].rearrange("p (g d) -> p g d", g=2),
                                         scalar1=vcol[:, b:b + 1], scalar2=None, op0=ALU.mult))
            T.op(DVE, ["onescol", "vcol"], [f"va{r4}"],
                 lambda: V.tensor_scalar(out=VA[r4][:, :, 64:65], in0=onescol[:], scalar1=vcol[:, b:b + 1], scalar2=None, op0=ALU.mult))

        def p_j2():
            b2, n2 = newbank()

            def f2():
                for i in range(4):
                    ins = fm(b2, i * 128, 4 + i)
                return ins
            T.op(PE, xr, [n2], f2)
            T.op(ACT, [n2], ["tB1"], lambda: A_.activation(out=tB1[:], in_=b2[:, 256:512], func=AF.Tanh, scale=0.5))
            c0 = 15 + r4 * 128
            T.op(DVE, ["tB1", n2], [f"YT{r4}"],
                 lambda: V.scalar_tensor_tensor(out=YT[:, :, c0:c0 + 128], in0=tB1[:].rearrange("p (c t) -> p c t", c=2), scalar=1.0,
                                                in1=b2[:, 0:256].rearrange("p (c t) -> p c t", c=2), op0=ALU.add, op1=ALU.mult))
            if b in (0, 1, NE - 2, NE - 1):
                T.op(POOL, [f"YT{r4}", "vcol"], [f"YT{r4}"],
                     lambda: G.tensor_scalar(out=YT[:, :, c0:c0 + 128], in0=YT[:, :, c0:c0 + 128], scalar1=vcol[:, b:b + 1], scalar2=None, op0=ALU.mult))
            if r4 == 3:
                T.op(POOL, ["YT3"], ["YTL"], lambda: G.tensor_copy(out=YT[:, :, 0:15], in_=YT[:, :, 512:527]))
            if r4 == 0:
                T.op(POOL, ["YT0"], ["YTR"], lambda: G.tensor_copy(out=YT[:, :, 527:542], in_=YT[:, :, 15:30]))

        def p_j3():
            b3, n3 = newbank()

            def f3():
                for i in range(4):
                    ins = fm(b3, i * 128, 8 + i)
                return ins
            T.op(PE, xr, [n3], f3)
            T.op(ACT, [n3], [f"qt{s2}"], lambda: A_.activation(out=QT[s2][:], in_=b3[:, :], func=AF.Copy))

        def p_j1pe():
            b1, n1 = newbank(slow=True)
            st["b1"], st["n1"] = b1, n1

            def f1():
                for i in range(4):
                    ins = fm(b1, i * 128, i)
                return ins
            T.op(PE, xr, [n1], f1)

        def p_j1ev():
            b1, n1 = st["b1"], st["n1"]
            au = b1[:, 0:256]; ag = b1[:, 256:512]
            T.op(ACT, [n1], ["tA1"], lambda: A_.activation(out=tA1[:], in_=au, func=AF.Square, scale=math.sqrt(GA)))
            T.op(ACT, [n1], ["tA2"], lambda: A_.activation(out=tA2[:], in_=ag, func=AF.Tanh, scale=0.5))
            T.op(DVE, ["tA1", n1], ["tA1"], lambda: V.scalar_tensor_tensor(out=tA1[:], in0=tA1[:], scalar=1.0, in1=au, op0=ALU.add, op1=ALU.mult))
            T.op(DVE, ["tA2", n1], ["tA2"], lambda: V.scalar_tensor_tensor(out=tA2[:], in0=tA2[:], scalar=1.0, in1=ag, op0=ALU.add, op1=ALU.mult))
            T.op(ACT, ["tA1"], ["tA1"], lambda: A_.activation(out=tA1[:], in_=tA1[:], func=AF.Tanh, scale=GC))
            T.op(DVE, ["tA1", n1], ["tA1"], lambda: V.scalar_tensor_tensor(out=tA1[:], in0=tA1[:], scalar=1.0, in1=au, op0=ALU.add, op1=ALU.mult))
            T.op(POOL, ["tA1", "tA2"], [f"gus{s2}"], lambda: G.tensor_tensor(out=GUS[s2][:], in0=tA1[:], in1=tA2[:], op=ALU.mult))

        def p_av():
            b4, n4 = st["b4"], st["n4"]
            av = b4[:, 256:512]
            T.op(ACT, [n4], ["tA3"], lambda: A_.activation(out=tA3[:], in_=av, func=AF.Square, scale=math.sqrt(GA)))
            T.op(DVE, ["tA3", n4], ["tA3"], lambda: V.scalar_tensor_tensor(out=tA3[:], in0=tA3[:], scalar=1.0, in1=av, op0=ALU.add, op1=ALU.mult))
            T.op(ACT, ["tA3"], ["tA3"], lambda: A_.activation(out=tA3[:], in_=tA3[:], func=AF.Tanh, scale=GC))
            T.op(DVE, ["tA3", n4], ["tA4"], lambda: V.scalar_tensor_tensor(out=tA4[:], in0=tA3[:], scalar=1.0, in1=av, op0=ALU.add, op1=ALU.mult))
            ln_a("lngm", [tA4[:]], ["tA4"])
            rs, nm, lb = ln_b("lngm", 4.0 * EPS)
            T.op(POOL, ["tA4"] + lb, [f"vh{s2}"],
                 lambda: G.tensor_scalar(out=VH[s2][:], in0=tA4[:], scalar1=rs[:], scalar2=nm[:], op0=ALU.mult, op1=ALU.add))

        def p_j5():
            b5, n5 = newbank()
            T.op(PE, xr, [n5], lambda: tm(b5, 0, T_BG, 256))
            T.op(ACT, [n5], ["tB2"], lambda: A_.activation(out=tB2[:], in_=b5[:, 0:256], func=AF.Tanh, scale=0.5))
            T.op(DVE, ["tB2", n5], [f"sg2b{s2}"],
                 lambda: V.scalar_tensor_tensor(out=SG2B[s2][:], in0=tB2[:], scalar=1.0, in1=b5[:, 0:256], op0=ALU.add, op1=ALU.mult))

        def p_j6():
            b6, n6 = newbank()
            T.op(PE, xr, [n6], lambda: tm(b6, 0, T_CG, 512))
            T.op(ACT, [n6], ["tC1"], lambda: A_.activation(out=tC1[:], in_=b6[:, :], func=AF.Tanh, scale=0.5))
            T.op(DVE, ["tC1", n6], [f"sg2c{s2}"],
                 lambda: V.scalar_tensor_tensor(out=SG2C[s2][:], in0=tC1[:], scalar=1.0, in1=b6[:, :], op0=ALU.add, op1=ALU.mult))

        def p_j1():
            p_j1pe()
            p_j1ev()

        return dict(tr=p_tr, j4=p_j4, j2=p_j2, j3=p_j3, j1=p_j1, j1pe=p_j1pe, j1ev=p_j1ev, av=p_av, j5=p_j5, j6=p_j6)

    def mix_pieces(l, e, nl_last):
        s2 = e % 3; r4 = e % 4; z2 = e % 2
        X = XR[e % NXR]; xb = f"xr{e % NXR}"
        Zt = Z[z2]; zb = f"z{z2}"
        st = {}

        def p_gconv():
            bk, bn = newbank(slow=True)
            st["bk"], st["bn"] = bk, bn

            def fg():
                for h in range(4):
                    P_.matmul(bk[(h % 2) * 64:(h % 2 + 1) * 64, (h // 2) * 128:(h // 2 + 1) * 128],
                              lhsT=VH[s2][:, h * 64:(h + 1) * 64], rhs=wsT[:, h, :], start=True, stop=True)
                for ch in range(2):
                    P_.matmul(bk[:, 256 + ch * 128:256 + (ch + 1) * 128], lhsT=ones_b[0:1, 0:128], rhs=cbrow[0:1, ch * 128:(ch + 1) * 128],
                              start=True, stop=False)
                    for k in range(31):
                        ins = P_.matmul(bk[:, 256 + ch * 128:256 + (ch + 1) * 128], lhsT=YT[:, ch, r4 * 128 + k:r4 * 128 + k + 128],
                                        rhs=dg[:, ch, k, :], start=False, stop=(k == 30))
                return ins
            ytn = [f"YT{(e - 1) % 4}", f"YT{r4}", f"YT{(e + 1) % 4}"] + (["YTL"] if r4 == 0 else []) + (["YTR"] if r4 == 3 else [])
            T.op(PE, [f"vh{s2}", "wsT", "ones_b", "cbrow", "dg"] + ytn, [bn], fg)

        def p_scores(kcs=(0, 1, 2)):
            for kc in kcs:
                ks = (e - 1 + kc) % 4
                bb = [newbank() for _ in range(2)]

                def fs():
                    for g in range(2):
                        P_.matmul(bb[g][0][:, :], lhsT=KT[ks][g * 64:(g + 1) * 64, :], rhs=QT[s2][g * 64:(g + 1) * 64, :], start=True, stop=False)
                    for g in range(2):
                        ins = P_.matmul(bb[g][0][:, :], lhsT=ident_b[:], rhs=expb[:, kc, g * 4:(g + 1) * 4, :], start=False, stop=True)
                    return ins
                T.op(PE, [f"kt{ks}", f"qt{s2}", "ident_b", "expb"], [bb[0][1], bb[1][1]], fs)
                for g in range(2):
                    T.op(ACT, [bb[g][1]], [f"PT{g}"],
                         lambda: A_.activation(out=PT[:, kc, g * 4:(g + 1) * 4, :].rearrange("p h q -> p (h q)"), in_=bb[g][0][:, :], func=AF.Exp, scale=0.125))

        def p_conv1():
            bk, bn = st["bk"], st["bn"]
            cv = bk[:, 256:512]
            ln_a("lncv", [cv], [bn])
            rs, nm, lb = ln_b("lncv", EPS)
            T.op(ACT, [bn] + lb, ["tCN"], lambda: A_.activation(out=tCN[:], in_=cv, func=AF.Identity, scale=rs[:], bias=nm[:]))

        def p_conv2():
            T.op(DVE, ["tCN", "Gc"], ["tCN"], lambda: V.tensor_tensor(out=tCN[:], in0=tCN[:], in1=Gc[:], op=ALU.mult))
            T.op(DVE, ["tCN", "Bc"], ["tCN"], lambda: V.tensor_tensor(out=tCN[:], in0=tCN[:], in1=Bc[:], op=ALU.add))
            T.op(ACT, ["tCN"], ["tCT"], lambda: A_.activation(out=tCT[:], in_=tCN[:], func=AF.Tanh, scale=0.5))

        def p_conv3():
            T.op(DVE, ["tCT", "tCN"], ["tCT"], lambda: V.scalar_tensor_tensor(out=tCT[:], in0=tCT[:], scalar=1.0, in1=tCN[:], op0=ALU.add, op1=ALU.mult))
            T.op(DVE, ["tCT", f"sg2b{s2}"], ["YB"], lambda: V.scalar_tensor_tensor(out=YB[:], in0=tCT[:], scalar=0.25, in1=SG2B[s2][:], op0=ALU.mult, op1=ALU.mult))

        def p_gmlp():
            bk, bn = st["bk"], st["bn"]
            for ch in range(2):
                T.op(DVE, [bn, "gmg", "Cst"], ["tSG"],
                     lambda: V.scalar_tensor_tensor(out=tSG[:, ch * 128:(ch + 1) * 128], in0=bk[:, ch * 128:(ch + 1) * 128], scalar=gmg[:, ch:ch + 1],
                                                    in1=Cst[:, ch, :], op0=ALU.mult, op1=ALU.add))
            T.op(POOL, ["tSG", f"gus{s2}"], ["mixA"], lambda: G.tensor_tensor(out=mixT[:, 0:256], in0=tSG[:], in1=GUS[s2][:], op=ALU.mult))

        def p_pv():
            for g in range(2):
                bo, bon = newbank()
                po = bo[:, 0:260].rearrange("p (h c) -> p h c", h=4)

                def fpv():
                    for hh in range(4):
                        for kc in range(3):
                            ks = (e - 1 + kc) % 4
                            ins = P_.matmul(po[:, hh, :], lhsT=PT[:, kc, g * 4 + hh, :], rhs=VA[ks][:, g, :], start=(kc == 0), stop=(kc == 2))
                    return ins
                T.op(PE, [f"PT{g}"] + [f"va{(e - 1 + kc) % 4}" for kc in range(3)], [bon], fpv)
                T.op(DVE, [bon, "esink"], [f"den{g}"],
                     lambda: V.tensor_tensor(out=den[:, g * 4:(g + 1) * 4].unsqueeze(2), in0=po[:, :, 64:65],
                                             in1=esink[:, g * 4:(g + 1) * 4].unsqueeze(2), op=ALU.add))
                T.op(DVE, [f"den{g}"], [f"rc{g}"], lambda: V.reciprocal(out=rc[:, g * 4:(g + 1) * 4], in_=den[:, g * 4:(g + 1) * 4]))
                T.op(DVE, [bon, f"rc{g}"], [f"On{g}"],
                     lambda: V.tensor_tensor(out=On[:, g * 256:(g + 1) * 256].rearrange("p (h d) -> p h d", h=4), in0=po[:, :, 0:64],
                                             in1=rc[:, g * 4:(g + 1) * 4].unsqueeze(2).to_broadcast([128, 4, 64]), op=ALU.mult))
                T.op(POOL, [f"On{g}", f"sg2c{s2}"], [f"YC{g}"],
                     lambda: G.tensor_tensor(out=YC[:, g * 256:(g + 1) * 256], in0=On[:, g * 256:(g + 1) * 256],
                                             in1=SG2C[s2][:, g * 256:(g + 1) * 256], op=ALU.mult))

        def p_mixtr():
            bt, btn = newbank()
            btb = bt[:].bitcast(BF16)

            def ftr():
                for c in range(2):
                    P_.transpose(out=btb[:, c * 128:(c + 1) * 128], in_=YB[:, c * 128:(c + 1) * 128], identity=ident_b[:])
                for c in range(4):
                    ins = P_.transpose(out=btb[:, (2 + c) * 128:(3 + c) * 128], in_=YC[:, c * 128:(c + 1) * 128], identity=ident_b[:])
                return ins
            T.op(PE, ["YB", "YC0", "YC1", "ident_b"], [btn], ftr)
            T.op(ACT, [btn], ["mixB"], lambda: A_.activation(out=mixT[:, 256:512], in_=btb[:, 0:256], func=AF.Copy))
            T.op(ACT, [btn], ["mixC"], lambda: A_.activation(out=mixT[:, 512:1024], in_=btb[:, 256:768], func=AF.Copy, scale=0.5))

        def p_outproj():
            for n in range(2):
                by, byn = newbank()

                def fo():
                    for kc in range(8):
                        ins = P_.matmul(by[:, :], lhsT=mixT[:, kc * 128:(kc + 1) * 128], rhs=wout[:, kc, n * 512:(n + 1) * 512],
                                        start=(kc == 0), stop=(kc == 7))
                    return ins
                T.op(PE, ["mixA", "mixB", "mixC"] + WOUT_NAMES, [byn], fo)
                T.op(DVE, [byn, xb], [zb + f"h{n}"],
                     lambda: V.scalar_tensor_tensor(out=Zt[:, n * 512:(n + 1) * 512], in0=X[:, n * 512:(n + 1) * 512], scalar=ALPHA, in1=by[:, :],
                                                    op0=ALU.mult, op1=ALU.add))

        def p_postln_a():
            ln_a("lnpo", [Zt[:, 0:512], Zt[:, 512:1024]], [zb + "h0", zb + "h1"])

        def p_postln_b():
            rs, nm, lb = ln_b("lnpo", EPS)
            T.op(ACT, [zb + "h0", zb + "h1"] + lb, [zb], lambda: A_.activation(out=Zt[:], in_=Zt[:], func=AF.Identity, scale=rs[:], bias=nm[:]))

        def p_postln_c():
            T.op(DVE, [zb, "Gp"], [zb], lambda: V.tensor_tensor(out=Zt[:], in0=Zt[:], in1=Gp[:], op=ALU.mult))

        def p_postln_d():
            T.op(POOL, [zb, "Bp"], [zb], lambda: G.tensor_tensor(out=Zt[:], in0=Zt[:], in1=Bp[:], op=ALU.add))
            if not nl_last:
                T.dma(SP, ST[z2], x1s_d.ap()[e * 128:(e + 1) * 128, :], Zt[:], [zb], [f"x1s{e}", zb + "h0", zb + "h1"])
            else:
                r0 = (e - 2) * 128
                T.dma(SP, ST[z2], out_d.ap()[r0:r0 + 128, :], Zt[:], [zb], [zb + "h0", zb + "h1"])

        return dict(gconv=p_gconv, scores=p_scores, sc0=lambda: p_scores((0,)), sc1=lambda: p_scores((1,)), sc2=lambda: p_scores((2,)), conv1=p_conv1, conv2=p_conv2, conv3=p_conv3, gmlp=p_gmlp, pv=p_pv,
                    mixtr=p_mixtr, outproj=p_outproj, postln=[p_postln_a, p_postln_b, p_postln_c, p_postln_d])

    IN_ALL = ["tr", "j4", "j2", "j3", "j1", "av", "j5", "j6"]
    STEP = STEP_ORDER.split()

    def run_input(l, b):
        p = input_pieces(l, b)
        for k in IN_ALL:
            p[k]()
            pump(1)

    for l in range(nlayers):
        first_b, last_b = l, NE - 1 - l
        nl_last = (l == nlayers - 1 and nlayers == DEPTH)
        if l == 0:
            setup_bias()
        for b in range(first_b, first_b + NXR):
            issue_load(l, b)
        if l == 0:
            queue_win(0)
            pump(16)
            cast_mode[0] = "alt"
        for b in range(first_b, first_b + NXR):
            lnin_stage(l, b)
        queue_wout(l)
        run_input(l, first_b)
        load_layer(l)
        issue_load(l, first_b + NXR)
        for b in range(first_b + 1, first_b + 3):
            run_input(l, b)
        pump(len(pending))
        deferred = [[], [], [], []]
        for e in range(l + 1, NE - 1 - l):
            pi = input_pieces(l, e + 2) if e + 2 <= last_b else None
            pm = mix_pieces(l, e, nl_last)
            for tok in STEP:
                pump(1)
                kind, name = tok.split(".")
                if kind == "m":
                    pm[name]()
                elif kind == "i":
                    if pi is not None:
                        pi[name]()
                elif kind == "d":
                    k = int(name)
                    for fn in deferred[k]:
                        fn()
                    deferred[k] = []
            for k in range(4):
                deferred[k].append(pm["postln"][k])
            if e + 4 <= last_b and e + 4 > first_b + 4:
                lp = lnin_pieces(l, e + 4)
                for k in range(4):
                    deferred[k].append(lp[k])
            if e + NXR <= last_b and e + NXR > first_b + NXR:
                issue_load(l, e + NXR)
            if e + 2 == last_b and l + 1 < nlayers:
                queue_win(l + 1)
        for k in range(4):
            for fn in deferred[k]:
                fn()
        pump(len(pending))
    T.wait_all(SP, ST)
    return nc


_PROG = {}


def t5_bucket(rel):
    nb = 16
    max_exact = 8
    ret = jnp.where(rel > 0, nb, 0)
    n = jnp.abs(rel)
    nf = jnp.maximum(n, 1).astype(jnp.float32)
    large = max_exact + (jnp.log(nf / max_exact) / math.log(128 / max_exact) * (nb - max_exact)).astype(jnp.int32)
    large = jnp.minimum(large, nb - 1)
    return ret + jnp.where(n < max_exact, n, large)


def host_consts():
    rel = 255 - np.arange(512)
    with jax.default_device(jax.devices("cpu")[0]):
        bucket = np.asarray(t5_bucket(jnp.asarray(rel, dtype=jnp.int32)))
    oh = np.zeros((33, 512), np.float32)
    oh[bucket, np.arange(512)] = 1.0
    oh[32] = np.where(np.abs(rel) <= 128, 0.0, -30000.0)
    return oh


def make_in_maps(inputs):
    f = lambda a: np.ascontiguousarray(np.asarray(a, dtype=np.float32))
    x = f(inputs["x"])
    w_in = np.ascontiguousarray(f(inputs["w_in"])[:, :, PERM])
    common = dict(
        w_in=w_in, w_out=f(inputs["w_out"]),
        ln_in_g=f(inputs["ln_in_g"]).reshape(1, D), ln_in_b=f(inputs["ln_in_b"]).reshape(1, D),
        post_g=f(inputs["post_ln_g"]), post_b=f(inputs["post_ln_b"]),
        gm_g=np.ascontiguousarray(f(inputs["gmlp_ln_g"]).reshape(DEPTH, 2, 128).transpose(0, 2, 1)),
        gm_b=f(inputs["gmlp_ln_b"]),
        wsT=np.ascontiguousarray(f(inputs["w_spatial"]).transpose(0, 3, 1, 2)),
        b_sp=f(inputs["b_spatial"]),
        conv_wT=np.ascontiguousarray(f(inputs["conv_w"]).reshape(DEPTH, 31, 2, 128).transpose(0, 3, 2, 1)),
        conv_b=f(inputs["conv_b"]), conv_g=f(inputs["conv_ln_g"]), conv_bb=f(inputs["conv_ln_b"]),
        sink=f(inputs["attn_sink"]), rel_bias=f(inputs["rel_bias"]), oh=host_consts(),
        ident=np.eye(128, dtype=np.float32), jflip=np.ascontiguousarray(np.eye(128, dtype=np.float32)[::-1]),
    )
    maps = []
    for c in range(NCORE):
        bi, sg = c // 4, c % 4
        t0 = sg * TOK_CORE - 256
        xs = np.zeros((NE * 128, D), np.float32)
        lo, hi = max(t0, 0), min(t0 + NE * 128, SEQ)
        xs[lo - t0:hi - t0] = x[bi, lo:hi]
        valid = np.zeros((1, NE), np.float32)
        for e in range(NE):
            tb = t0 + e * 128
            valid[0, e] = 1.0 if (0 <= tb < SEQ) else 0.0
        m = dict(common)
        m["x"] = xs
        m["valid"] = valid
        maps.append(m)
    return maps


def kernel(**inputs):
    if "nc" not in _PROG:
        _PROG["nc"] = build_program()
    nc = _PROG["nc"]
    maps = make_in_maps(inputs)
    res = run_bass_kernel_spmd(nc, maps, core_ids=list(range(NCORE)))
    _PROG["res"] = res
    out = np.zeros((2, SEQ, D), np.float32)
    for c in range(NCORE):
        bi, sg = c // 4, c % 4
        out[bi, sg * TOK_CORE:(sg + 1) * TOK_CORE] = res.results[c]["out"]
    return out
```

```python
import math
from contextlib import ExitStack

import numpy as np
import jax
import jax.numpy as jnp
import concourse.bass as bass
import concourse.mybir as mybir
from concourse.bass_utils import run_bass_kernel_spmd

F32 = mybir.dt.float32
BF16 = mybir.dt.bfloat16
AF = mybir.ActivationFunctionType
ALU = mybir.AluOpType

D = 1024
SEQ = 16384
NCORE = 8
TOK_CORE = 4096
NE = 36
DIN = 2816
ALPHA = 4 ** 0.25
EPS = 1e-5
GA = 0.044715
GC = 0.7978845608028654
DEPTH = 2
import os
SS = os.environ.get("KSS", "act,dve,pool").split(",")
ORDER = os.environ.get("KORDER", "mimimiml")
NSLOW = int(os.environ.get("KNSLOW", "3"))
NXR = 5
STEP_ORDER = os.environ.get("KSTEP", "m.gconv m.sc0 m.sc1 i.tr m.sc2 m.conv1 d.0 i.j4 d.1 m.conv2 m.pv d.2 m.conv3 m.gmlp i.j2 d.3 i.j3 i.j1pe m.mixtr i.j5 m.outproj i.j1ev i.av i.j6")

_o = dict(au=0, av=256, ag=512, ba=768, bb=1024, bg=1280, cq=1536, ck=2048, cv=2176, cg=2304)
_perm = []
_perm += list(range(_o['au'], _o['au'] + 256))
_perm += list(range(_o['ag'], _o['ag'] + 256))
_perm += list(range(_o['ba'], _o['ba'] + 256))
_perm += list(range(_o['bb'], _o['bb'] + 256))
for _c in range(4):
    _perm += list(range(_o['cq'] + _c * 64, _o['cq'] + _c * 64 + 64))
    _perm += list(range(_o['cq'] + (_c + 4) * 64, _o['cq'] + (_c + 4) * 64 + 64))
_perm += list(range(_o['ck'], _o['ck'] + 128))
T_AV = len(_perm); _perm += list(range(_o['av'], _o['av'] + 256))
T_BG = len(_perm); _perm += list(range(_o['bg'], _o['bg'] + 256))
T_CV = len(_perm); _perm += list(range(_o['cv'], _o['cv'] + 128))
T_CG = len(_perm); _perm += list(range(_o['cg'], _o['cg'] + 512))
PERM = np.array(_perm)
assert len(PERM) == DIN and len(set(_perm)) == DIN


class Eng:
    def __init__(self, name, h, sem, inc=1, selfsync=False):
        self.name, self.h, self.sem, self.inc, self.selfsync = name, h, sem, inc, selfsync
        self.count = 0
        self.waited = {}


class Buf:
    __slots__ = ("w", "r")

    def __init__(self):
        self.w = None
        self.r = {}


class Tracker:
    def __init__(self):
        self.bufs = {}

    def buf(self, name):
        b = self.bufs.get(name)
        if b is None:
            b = self.bufs[name] = Buf()
        return b

    def _deps(self, reads, writes):
        need = {}
        for n in reads:
            ev = self.buf(n).w
            if ev is not None and need.get(ev[0], 0) < ev[1]:
                need[ev[0]] = ev[1]
        for n in writes:
            b = self.buf(n)
            if b.w is not None and need.get(b.w[0], 0) < b.w[1]:
                need[b.w[0]] = b.w[1]
            for e, t in b.r.items():
                if need.get(e, 0) < t:
                    need[e] = t
        return need

    def _wait(self, eng, need):
        for e, t in need.items():
            if e is eng and not eng.selfsync:
                continue
            if eng.waited.get(e, 0) >= t:
                continue
            assert t <= e.count, f"{eng.name} needs unissued tick {t} of {e.name} ({e.count})"
            eng.h.wait_ge(e.sem, t * e.inc)
            eng.waited[e] = t
            if e.inc == 16 and t > getattr(e, "max_wait", 0):
                e.max_wait = t

    def _record(self, ev_eng, tick, reads, writes):
        for n in reads:
            b = self.buf(n)
            if b.r.get(ev_eng, 0) < tick:
                b.r[ev_eng] = tick
        for n in writes:
            b = self.buf(n)
            b.w = (ev_eng, tick)
            b.r = {}

    def op(self, eng, reads, writes, fn):
        self._wait(eng, self._deps(reads, writes))
        inst = fn()
        eng.count += 1
        inst.then_inc(eng.sem, 1)
        self._record(eng, eng.count, reads, writes)

    def dma(self, q, stream, out, in_, reads, writes, **kw):
        self._wait(q, self._deps(reads, writes))
        mw = getattr(stream, "max_wait", 0)
        if mw > q.waited.get(stream, 0):
            q.h.wait_ge(stream.sem, mw * stream.inc)
            q.waited[stream] = mw
        inst = q.h.dma_start(out=out, in_=in_, **kw)
        stream.count += 1
        inst.then_inc(stream.sem, 16)
        self._record(stream, stream.count, reads, writes)

    def set_writer_latest(self, names, stream):
        for n in names:
            self.buf(n).w = (stream, stream.count)

    def wait_all(self, eng, streams):
        for s in streams:
            if s.count > 0 and eng.waited.get(s, 0) < s.count:
                eng.h.wait_ge(s.sem, s.count * s.inc)
                eng.waited[s] = s.count


def build_program(nlayers=DEPTH, debug=False):
    nc = bass.Bass("TRN2", target_bir_lowering=False)
    es = ExitStack()

    def sb(name, shape, dt=F32):
        return es.enter_context(nc.sbuf_tensor(name, shape, dt))

    def sem(name):
        return es.enter_context(nc.semaphore(name))

    def din(name, shape):
        return nc.dram_tensor(name, shape, F32, kind="ExternalInput")

    T = Tracker()
    PE = Eng("pe", nc.tensor, sem("s_pe"))
    ACT = Eng("act", nc.scalar, sem("s_act"), selfsync=("act" in SS))
    DVE = Eng("dve", nc.vector, sem("s_dve"), selfsync=("dve" in SS))
    POOL = Eng("pool", nc.gpsimd, sem("s_pool"), selfsync=("pool" in SS))
    SP = Eng("sp", nc.sync, sem("s_sp"))
    LD = [Eng(f"ld{i}", None, sem(f"s_ld{i}"), inc=16) for i in range(NXR)]
    ST = [Eng(f"st{i}", None, sem(f"s_st{i}"), inc=16) for i in range(2)]
    LDW = Eng("ldw", None, sem("s_ldw"), inc=16)
    LDS = [Eng(f"lds{i}", None, sem(f"s_lds{i}"), inc=16) for i in range(2)]
    LDC = Eng("ldc", None, sem("s_ldc"), inc=16)
    LDT = Eng("ldt", None, sem("s_ldt"), inc=16)

    x_d = din("x", [NE * 128, D])
    valid_d = din("valid", [1, NE])
    w_in_d = din("w_in", [DEPTH, D, DIN])
    w_out_d = din("w_out", [DEPTH, D, D])
    ln_in_g_d = din("ln_in_g", [1, D]); ln_in_b_d = din("ln_in_b", [1, D])
    post_g_d = din("post_g", [DEPTH, D]); post_b_d = din("post_b", [DEPTH, D])
    gm_g_d = din("gm_g", [DEPTH, 128, 2]); gm_b_d = din("gm_b", [DEPTH, 256])
    wsT_d = din("wsT", [DEPTH, 128, 4, 128]); b_sp_d = din("b_sp", [DEPTH, 4, 128])
    conv_wT_d = din("conv_wT", [DEPTH, 128, 2, 31]); conv_b_d = din("conv_b", [DEPTH, 256])
    conv_g_d = din("conv_g", [DEPTH, 256]); conv_bb_d = din("conv_bb", [DEPTH, 256])
    sink_d = din("sink", [DEPTH, 8]); relb_d = din("rel_bias", [32, 8]); oh_d = din("oh", [33, 512])
    ident_d = din("ident", [128, 128]); jflip_d = din("jflip", [128, 128])
    out_d = nc.dram_tensor("out", [TOK_CORE, D], F32, kind="ExternalOutput")
    x1s_d = nc.dram_tensor("x1s", [NE * 128, D], F32, kind="ExternalOutput") if debug else nc.dram_tensor("x1s", [NE * 128, D], F32)
    fv_d = nc.dram_tensor("fvs", [8, 512], F32)

    win = sb("win", [128, 8, DIN], BF16)
    wout = sb("wout", [128, 8, D], BF16)
    dg = sb("dg", [128, 2, 31, 128], BF16)
    XR = [sb(f"xr{i}", [128, D]) for i in range(NXR)]
    Z = [sb(f"z{i}", [128, D]) for i in range(2)]
    xT = sb("xT", [128, D], BF16)
    mixT = sb("mixT", [128, D], BF16)
    Gin = sb("Gin", [128, D]); Bin = sb("Bin", [128, D]); Gp = sb("Gp", [128, D]); Bp = sb("Bp", [128, D])
    expb = sb("expb", [128, 3, 8, 128], BF16)
    PT = sb("PT", [128, 3, 8, 128], BF16)
    GUS = [sb(f"gus{i}", [128, 256]) for i in range(3)]
    VH = [sb(f"vh{i}", [128, 256], BF16) for i in range(3)]
    SG2B = [sb(f"sg2b{i}", [128, 256]) for i in range(3)]
    QT = [sb(f"qt{i}", [128, 512], BF16) for i in range(3)]
    SG2C = [sb(f"sg2c{i}", [128, 512]) for i in range(3)]
    KT = [sb(f"kt{i}", [128, 128], BF16) for i in range(4)]
    VA = [sb(f"va{i}", [128, 2, 65], BF16) for i in range(4)]
    YT = sb("YT", [128, 2, 4 * 128 + 30], BF16)
    tA1 = sb("tA1", [128, 256]); tA2 = sb("tA2", [128, 256]); tA3 = sb("tA3", [128, 256]); tA4 = sb("tA4", [128, 256])
    tB1 = sb("tB1", [128, 256]); tB2 = sb("tB2", [128, 256]); tC1 = sb("tC1", [128, 512])
    tSG = sb("tSG", [128, 256]); tCN = sb("tCN", [128, 256]); tCT = sb("tCT", [128, 256])
    On = sb("On", [128, 512]); YB = sb("YB", [128, 256], BF16); YC = sb("YC", [128, 512], BF16)
    ident_f = sb("ident_f", [128, 128]); ident_b = sb("ident_b", [128, 128], BF16); jflip = sb("jflip_sb", [128, 128])
    cwT = sb("cwT", [128, 2, 31]); wsT = sb("wsT_sb", [128, 4, 128], BF16); LBb = sb("LBb", [128, 256], BF16)
    bsT = sb("bsT", [128, 2, 128]); Cst = sb("Cst", [128, 2, 128]); gmg = sb("gmg", [128, 2])
    Gc = sb("Gc", [128, 256]); Bc = sb("Bc", [128, 256]); cbrow = sb("cbrow", [1, 256], BF16); ones_b = sb("ones_b", [1, 128], BF16)
    esink = sb("esink", [128, 8]); vcol = sb("vcol", [128, NE]); onescol = sb("onescol", [128, 2, 1])
    RB = sb("RB", [33, 8]); OHt = sb("OHt", [33, 512]); fvs = sb("fvs_sb", [8, 512]); hank = sb("hank", [128, 8, 128])
    mhalf = sb("mhalf", [128, 1])
    stg = [sb(f"stg{i}", [128, 1408]) for i in range(2)]
    den = sb("den", [128, 8]); rc = sb("rc", [128, 8])
    lnt = {}
    for tag in ("lnin", "lngm", "lncv", "lnpo"):
        lnt[tag] = dict(st=sb(tag + "_st", [128, 12]), mv=sb(tag + "_mv", [128, 2]), ve=sb(tag + "_ve", [128, 1]),
                        rs=sb(tag + "_rs", [128, 1]), nm=sb(tag + "_nm", [128, 1]))

    banks = [es.enter_context(nc.psum_tensor(f"bank{i}", [128, 512], F32)) for i in range(8)]
    bstate = [0]

    sstate = [0]

    def newbank(slow=False):
        if slow:
            i = sstate[0] % NSLOW
            sstate[0] += 1
        else:
            i = NSLOW + bstate[0] % (8 - NSLOW)
            bstate[0] += 1
        return banks[i], f"bank{i}"

    V, G, A_, P_ = nc.vector, nc.gpsimd, nc.scalar, nc.tensor

    def bc_rows(dram, row, n):
        return bass.AP(dram, row * n, [[0, 128], [1, n]])

    T.dma(SP, LDC, ident_f[:], ident_d.ap()[:, :], [], ["ident_f"])
    T.dma(SP, LDC, vcol[:], bc_rows(valid_d, 0, NE), [], ["vcol"])
    T.dma(SP, LDC, Gin[:], bc_rows(ln_in_g_d, 0, D), [], ["Gin"])
    T.dma(SP, LDC, Bin[:], bc_rows(ln_in_b_d, 0, D), [], ["Bin"])
    T.set_writer_latest(["ident_f", "vcol", "Gin", "Bin"], LDC)
    T.op(DVE, ["ident_f"], ["ident_b"], lambda: V.tensor_copy(out=ident_b[:], in_=ident_f[:]))
    T.op(DVE, [], ["mhalf"], lambda: V.memset(mhalf[:], -0.5))
    T.op(DVE, [], ["onescol"], lambda: V.memset(onescol[:], 1.0))
    T.op(DVE, [], ["ones_b"], lambda: V.memset(ones_b[:], 1.0))
    T.op(POOL, [], ["YT"], lambda: G.memset(YT[:], 0.0))

    def setup_bias():
        T.dma(SP, LDC, jflip[:], jflip_d.ap()[:, :], [], ["jflip"])
        T.dma(SP, LDC, OHt[:], oh_d.ap()[:, :], [], ["OHt"])
        T.op(DVE, [], ["RB"], lambda: V.memset(RB[:], 1.0))
        T.dma(SP, LDC, RB[0:32, :], relb_d.ap()[:, :], [], ["RB"])
        T.set_writer_latest(["jflip", "OHt", "RB"], LDC)
        bk, bn = newbank()
        T.op(PE, ["RB", "OHt"], [bn], lambda: P_.matmul(bk[0:8, :], lhsT=RB[0:33, 0:8], rhs=OHt[0:33, :], start=True, stop=True))
        T.op(DVE, [bn], ["fvs"], lambda: V.tensor_copy(out=fvs[:], in_=bk[0:8, :]))
        T.dma(POOL, LDT, fv_d.ap()[:, :], fvs[:], ["fvs"], ["fv_d"])
        for kc in range(3):
            T.dma(POOL, LDT, hank[:], bass.AP(fv_d, 256 - kc * 128, [[1, 128], [512, 8], [1, 128]]), ["fv_d"], ["hank"])
            for hh in range(2):
                bk, bn = newbank()
                T.op(PE, ["hank", "jflip"], [bn],
                     lambda: P_.matmul(bk[:, :], lhsT=jflip[:], rhs=hank[:, hh * 4:(hh + 1) * 4, :], start=True, stop=True))
                T.op(ACT, [bn], ["expb"],
                     lambda: A_.activation(out=expb[:, kc, hh * 4:(hh + 1) * 4, :].rearrange("p h q -> p (h q)"), in_=bk[:, :], func=AF.Copy, scale=8.0))


    WIN_NAMES = [f"win{k}" for k in range(16)]
    WOUT_NAMES = [f"wout{k}" for k in range(8)]
    stg_i = [0]
    cast_mode = ["dve"]

    def staged_cast(dst_ap, src_ap, dstname, n):
        i = stg_i[0] % 2
        stg_i[0] += 1
        T.dma(SP, LDS[i], stg[i][:, 0:n], src_ap, [], [f"stg{i}"])
        if i == 0 and cast_mode[0] == "alt":
            T.op(ACT, [f"stg{i}"], [dstname], lambda: A_.activation(out=dst_ap, in_=stg[i][:, 0:n], func=AF.Copy))
        else:
            T.op(DVE, [f"stg{i}"], [dstname], lambda: V.tensor_copy(out=dst_ap, in_=stg[i][:, 0:n]))

    pending = []

    def pump(n=1):
        for _ in range(n):
            if pending:
                pending.pop(0)()

    def queue_win(l):
        for kc in range(8):
            for hf in range(2):
                pending.append(lambda kc=kc, hf=hf: staged_cast(
                    win[:, kc, hf * 1408:(hf + 1) * 1408], w_in_d.ap()[l, kc * 128:(kc + 1) * 128, hf * 1408:(hf + 1) * 1408],
                    WIN_NAMES[kc * 2 + hf], 1408))

    def queue_wout(l):
        for kc in range(8):
            pending.append(lambda kc=kc: staged_cast(wout[:, kc, :], w_out_d.ap()[l, kc * 128:(kc + 1) * 128, :], WOUT_NAMES[kc], 1024))

    def load_layer(l):
        T.dma(POOL, LDW, wsT[:], wsT_d.ap()[l, :, :, :], [], ["wsT"])
        T.dma(POOL, LDW, LBb[:], bc_rows(gm_b_d, l, 256), [], ["LBb"])
        T.dma(POOL, LDW, cbrow[:], conv_b_d.ap()[l:l + 1, :], [], ["cbrow"])
        T.set_writer_latest(["wsT", "LBb", "cbrow"], LDW)
        T.dma(SP, LDC, cwT[:], conv_wT_d.ap()[l, :, :, :], [], ["cwT"])
        T.dma(SP, LDC, gmg[:], gm_g_d.ap()[l, :, :], [], ["gmg"])
        for h in range(4):
            T.dma(SP, LDC, bsT[(h % 2) * 64:(h % 2 + 1) * 64, h // 2, :],
                  bass.AP(b_sp_d, (l * 4 + h) * 128, [[0, 64], [1, 128]]), [], ["bsT"])
        T.dma(SP, LDC, Gc[:], bc_rows(conv_g_d, l, 256), [], ["Gc"])
        T.dma(SP, LDC, Bc[:], bc_rows(conv_bb_d, l, 256), [], ["Bc"])
        T.dma(SP, LDC, esink[:], bc_rows(sink_d, l, 8), [], ["esink"])
        T.dma(SP, LDC, Gp[:], bc_rows(post_g_d, l, D), [], ["Gp"])
        T.dma(SP, LDC, Bp[:], bc_rows(post_b_d, l, D), [], ["Bp"])
        T.set_writer_latest(["cwT", "gmg", "bsT", "Gc", "Bc", "esink", "Gp", "Bp"], LDC)
        T.op(ACT, ["esink"], ["esink"], lambda: A_.activation(out=esink[:], in_=esink[:], func=AF.Exp))
        T.op(DVE, ["gmg"], ["gmg"], lambda: V.tensor_scalar(out=gmg[:], in0=gmg[:], scalar1=0.25, scalar2=None, op0=ALU.mult))
        def dg_piece(ch, k0, k1):
            for k in range(k0, k1):
                T.op(POOL, ["ident_f", "cwT"], ["dg"],
                     lambda: G.tensor_scalar(out=dg[:, ch, k, :], in0=ident_f[:], scalar1=cwT[:, ch, k:k + 1], scalar2=0.5,
                                             op0=ALU.mult, op1=ALU.mult))
        for ch in range(2):
            for k0 in range(0, 31, 8):
                pending.append(lambda ch=ch, k0=k0: dg_piece(ch, k0, min(k0 + 8, 31)))
        bk, bn = newbank()

        def f():
            for h in range(4):
                ins = P_.matmul(bk[(h % 2) * 64:(h % 2 + 1) * 64, (h // 2) * 128:(h // 2 + 1) * 128],
                                lhsT=LBb[:, h * 64:(h + 1) * 64], rhs=wsT[:, h, :], start=True, stop=True)
            return ins
        T.op(PE, ["LBb", "wsT"], [bn], f)
        T.op(DVE, [bn, "bsT"], ["Cst"], lambda: V.tensor_tensor(out=Cst[:].rearrange("p c i -> p (c i)"), in0=bk[:, 0:256],
                                                               in1=bsT[:].rearrange("p c i -> p (c i)"), op=ALU.add))
        T.op(DVE, ["Cst"], ["Cst"], lambda: V.tensor_scalar(out=Cst[:], in0=Cst[:], scalar1=0.25, scalar2=None, op0=ALU.mult))

    def ln_a(tag, aps, srcbufs):
        t = lnt[tag]
        for i, ap in enumerate(aps):
            T.op(DVE, srcbufs, [tag + "st"], lambda: V.bn_stats(out=t["st"][:, 6 * i:6 * i + 6], in_=ap))
        k = len(aps)
        T.op(DVE, [tag + "st"], [tag + "mv"], lambda: V.bn_aggr(out=t["mv"][:, 0:2], in_=t["st"][:, 0:6 * k]))

    def ln_b(tag, eps):
        t = lnt[tag]
        T.op(POOL, [tag + "mv"], [tag + "ve"], lambda: G.tensor_scalar(out=t["ve"][:], in0=t["mv"][:, 1:2], scalar1=eps, scalar2=None, op0=ALU.add))
        T.op(POOL, [tag + "ve", "mhalf"], [tag + "rs"], lambda: G.tensor_tensor(out=t["rs"][:], in0=t["ve"][:], in1=mhalf[:], op=ALU.pow))
        T.op(POOL, [tag + "mv", tag + "rs"], [tag + "nm"],
             lambda: G.tensor_scalar(out=t["nm"][:], in0=t["mv"][:, 0:1], scalar1=t["rs"][:], scalar2=-1.0, op0=ALU.mult, op1=ALU.mult))
        return t["rs"], t["nm"], [tag + "rs", tag + "nm"]

    def issue_load(l, b):
        X = XR[b % NXR]
        if l == 0:
            T.dma(SP, LD[b % NXR], X[:], x_d.ap()[b * 128:(b + 1) * 128, :], [], [f"xr{b % NXR}"])
        else:
            T.dma(SP, LD[b % NXR], X[:], x1s_d.ap()[b * 128:(b + 1) * 128, :], [f"x1s{b}"], [f"xr{b % NXR}"])

    def lnin_pieces(l, b):
        X = XR[b % NXR]; xb = f"xr{b % NXR}"
        st = {}

        def a():
            ln_a("lnin", [X[:, 0:512], X[:, 512:1024]], [xb])

        def bb():
            rs, nm, lb = ln_b("lnin", EPS)
            T.op(ACT, [xb] + lb, [xb], lambda: A_.activation(out=X[:], in_=X[:], func=AF.Identity, scale=rs[:], bias=nm[:]))

        def c():
            T.op(DVE, [xb, "Gin"], [xb], lambda: V.tensor_tensor(out=X[:], in0=X[:], in1=Gin[:], op=ALU.mult))

        def d():
            T.op(POOL, [xb, "Bin"], [xb], lambda: G.tensor_tensor(out=X[:], in0=X[:], in1=Bin[:], op=ALU.add))
        if l != 0:
            return [lambda: None] * 4
        return [a, bb, c, d]

    def lnin_stage(l, b):
        for fn in lnin_pieces(l, b):
            fn()

    def input_pieces(l, b):
        X = XR[b % NXR]; xb = f"xr{b % NXR}"
        s2 = b % 3; r4 = b % 4
        xr = WIN_NAMES + ["xT0", "xT1"]
        st = {}

        def fm(bk, col, ci):
            for kc in range(8):
                ins = P_.matmul(bk[:, col:col + 128], lhsT=win[:, kc, ci * 128:(ci + 1) * 128], rhs=xT[:, kc * 128:(kc + 1) * 128],
                                start=(kc == 0), stop=(kc == 7))
            return ins

        def tm(bk, col, w0, n):
            for kc in range(8):
                ins = P_.matmul(bk[:, col:col + n], lhsT=xT[:, kc * 128:(kc + 1) * 128], rhs=win[:, kc, w0:w0 + n],
                                start=(kc == 0), stop=(kc == 7))
            return ins

        def p_tr():
            for half in range(2):
                bk, bn = newbank()

                def f():
                    for i in range(4):
                        kc = half * 4 + i
                        ins = P_.transpose(out=bk[:, i * 128:(i + 1) * 128], in_=X[:, kc * 128:(kc + 1) * 128], identity=ident_f[:])
                    return ins
                T.op(PE, [xb, "ident_f"], [bn], f)
                T.op(ACT, [bn], [f"xT{half}"], lambda: A_.activation(out=xT[:, half * 512:(half + 1) * 512], in_=bk[:, :], func=AF.Copy))

        def p_j4():
            b4, n4 = newbank(slow=True)
            st["b4"], st["n4"] = b4, n4

            def f4():
                fm(b4, 0, 12)
                tm(b4, 128, T_CV, 128)
                return tm(b4, 256, T_AV, 256)
            T.op(PE, xr, [n4], f4)
            T.op(DVE, [n4], [f"kt{r4}"], lambda: V.tensor_copy(out=KT[r4][:], in_=b4[:, 0:128]))
            T.op(DVE, [n4, "vcol"], [f"va{r4}"],
                 lambda: V.tensor_scalar(out=VA[r4][:, :, 0:64], in0=b4[:, 128:256].rearrange("p (g d) -> p g d", g=2),
                                         scalar1=vcol[:, b:b + 1], scalar2=None, op0=ALU.mult))
            T.op(DVE, ["onescol", "vcol"], [f"va{r4}"],
                 lambda: V.tensor_scalar(out=VA[r4][:, :, 64:65], in0=onescol[:], scalar1=vcol[:, b:b + 1], scalar2=None, op0=ALU.mult))

        def p_j2():
            b2, n2 = newbank()

            def f2():
                for i in range(4):
                    ins = fm(b2, i * 128, 4 + i)
                return ins
            T.op(PE, xr, [n2], f2)
            T.op(ACT, [n2], ["tB1"], lambda: A_.activation(out=tB1[:], in_=b2[:, 256:512], func=AF.Tanh, scale=0.5))
            c0 = 15 + r4 * 128
            T.op(DVE, ["tB1", n2], [f"YT{r4}"],
                 lambda: V.scalar_tensor_tensor(out=YT[:, :, c0:c0 + 128], in0=tB1[:].rearrange("p (c t) -> p c t", c=2), scalar=1.0,
                                                in1=b2[:, 0:256].rearrange("p (c t) -> p c t", c=2), op0=ALU.add, op1=ALU.mult))
            if b in (0, 1, NE - 2, NE - 1):
                T.op(POOL, [f"YT{r4}", "vcol"], [f"YT{r4}"],
                     lambda: G.tensor_scalar(out=YT[:, :, c0:c0 + 128], in0=YT[:, :, c0:c0 + 128], scalar1=vcol[:, b:b + 1], scalar2=None, op0=ALU.mult))
            if r4 == 3:
                T.op(POOL, ["YT3"], ["YTL"], lambda: G.tensor_copy(out=YT[:, :, 0:15], in_=YT[:, :, 512:527]))
            if r4 == 0:
                T.op(POOL, ["YT0"], ["YTR"], lambda: G.tensor_copy(out=YT[:, :, 527:542], in_=YT[:, :, 15:30]))

        def p_j3():
            b3, n3 = newbank()

            def f3():
                for i in range(4):
                    ins = fm(b3, i * 128, 8 + i)
                return ins
            T.op(PE, xr, [n3], f3)
            T.op(ACT, [n3], [f"qt{s2}"], lambda: A_.activation(out=QT[s2][:], in_=b3[:, :], func=AF.Copy))

        def p_j1pe():
            b1, n1 = newbank(slow=True)
            st["b1"], st["n1"] = b1, n1

            def f1():
                for i in range(4):
                    ins = fm(b1, i * 128, i)
                return ins
            T.op(PE, xr, [n1], f1)

        def p_j1ev():
            b1, n1 = st["b1"], st["n1"]
            au = b1[:, 0:256]; ag = b1[:, 256:512]
            T.op(ACT, [n1], ["tA1"], lambda: A_.activation(out=tA1[:], in_=au, func=AF.Square, scale=math.sqrt(GA)))
            T.op(ACT, [n1], ["tA2"], lambda: A_.activation(out=tA2[:], in_=ag, func=AF.Tanh, scale=0.5))
            T.op(DVE, ["tA1", n1], ["tA1"], lambda: V.scalar_tensor_tensor(out=tA1[:], in0=tA1[:], scalar=1.0, in1=au, op0=ALU.add, op1=ALU.mult))
            T.op(DVE, ["tA2", n1], ["tA2"], lambda: V.scalar_tensor_tensor(out=tA2[:], in0=tA2[:], scalar=1.0, in1=ag, op0=ALU.add, op1=ALU.mult))
            T.op(ACT, ["tA1"], ["tA1"], lambda: A_.activation(out=tA1[:], in_=tA1[:], func=AF.Tanh, scale=GC))
            T.op(DVE, ["tA1", n1], ["tA1"], lambda: V.scalar_tensor_tensor(out=tA1[:], in0=tA1[:], scalar=1.0, in1=au, op0=ALU.add, op1=ALU.mult))
            T.op(POOL, ["tA1", "tA2"], [f"gus{s2}"], lambda: G.tensor_tensor(out=GUS[s2][:], in0=tA1[:], in1=tA2[:], op=ALU.mult))

        def p_av():
            b4, n4 = st["b4"], st["n4"]
            av = b4[:, 256:512]
            T.op(ACT, [n4], ["tA3"], lambda: A_.activation(out=tA3[:], in_=av, func=AF.Square, scale=math.sqrt(GA)))
            T.op(DVE, ["tA3", n4], ["tA3"], lambda: V.scalar_tensor_tensor(out=tA3[:], in0=tA3[:], scalar=1.0, in1=av, op0=ALU.add, op1=ALU.mult))
            T.op(ACT, ["tA3"], ["tA3"], lambda: A_.activation(out=tA3[:], in_=tA3[:], func=AF.Tanh, scale=GC))
            T.op(DVE, ["tA3", n4], ["tA4"], lambda: V.scalar_tensor_tensor(out=tA4[:], in0=tA3[:], scalar=1.0, in1=av, op0=ALU.add, op1=ALU.mult))
            ln_a("lngm", [tA4[:]], ["tA4"])
            rs, nm, lb = ln_b("lngm", 4.0 * EPS)
            T.op(POOL, ["tA4"] + lb, [f"vh{s2}"],
                 lambda: G.tensor_scalar(out=VH[s2][:], in0=tA4[:], scalar1=rs[:], scalar2=nm[:], op0=ALU.mult, op1=ALU.add))

        def p_j5():
            b5, n5 = newbank()
            T.op(PE, xr, [n5], lambda: tm(b5, 0, T_BG, 256))
            T.op(ACT, [n5], ["tB2"], lambda: A_.activation(out=tB2[:], in_=b5[:, 0:256], func=AF.Tanh, scale=0.5))
            T.op(DVE, ["tB2", n5], [f"sg2b{s2}"],
                 lambda: V.scalar_tensor_tensor(out=SG2B[s2][:], in0=tB2[:], scalar=1.0, in1=b5[:, 0:256], op0=ALU.add, op1=ALU.mult))

        def p_j6():
            b6, n6 = newbank()
            T.op(PE, xr, [n6], lambda: tm(b6, 0, T_CG, 512))
            T.op(ACT, [n6], ["tC1"], lambda: A_.activation(out=tC1[:], in_=b6[:, :], func=AF.Tanh, scale=0.5))
            T.op(DVE, ["tC1", n6], [f"sg2c{s2}"],
                 lambda: V.scalar_tensor_tensor(out=SG2C[s2][:], in0=tC1[:], scalar=1.0, in1=b6[:, :], op0=ALU.add, op1=ALU.mult))

        def p_j1():
            p_j1pe()
            p_j1ev()

        return dict(tr=p_tr, j4=p_j4, j2=p_j2, j3=p_j3, j1=p_j1, j1pe=p_j1pe, j1ev=p_j1ev, av=p_av, j5=p_j5, j6=p_j6)

    def mix_pieces(l, e, nl_last):
        s2 = e % 3; r4 = e % 4; z2 = e % 2
        X = XR[e % NXR]; xb = f"xr{e % NXR}"
        Zt = Z[z2]; zb = f"z{z2}"
        st = {}

        def p_gconv():
            bk, bn = newbank(slow=True)
            st["bk"], st["bn"] = bk, bn

            def fg():
                for h in range(4):
                    P_.matmul(bk[(h % 2) * 64:(h % 2 + 1) * 64, (h // 2) * 128:(h // 2 + 1) * 128],
                              lhsT=VH[s2][:, h * 64:(h + 1) * 64], rhs=wsT[:, h, :], start=True, stop=True)
                for ch in range(2):
                    P_.matmul(bk[:, 256 + ch * 128:256 + (ch + 1) * 128], lhsT=ones_b[0:1, 0:128], rhs=cbrow[0:1, ch * 128:(ch + 1) * 128],
                              start=True, stop=False)
                    for k in range(31):
                        ins = P_.matmul(bk[:, 256 + ch * 128:256 + (ch + 1) * 128], lhsT=YT[:, ch, r4 * 128 + k:r4 * 128 + k + 128],
                                        rhs=dg[:, ch, k, :], start=False, stop=(k == 30))
                return ins
            ytn = [f"YT{(e - 1) % 4}", f"YT{r4}", f"YT{(e + 1) % 4}"] + (["YTL"] if r4 == 0 else []) + (["YTR"] if r4 == 3 else [])
            T.op(PE, [f"vh{s2}", "wsT", "ones_b", "cbrow", "dg"] + ytn, [bn], fg)

        def p_scores(kcs=(0, 1, 2)):
            for kc in kcs:
                ks = (e - 1 + kc) % 4
                bb = [newbank() for _ in range(2)]

                def fs():
                    for g in range(2):
                        P_.matmul(bb[g][0][:, :], lhsT=KT[ks][g * 64:(g + 1) * 64, :], rhs=QT[s2][g * 64:(g + 1) * 64, :], start=True, stop=False)
                    for g in range(2):
                        ins = P_.matmul(bb[g][0][:, :], lhsT=ident_b[:], rhs=expb[:, kc, g * 4:(g + 1) * 4, :], start=False, stop=True)
                    return ins
                T.op(PE, [f"kt{ks}", f"qt{s2}", "ident_b", "expb"], [bb[0][1], bb[1][1]], fs)
                for g in range(2):
                    T.op(ACT, [bb[g][1]], [f"PT{g}"],
                         lambda: A_.activation(out=PT[:, kc, g * 4:(g + 1) * 4, :].rearrange("p h q -> p (h q)"), in_=bb[g][0][:, :], func=AF.Exp, scale=0.125))

        def p_conv1():
            bk, bn = st["bk"], st["bn"]
            cv = bk[:, 256:512]
            ln_a("lncv", [cv], [bn])
            rs, nm, lb = ln_b("lncv", EPS)
            T.op(ACT, [bn] + lb, ["tCN"], lambda: A_.activation(out=tCN[:], in_=cv, func=AF.Identity, scale=rs[:], bias=nm[:]))

        def p_conv2():
            T.op(DVE, ["tCN", "Gc"], ["tCN"], lambda: V.tensor_tensor(out=tCN[:], in0=tCN[:], in1=Gc[:], op=ALU.mult))
            T.op(DVE, ["tCN", "Bc"], ["tCN"], lambda: V.tensor_tensor(out=tCN[:], in0=tCN[:], in1=Bc[:], op=ALU.add))
            T.op(ACT, ["tCN"], ["tCT"], lambda: A_.activation(out=tCT[:], in_=tCN[:], func=AF.Tanh, scale=0.5))

        def p_conv3():
            T.op(DVE, ["tCT", "tCN"], ["tCT"], lambda: V.scalar_tensor_tensor(out=tCT[:], in0=tCT[:], scalar=1.0, in1=tCN[:], op0=ALU.add, op1=ALU.mult))
            T.op(DVE, ["tCT", f"sg2b{s2}"], ["YB"], lambda: V.scalar_tensor_tensor(out=YB[:], in0=tCT[:], scalar=0.25, in1=SG2B[s2][:], op0=ALU.mult, op1=ALU.mult))

        def p_gmlp():
            bk, bn = st["bk"], st["bn"]
            for ch in range(2):
                T.op(DVE, [bn, "gmg", "Cst"], ["tSG"],
                     lambda: V.scalar_tensor_tensor(out=tSG[:, ch * 128:(ch + 1) * 128], in0=bk[:, ch * 128:(ch + 1) * 128], scalar=gmg[:, ch:ch + 1],
                                                    in1=Cst[:, ch, :], op0=ALU.mult, op1=ALU.add))
            T.op(POOL, ["tSG", f"gus{s2}"], ["mixA"], lambda: G.tensor_tensor(out=mixT[:, 0:256], in0=tSG[:], in1=GUS[s2][:], op=ALU.mult))

        def p_pv():
            for g in range(2):
                bo, bon = newbank()
                po = bo[:, 0:260].rearrange("p (h c) -> p h c", h=4)

                def fpv():
                    for hh in range(4):
                        for kc in range(3):
                            ks = (e - 1 + kc) % 4
                            ins = P_.matmul(po[:, hh, :], lhsT=PT[:, kc, g * 4 + hh, :], rhs=VA[ks][:, g, :], start=(kc == 0), stop=(kc == 2))
                    return ins
                T.op(PE, [f"PT{g}"] + [f"va{(e - 1 + kc) % 4}" for kc in range(3)], [bon], fpv)
                T.op(DVE, [bon, "esink"], [f"den{g}"],
                     lambda: V.tensor_tensor(out=den[:, g * 4:(g + 1) * 4].unsqueeze(2), in0=po[:, :, 64:65],
                                             in1=esink[:, g * 4:(g + 1) * 4].unsqueeze(2), op=ALU.add))
                T.op(DVE, [f"den{g}"], [f"rc{g}"], lambda: V.reciprocal(out=rc[:, g * 4:(g + 1) * 4], in_=den[:, g * 4:(g + 1) * 4]))
                T.op(DVE, [bon, f"rc{g}"], [f"On{g}"],
                     lambda: V.tensor_tensor(out=On[:, g * 256:(g + 1) * 256].rearrange("p (h d) -> p h d", h=4), in0=po[:, :, 0:64],
                                             in1=rc[:, g * 4:(g + 1) * 4].unsqueeze(2).to_broadcast([128, 4, 64]), op=ALU.mult))
                T.op(POOL, [f"On{g}", f"sg2c{s2}"], [f"YC{g}"],
                     lambda: G.tensor_tensor(out=YC[:, g * 256:(g + 1) * 256], in0=On[:, g * 256:(g + 1) * 256],
                                             in1=SG2C[s2][:, g * 256:(g + 1) * 256], op=ALU.mult))

        def p_mixtr():
            bt, btn = newbank()
            btb = bt[:].bitcast(BF16)

            def ftr():
                for c in range(2):
                    P_.transpose(out=btb[:, c * 128:(c + 1) * 128], in_=YB[:, c * 128:(c + 1) * 128], identity=ident_b[:])
                for c in range(4):
                    ins = P_.transpose(out=btb[:, (2 + c) * 128:(3 + c) * 128], in_=YC[:, c * 128:(c + 1) * 128], identity=ident_b[:])
                return ins
            T.op(PE, ["YB", "YC0", "YC1", "ident_b"], [btn], ftr)
            T.op(ACT, [btn], ["mixB"], lambda: A_.activation(out=mixT[:, 256:512], in_=btb[:, 0:256], func=AF.Copy))
            T.op(ACT, [btn], ["mixC"], lambda: A_.activation(out=mixT[:, 512:1024], in_=btb[:, 256:768], func=AF.Copy, scale=0.5))

        def p_outproj():
            for n in range(2):
                by, byn = newbank()

                def fo():
                    for kc in range(8):
                        ins = P_.matmul(by[:, :], lhsT=mixT[:, kc * 128:(kc + 1) * 128], rhs=wout[:, kc, n * 512:(n + 1) * 512],
                                        start=(kc == 0), stop=(kc == 7))
                    return ins
                T.op(PE, ["mixA", "mixB", "mixC"] + WOUT_NAMES, [byn], fo)
                T.op(DVE, [byn, xb], [zb + f"h{n}"],
                     lambda: V.scalar_tensor_tensor(out=Zt[:, n * 512:(n + 1) * 512], in0=X[:, n * 512:(n + 1) * 512], scalar=ALPHA, in1=by[:, :],
                                                    op0=ALU.mult, op1=ALU.add))

        def p_postln_a():
            ln_a("lnpo", [Zt[:, 0:512], Zt[:, 512:1024]], [zb + "h0", zb + "h1"])

        def p_postln_b():
            rs, nm, lb = ln_b("lnpo", EPS)
            T.op(ACT, [zb + "h0", zb + "h1"] + lb, [zb], lambda: A_.activation(out=Zt[:], in_=Zt[:], func=AF.Identity, scale=rs[:], bias=nm[:]))

        def p_postln_c():
            T.op(DVE, [zb, "Gp"], [zb], lambda: V.tensor_tensor(out=Zt[:], in0=Zt[:], in1=Gp[:], op=ALU.mult))

        def p_postln_d():
            T.op(POOL, [zb, "Bp"], [zb], lambda: G.tensor_tensor(out=Zt[:], in0=Zt[:], in1=Bp[:], op=ALU.add))
            if not nl_last:
                T.dma(SP, ST[z2], x1s_d.ap()[e * 128:(e + 1) * 128, :], Zt[:], [zb], [f"x1s{e}", zb + "h0", zb + "h1"])
            else:
                r0 = (e - 2) * 128
                T.dma(SP, ST[z2], out_d.ap()[r0:r0 + 128, :], Zt[:], [zb], [zb + "h0", zb + "h1"])

        return dict(gconv=p_gconv, scores=p_scores, sc0=lambda: p_scores((0,)), sc1=lambda: p_scores((1,)), sc2=lambda: p_scores((2,)), conv1=p_conv1, conv2=p_conv2, conv3=p_conv3, gmlp=p_gmlp, pv=p_pv,
                    mixtr=p_mixtr, outproj=p_outproj, postln=[p_postln_a, p_postln_b, p_postln_c, p_postln_d])

    IN_ALL = ["tr", "j4", "j2", "j3", "j1", "av", "j5", "j6"]
    STEP = STEP_ORDER.split()

    def run_input(l, b):
        p = input_pieces(l, b)
        for k in IN_ALL:
            p[k]()
            pump(1)

    for l in range(nlayers):
        first_b, last_b = l, NE - 1 - l
        nl_last = (l == nlayers - 1 and nlayers == DEPTH)
        if l == 0:
            setup_bias()
        for b in range(first_b, first_b + NXR):
            issue_load(l, b)
        if l == 0:
            queue_win(0)
            pump(16)
            cast_mode[0] = "alt"
        for b in range(first_b, first_b + NXR):
            lnin_stage(l, b)
        queue_wout(l)
        run_input(l, first_b)
        load_layer(l)
        issue_load(l, first_b + NXR)
        for b in range(first_b + 1, first_b + 3):
            run_input(l, b)
        pump(len(pending))
        deferred = [[], [], [], []]
        for e in range(l + 1, NE - 1 - l):
            pi = input_pieces(l, e + 2) if e + 2 <= last_b else None
            pm = mix_pieces(l, e, nl_last)
            for tok in STEP:
                pump(1)
                kind, name = tok.split(".")
                if kind == "m":
                    pm[name]()
                elif kind == "i":
                    if pi is not None:
                        pi[name]()
                elif kind == "d":
                    k = int(name)
                    for fn in deferred[k]:
                        fn()
                    deferred[k] = []
            for k in range(4):
                deferred[k].append(pm["postln"][k])
            if e + 4 <= last_b and e + 4 > first_b + 4:
                lp = lnin_pieces(l, e + 4)
                for k in range(4):
                    deferred[k].append(lp[k])
            if e + NXR <= last_b and e + NXR > first_b + NXR:
                issue_load(l, e + NXR)
            if e + 2 == last_b and l + 1 < nlayers:
                queue_win(l + 1)
        for k in range(4):
            for fn in deferred[k]:
                fn()
        pump(len(pending))
    T.wait_all(SP, ST)
    return nc


_PROG = {}


def t5_bucket(rel):
    nb = 16
    max_exact = 8
    ret = jnp.where(rel > 0, nb, 0)
    n = jnp.abs(rel)
    nf = jnp.maximum(n, 1).astype(jnp.float32)
    large = max_exact + (jnp.log(nf / max_exact) / math.log(128 / max_exact) * (nb - max_exact)).astype(jnp.int32)
    large = jnp.minimum(large, nb - 1)
    return ret + jnp.where(n < max_exact, n, large)


def host_consts():
    rel = 255 - np.arange(512)
    with jax.default_device(jax.devices("cpu")[0]):
        bucket = np.asarray(t5_bucket(jnp.asarray(rel, dtype=jnp.int32)))
    oh = np.zeros((33, 512), np.float32)
    oh[bucket, np.arange(512)] = 1.0
    oh[32] = np.where(np.abs(rel) <= 128, 0.0, -30000.0)
    return oh


def make_in_maps(inputs):
    f = lambda a: np.ascontiguousarray(np.asarray(a, dtype=np.float32))
    x = f(inputs["x"])
    w_in = np.ascontiguousarray(f(inputs["w_in"])[:, :, PERM])
    common = dict(
        w_in=w_in, w_out=f(inputs["w_out"]),
        ln_in_g=f(inputs["ln_in_g"]).reshape(1, D), ln_in_b=f(inputs["ln_in_b"]).reshape(1, D),
        post_g=f(inputs["post_ln_g"]), post_b=f(inputs["post_ln_b"]),
        gm_g=np.ascontiguousarray(f(inputs["gmlp_ln_g"]).reshape(DEPTH, 2, 128).transpose(0, 2, 1)),
        gm_b=f(inputs["gmlp_ln_b"]),
        wsT=np.ascontiguousarray(f(inputs["w_spatial"]).transpose(0, 3, 1, 2)),
        b_sp=f(inputs["b_spatial"]),
        conv_wT=np.ascontiguousarray(f(inputs["conv_w"]).reshape(DEPTH, 31, 2, 128).transpose(0, 3, 2, 1)),
        conv_b=f(inputs["conv_b"]), conv_g=f(inputs["conv_ln_g"]), conv_bb=f(inputs["conv_ln_b"]),
        sink=f(inputs["attn_sink"]), rel_bias=f(inputs["rel_bias"]), oh=host_consts(),
        ident=np.eye(128, dtype=np.float32), jflip=np.ascontiguousarray(np.eye(128, dtype=np.float32)[::-1]),
    )
    maps = []
    for c in range(NCORE):
        bi, sg = c // 4, c % 4
        t0 = sg * TOK_CORE - 256
        xs = np.zeros((NE * 128, D), np.float32)
        lo, hi = max(t0, 0), min(t0 + NE * 128, SEQ)
        xs[lo - t0:hi - t0] = x[bi, lo:hi]
        valid = np.zeros((1, NE), np.float32)
        for e in range(NE):
            tb = t0 + e * 128
            valid[0, e] = 1.0 if (0 <= tb < SEQ) else 0.0
        m = dict(common)
        m["x"] = xs
        m["valid"] = valid
        maps.append(m)
    return maps


def kernel(**inputs):
    if "nc" not in _PROG:
        _PROG["nc"] = build_program()
    nc = _PROG["nc"]
    maps = make_in_maps(inputs)
    res = run_bass_kernel_spmd(nc, maps, core_ids=list(range(NCORE)))
    _PROG["res"] = res
    out = np.zeros((2, SEQ, D), np.float32)
    for c in range(NCORE):
        bi, sg = c // 4, c % 4
        out[bi, sg * TOK_CORE:(sg + 1) * TOK_CORE] = res.results[c]["out"]
    return out
```

```python
import math
from contextlib import ExitStack

import numpy as np
import jax
import jax.numpy as jnp
import concourse.bass as bass
import concourse.mybir as mybir
from concourse.bass_utils import run_bass_kernel_spmd

F32 = mybir.dt.float32
BF16 = mybir.dt.bfloat16
AF = mybir.ActivationFunctionType
ALU = mybir.AluOpType

D = 1024
SEQ = 16384
NCORE = 8
TOK_CORE = 4096
NE = 36
DIN = 2816
ALPHA = 4 ** 0.25
EPS = 1e-5
GA = 0.044715
GC = 0.7978845608028654
DEPTH = 2
import os
SS = os.environ.get("KSS", "act,dve,pool").split(",")
ORDER = os.environ.get("KORDER", "mimimiml")
NSLOW = int(os.environ.get("KNSLOW", "3"))
NXR = 5
STEP_ORDER = os.environ.get("KSTEP", "m.gconv m.sc0 m.sc1 i.tr m.sc2 m.conv1 d.0 i.j4 d.1 m.conv2 m.pv d.2 m.conv3 m.gmlp i.j2 d.3 i.j3 i.j1pe m.mixtr i.j5 m.outproj i.j1ev i.av i.j6")

_o = dict(au=0, av=256, ag=512, ba=768, bb=1024, bg=1280, cq=1536, ck=2048, cv=2176, cg=2304)
_perm = []
_perm += list(range(_o['au'], _o['au'] + 256))
_perm += list(range(_o['ag'], _o['ag'] + 256))
_perm += list(range(_o['ba'], _o['ba'] + 256))
_perm += list(range(_o['bb'], _o['bb'] + 256))
for _c in range(4):
    _perm += list(range(_o['cq'] + _c * 64, _o['cq'] + _c * 64 + 64))
    _perm += list(range(_o['cq'] + (_c + 4) * 64, _o['cq'] + (_c + 4) * 64 + 64))
_perm += list(range(_o['ck'], _o['ck'] + 128))
T_AV = len(_perm); _perm += list(range(_o['av'], _o['av'] + 256))
T_BG = len(_perm); _perm += list(range(_o['bg'], _o['bg'] + 256))
T_CV = len(_perm); _perm += list(range(_o['cv'], _o['cv'] + 128))
T_CG = len(_perm); _perm += list(range(_o['cg'], _o['cg'] + 512))
PERM = np.array(_perm)
assert len(PERM) == DIN and len(set(_perm)) == DIN


class Eng:
    def __init__(self, name, h, sem, inc=1, selfsync=False):
        self.name, self.h, self.sem, self.inc, self.selfsync = name, h, sem, inc, selfsync
        self.count = 0
        self.waited = {}


class Buf:
    __slots__ = ("w", "r")

    def __init__(self):
        self.w = None
        self.r = {}


class Tracker:
    def __init__(self):
        self.bufs = {}

    def buf(self, name):
        b = self.bufs.get(name)
        if b is None:
            b = self.bufs[name] = Buf()
        return b

    def _deps(self, reads, writes):
        need = {}
        for n in reads:
            ev = self.buf(n).w
            if ev is not None and need.get(ev[0], 0) < ev[1]:
                need[ev[0]] = ev[1]
        for n in writes:
            b = self.buf(n)
            if b.w is not None and need.get(b.w[0], 0) < b.w[1]:
                need[b.w[0]] = b.w[1]
            for e, t in b.r.items():
                if need.get(e, 0) < t:
                    need[e] = t
        return need

    def _wait(self, eng, need):
        for e, t in need.items():
            if e is eng and not eng.selfsync:
                continue
            if eng.waited.get(e, 0) >= t:
                continue
            assert t <= e.count, f"{eng.name} needs unissued tick {t} of {e.name} ({e.count})"
            eng.h.wait_ge(e.sem, t * e.inc)
            eng.waited[e] = t
            if e.inc == 16 and t > getattr(e, "max_wait", 0):
                e.max_wait = t

    def _record(self, ev_eng, tick, reads, writes):
        for n in reads:
            b = self.buf(n)
            if b.r.get(ev_eng, 0) < tick:
                b.r[ev_eng] = tick
        for n in writes:
            b = self.buf(n)
            b.w = (ev_eng, tick)
            b.r = {}

    def op(self, eng, reads, writes, fn):
        self._wait(eng, self._deps(reads, writes))
        inst = fn()
        eng.count += 1
        inst.then_inc(eng.sem, 1)
        self._record(eng, eng.count, reads, writes)

    def dma(self, q, stream, out, in_, reads, writes, **kw):
        self._wait(q, self._deps(reads, writes))
        mw = getattr(stream, "max_wait", 0)
        if mw > q.waited.get(stream, 0):
            q.h.wait_ge(stream.sem, mw * stream.inc)
            q.waited[stream] = mw
        inst = q.h.dma_start(out=out, in_=in_, **kw)
        stream.count += 1
        inst.then_inc(stream.sem, 16)
        self._record(stream, stream.count, reads, writes)

    def set_writer_latest(self, names, stream):
        for n in names:
            self.buf(n).w = (stream, stream.count)

    def wait_all(self, eng, streams):
        for s in streams:
            if s.count > 0 and eng.waited.get(s, 0) < s.count:
                eng.h.wait_ge(s.sem, s.count * s.inc)
                eng.waited[s] = s.count


def build_program(nlayers=DEPTH, debug=False):
    nc = bass.Bass("TRN2", target_bir_lowering=False)
    es = ExitStack()

    def sb(name, shape, dt=F32):
        return es.enter_context(nc.sbuf_tensor(name, shape, dt))

    def sem(name):
        return es.enter_context(nc.semaphore(name))

    def din(name, shape):
        return nc.dram_tensor(name, shape, F32, kind="ExternalInput")

    T = Tracker()
    PE = Eng("pe", nc.tensor, sem("s_pe"))
    ACT = Eng("act", nc.scalar, sem("s_act"), selfsync=("act" in SS))
    DVE = Eng("dve", nc.vector, sem("s_dve"), selfsync=("dve" in SS))
    POOL = Eng("pool", nc.gpsimd, sem("s_pool"), selfsync=("pool" in SS))
    SP = Eng("sp", nc.sync, sem("s_sp"))
    LD = [Eng(f"ld{i}", None, sem(f"s_ld{i}"), inc=16) for i in range(NXR)]
    ST = [Eng(f"st{i}", None, sem(f"s_st{i}"), inc=16) for i in range(2)]
    LDW = Eng("ldw", None, sem("s_ldw"), inc=16)
    LDS = [Eng(f"lds{i}", None, sem(f"s_lds{i}"), inc=16) for i in range(2)]
    LDC = Eng("ldc", None, sem("s_ldc"), inc=16)
    LDT = Eng("ldt", None, sem("s_ldt"), inc=16)

    x_d = din("x", [NE * 128, D])
    valid_d = din("valid", [1, NE])
    w_in_d = din("w_in", [DEPTH, D, DIN])
    w_out_d = din("w_out", [DEPTH, D, D])
    ln_in_g_d = din("ln_in_g", [1, D]); ln_in_b_d = din("ln_in_b", [1, D])
    post_g_d = din("post_g", [DEPTH, D]); post_b_d = din("post_b", [DEPTH, D])
    gm_g_d = din("gm_g", [DEPTH, 128, 2]); gm_b_d = din("gm_b", [DEPTH, 256])
    wsT_d = din("wsT", [DEPTH, 128, 4, 128]); b_sp_d = din("b_sp", [DEPTH, 4, 128])
    conv_wT_d = din("conv_wT", [DEPTH, 128, 2, 31]); conv_b_d = din("conv_b", [DEPTH, 256])
    conv_g_d = din("conv_g", [DEPTH, 256]); conv_bb_d = din("conv_bb", [DEPTH, 256])
    sink_d = din("sink", [DEPTH, 8]); relb_d = din("rel_bias", [32, 8]); oh_d = din("oh", [33, 512])
    ident_d = din("ident", [128, 128]); jflip_d = din("jflip", [128, 128])
    out_d = nc.dram_tensor("out", [TOK_CORE, D], F32, kind="ExternalOutput")
    x1s_d = nc.dram_tensor("x1s", [NE * 128, D], F32, kind="ExternalOutput") if debug else nc.dram_tensor("x1s", [NE * 128, D], F32)
    fv_d = nc.dram_tensor("fvs", [8, 512], F32)

    win = sb("win", [128, 8, DIN], BF16)
    wout = sb("wout", [128, 8, D], BF16)
    dg = sb("dg", [128, 2, 31, 128], BF16)
    XR = [sb(f"xr{i}", [128, D]) for i in range(NXR)]
    Z = [sb(f"z{i}", [128, D]) for i in range(2)]
    xT = sb("xT", [128, D], BF16)
    mixT = sb("mixT", [128, D], BF16)
    Gin = sb("Gin", [128, D]); Bin = sb("Bin", [128, D]); Gp = sb("Gp", [128, D]); Bp = sb("Bp", [128, D])
    expb = sb("expb", [128, 3, 8, 128], BF16)
    PT = sb("PT", [128, 3, 8, 128], BF16)
    GUS = [sb(f"gus{i}", [128, 256]) for i in range(3)]
    VH = [sb(f"vh{i}", [128, 256], BF16) for i in range(3)]
    SG2B = [sb(f"sg2b{i}", [128, 256]) for i in range(3)]
    QT = [sb(f"qt{i}", [128, 512], BF16) for i in range(3)]
    SG2C = [sb(f"sg2c{i}", [128, 512]) for i in range(3)]
    KT = [sb(f"kt{i}", [128, 128], BF16) for i in range(4)]
    VA = [sb(f"va{i}", [128, 2, 65], BF16) for i in range(4)]
    YT = sb("YT", [128, 2, 4 * 128 + 30], BF16)
    tA1 = sb("tA1", [128, 256]); tA2 = sb("tA2", [128, 256]); tA3 = sb("tA3", [128, 256]); tA4 = sb("tA4", [128, 256])
    tB1 = sb("tB1", [128, 256]); tB2 = sb("tB2", [128, 256]); tC1 = sb("tC1", [128, 512])
    tSG = sb("tSG", [128, 256]); tCN = sb("tCN", [128, 256]); tCT = sb("tCT", [128, 256])
    On = sb("On", [128, 512]); YB = sb("YB", [128, 256], BF16); YC = sb("YC", [128, 512], BF16)
    ident_f = sb("ident_f", [128, 128]); ident_b = sb("ident_b", [128, 128], BF16); jflip = sb("jflip_sb", [128, 128])
    cwT = sb("cwT", [128, 2, 31]); wsT = sb("wsT_sb", [128, 4, 128], BF16); LBb = sb("LBb", [128, 256], BF16)
    bsT = sb("bsT", [128, 2, 128]); Cst = sb("Cst", [128, 2, 128]); gmg = sb("gmg", [128, 2])
    Gc = sb("Gc", [128, 256]); Bc = sb("Bc", [128, 256]); cbrow = sb("cbrow", [1, 256], BF16); ones_b = sb("ones_b", [1, 128], BF16)
    esink = sb("esink", [128, 8]); vcol = sb("vcol", [128, NE]); onescol = sb("onescol", [128, 2, 1])
    RB = sb("RB", [33, 8]); OHt = sb("OHt", [33, 512]); fvs = sb("fvs_sb", [8, 512]); hank = sb("hank", [128, 8, 128])
    mhalf = sb("mhalf", [128, 1])
    stg = [sb(f"stg{i}", [128, 1408]) for i in range(2)]
    den = sb("den", [128, 8]); rc = sb("rc", [128, 8])
    lnt = {}
    for tag in ("lnin", "lngm", "lncv", "lnpo"):
        lnt[tag] = dict(st=sb(tag + "_st", [128, 12]), mv=sb(tag + "_mv", [128, 2]), ve=sb(tag + "_ve", [128, 1]),
                        rs=sb(tag + "_rs", [128, 1]), nm=sb(tag + "_nm", [128, 1]))

    banks = [es.enter_context(nc.psum_tensor(f"bank{i}", [128, 512], F32)) for i in range(8)]
    bstate = [0]

    sstate = [0]

    def newbank(slow=False):
        if slow:
            i = sstate[0] % NSLOW
            sstate[0] += 1
        else:
            i = NSLOW + bstate[0] % (8 - NSLOW)
            bstate[0] += 1
        return banks[i], f"bank{i}"

    V, G, A_, P_ = nc.vector, nc.gpsimd, nc.scalar, nc.tensor

    def bc_rows(dram, row, n):
        return bass.AP(dram, row * n, [[0, 128], [1, n]])

    T.dma(SP, LDC, ident_f[:], ident_d.ap()[:, :], [], ["ident_f"])
    T.dma(SP, LDC, vcol[:], bc_rows(valid_d, 0, NE), [], ["vcol"])
    T.dma(SP, LDC, Gin[:], bc_rows(ln_in_g_d, 0, D), [], ["Gin"])
    T.dma(SP, LDC, Bin[:], bc_rows(ln_in_b_d, 0, D), [], ["Bin"])
    T.set_writer_latest(["ident_f", "vcol", "Gin", "Bin"], LDC)
    T.op(DVE, ["ident_f"], ["ident_b"], lambda: V.tensor_copy(out=ident_b[:], in_=ident_f[:]))
    T.op(DVE, [], ["mhalf"], lambda: V.memset(mhalf[:], -0.5))
    T.op(DVE, [], ["onescol"], lambda: V.memset(onescol[:], 1.0))
    T.op(DVE, [], ["ones_b"], lambda: V.memset(ones_b[:], 1.0))
    T.op(POOL, [], ["YT"], lambda: G.memset(YT[:], 0.0))

    def setup_bias():
        T.dma(SP, LDC, jflip[:], jflip_d.ap()[:, :], [], ["jflip"])
        T.dma(SP, LDC, OHt[:], oh_d.ap()[:, :], [], ["OHt"])
        T.op(DVE, [], ["RB"], lambda: V.memset(RB[:], 1.0))
        T.dma(SP, LDC, RB[0:32, :], relb_d.ap()[:, :], [], ["RB"])
        T.set_writer_latest(["jflip", "OHt", "RB"], LDC)
        bk, bn = newbank()
        T.op(PE, ["RB", "OHt"], [bn], lambda: P_.matmul(bk[0:8, :], lhsT=RB[0:33, 0:8], rhs=OHt[0:33, :], start=True, stop=True))
        T.op(DVE, [bn], ["fvs"], lambda: V.tensor_copy(out=fvs[:], in_=bk[0:8, :]))
        T.dma(POOL, LDT, fv_d.ap()[:, :], fvs[:], ["fvs"], ["fv_d"])
        for kc in range(3):
            T.dma(POOL, LDT, hank[:], bass.AP(fv_d, 256 - kc * 128, [[1, 128], [512, 8], [1, 128]]), ["fv_d"], ["hank"])
            for hh in range(2):
                bk, bn = newbank()
                T.op(PE, ["hank", "jflip"], [bn],
                     lambda: P_.matmul(bk[:, :], lhsT=jflip[:], rhs=hank[:, hh * 4:(hh + 1) * 4, :], start=True, stop=True))
                T.op(ACT, [bn], ["expb"],
                     lambda: A_.activation(out=expb[:, kc, hh * 4:(hh + 1) * 4, :].rearrange("p h q -> p (h q)"), in_=bk[:, :], func=AF.Copy, scale=8.0))


    WIN_NAMES = [f"win{k}" for k in range(16)]
    WOUT_NAMES = [f"wout{k}" for k in range(8)]
    stg_i = [0]
    cast_mode = ["dve"]

    def staged_cast(dst_ap, src_ap, dstname, n):
        i = stg_i[0] % 2
        stg_i[0] += 1
        T.dma(SP, LDS[i], stg[i][:, 0:n], src_ap, [], [f"stg{i}"])
        if i == 0 and cast_mode[0] == "alt":
            T.op(ACT, [f"stg{i}"], [dstname], lambda: A_.activation(out=dst_ap, in_=stg[i][:, 0:n], func=AF.Copy))
        else:
            T.op(DVE, [f"stg{i}"], [dstname], lambda: V.tensor_copy(out=dst_ap, in_=stg[i][:, 0:n]))

    pending = []

    def pump(n=1):
        for _ in range(n):
            if pending:
                pending.pop(0)()

    def queue_win(l):
        for kc in range(8):
            for hf in range(2):
                pending.append(lambda kc=kc, hf=hf: staged_cast(
                    win[:, kc, hf * 1408:(hf + 1) * 1408], w_in_d.ap()[l, kc * 128:(kc + 1) * 128, hf * 1408:(hf + 1) * 1408],
                    WIN_NAMES[kc * 2 + hf], 1408))

    def queue_wout(l):
        for kc in range(8):
            pending.append(lambda kc=kc: staged_cast(wout[:, kc, :], w_out_d.ap()[l, kc * 128:(kc + 1) * 128, :], WOUT_NAMES[kc], 1024))

    def load_layer(l):
        T.dma(POOL, LDW, wsT[:], wsT_d.ap()[l, :, :, :], [], ["wsT"])
        T.dma(POOL, LDW, LBb[:], bc_rows(gm_b_d, l, 256), [], ["LBb"])
        T.dma(POOL, LDW, cbrow[:], conv_b_d.ap()[l:l + 1, :], [], ["cbrow"])
        T.set_writer_latest(["wsT", "LBb", "cbrow"], LDW)
        T.dma(SP, LDC, cwT[:], conv_wT_d.ap()[l, :, :, :], [], ["cwT"])
        T.dma(SP, LDC, gmg[:], gm_g_d.ap()[l, :, :], [], ["gmg"])
        for h in range(4):
            T.dma(SP, LDC, bsT[(h % 2) * 64:(h % 2 + 1) * 64, h // 2, :],
                  bass.AP(b_sp_d, (l * 4 + h) * 128, [[0, 64], [1, 128]]), [], ["bsT"])
        T.dma(SP, LDC, Gc[:], bc_rows(conv_g_d, l, 256), [], ["Gc"])
        T.dma(SP, LDC, Bc[:], bc_rows(conv_bb_d, l, 256), [], ["Bc"])
        T.dma(SP, LDC, esink[:], bc_rows(sink_d, l, 8), [], ["esink"])
        T.dma(SP, LDC, Gp[:], bc_rows(post_g_d, l, D), [], ["Gp"])
        T.dma(SP, LDC, Bp[:], bc_rows(post_b_d, l, D), [], ["Bp"])
        T.set_writer_latest(["cwT", "gmg", "bsT", "Gc", "Bc", "esink", "Gp", "Bp"], LDC)
        T.op(ACT, ["esink"], ["esink"], lambda: A_.activation(out=esink[:], in_=esink[:], func=AF.Exp))
        T.op(DVE, ["gmg"], ["gmg"], lambda: V.tensor_scalar(out=gmg[:], in0=gmg[:], scalar1=0.25, scalar2=None, op0=ALU.mult))
        def dg_piece(ch, k0, k1):
            for k in range(k0, k1):
                T.op(POOL, ["ident_f", "cwT"], ["dg"],
                     lambda: G.tensor_scalar(out=dg[:, ch, k, :], in0=ident_f[:], scalar1=cwT[:, ch, k:k + 1], scalar2=0.5,
                                             op0=ALU.mult, op1=ALU.mult))
        for ch in range(2):
            for k0 in range(0, 31, 8):
                pending.append(lambda ch=ch, k0=k0: dg_piece(ch, k0, min(k0 + 8, 31)))
        bk, bn = newbank()

        def f():
            for h in range(4):
                ins = P_.matmul(bk[(h % 2) * 64:(h % 2 + 1) * 64, (h // 2) * 128:(h // 2 + 1) * 128],
                                lhsT=LBb[:, h * 64:(h + 1) * 64], rhs=wsT[:, h, :], start=True, stop=True)
            return ins
        T.op(PE, ["LBb", "wsT"], [bn], f)
        T.op(DVE, [bn, "bsT"], ["Cst"], lambda: V.tensor_tensor(out=Cst[:].rearrange("p c i -> p (c i)"), in0=bk[:, 0:256],
                                                               in1=bsT[:].rearrange("p c i -> p (c i)"), op=ALU.add))
        T.op(DVE, ["Cst"], ["Cst"], lambda: V.tensor_scalar(out=Cst[:], in0=Cst[:], scalar1=0.25, scalar2=None, op0=ALU.mult))

    def ln_a(tag, aps, srcbufs):
        t = lnt[tag]
        for i, ap in enumerate(aps):
            T.op(DVE, srcbufs, [tag + "st"], lambda: V.bn_stats(out=t["st"][:, 6 * i:6 * i + 6], in_=ap))
        k = len(aps)
        T.op(DVE, [tag + "st"], [tag + "mv"], lambda: V.bn_aggr(out=t["mv"][:, 0:2], in_=t["st"][:, 0:6 * k]))

    def ln_b(tag, eps):
        t = lnt[tag]
        T.op(POOL, [tag + "mv"], [tag + "ve"], lambda: G.tensor_scalar(out=t["ve"][:], in0=t["mv"][:, 1:2], scalar1=eps, scalar2=None, op0=ALU.add))
        T.op(POOL, [tag + "ve", "mhalf"], [tag + "rs"], lambda: G.tensor_tensor(out=t["rs"][:], in0=t["ve"][:], in1=mhalf[:], op=ALU.pow))
        T.op(POOL, [tag + "mv", tag + "rs"], [tag + "nm"],
             lambda: G.tensor_scalar(out=t["nm"][:], in0=t["mv"][:, 0:1], scalar1=t["rs"][:], scalar2=-1.0, op0=ALU.mult, op1=ALU.mult))
        return t["rs"], t["nm"], [tag + "rs", tag + "nm"]

    def issue_load(l, b):
        X = XR[b % NXR]
        if l == 0:
            T.dma(SP, LD[b % NXR], X[:], x_d.ap()[b * 128:(b + 1) * 128, :], [], [f"xr{b % NXR}"])
        else:
            T.dma(SP, LD[b % NXR], X[:], x1s_d.ap()[b * 128:(b + 1) * 128, :], [f"x1s{b}"], [f"xr{b % NXR}"])

    def lnin_pieces(l, b):
        X = XR[b % NXR]; xb = f"xr{b % NXR}"
        st = {}

        def a():
            ln_a("lnin", [X[:, 0:512], X[:, 512:1024]], [xb])

        def bb():
            rs, nm, lb = ln_b("lnin", EPS)
            T.op(ACT, [xb] + lb, [xb], lambda: A_.activation(out=X[:], in_=X[:], func=AF.Identity, scale=rs[:], bias=nm[:]))

        def c():
            T.op(DVE, [xb, "Gin"], [xb], lambda: V.tensor_tensor(out=X[:], in0=X[:], in1=Gin[:], op=ALU.mult))

        def d():
            T.op(POOL, [xb, "Bin"], [xb], lambda: G.tensor_tensor(out=X[:], in0=X[:], in1=Bin[:], op=ALU.add))
        if l != 0:
            return [lambda: None] * 4
        return [a, bb, c, d]

    def lnin_stage(l, b):
        for fn in lnin_pieces(l, b):
            fn()

    def input_pieces(l, b):
        X = XR[b % NXR]; xb = f"xr{b % NXR}"
        s2 = b % 3; r4 = b % 4
        xr = WIN_NAMES + ["xT0", "xT1"]
        st = {}

        def fm(bk, col, ci):
            for kc in range(8):
                ins = P_.matmul(bk[:, col:col + 128], lhsT=win[:, kc, ci * 128:(ci + 1) * 128], rhs=xT[:, kc * 128:(kc + 1) * 128],
                                start=(kc == 0), stop=(kc == 7))
            return ins

        def tm(bk, col, w0, n):
            for kc in range(8):
                ins = P_.matmul(bk[:, col:col + n], lhsT=xT[:, kc * 128:(kc + 1) * 128], rhs=win[:, kc, w0:w0 + n],
                                start=(kc == 0), stop=(kc == 7))
            return ins

        def p_tr():
            for half in range(2):
                bk, bn = newbank()

                def f():
                    for i in range(4):
                        kc = half * 4 + i
                        ins = P_.transpose(out=bk[:, i * 128:(i + 1) * 128], in_=X[:, kc * 128:(kc + 1) * 128], identity=ident_f[:])
                    return ins
                T.op(PE, [xb, "ident_f"], [bn], f)
                T.op(ACT, [bn], [f"xT{half}"], lambda: A_.activation(out=xT[:, half * 512:(half + 1) * 512], in_=bk[:, :], func=AF.Copy))

        def p_j4():
            b4, n4 = newbank(slow=True)
            st["b4"], st["n4"] = b4, n4

            def f4():
                fm(b4, 0, 12)
                tm(b4, 128, T_CV, 128)
                return tm(b4, 256, T_AV, 256)
            T.op(PE, xr, [n4], f4)
            T.op(DVE, [n4], [f"kt{r4}"], lambda: V.tensor_copy(out=KT[r4][:], in_=b4[:, 0:128]))
            T.op(DVE, [n4, "vcol"], [f"va{r4}"],
                 lambda: V.tensor_scalar(out=VA[r4][:, :, 0:64], in0=b4[:, 128:256].rearrange("p (g d) -> p g d", g=2),
                                         scalar1=vcol[:, b:b + 1], scalar2=None, op0=ALU.mult))
            T.op(DVE, ["onescol", "vcol"], [f"va{r4}"],
                 lambda: V.tensor_scalar(out=VA[r4][:, :, 64:65], in0=onescol[:], scalar1=vcol[:, b:b + 1], scalar2=None, op0=ALU.mult))

        def p_j2():
            b2, n2 = newbank()

            def f2():
                for i in range(4):
                    ins = fm(b2, i * 128, 4 + i)
                return ins
            T.op(PE, xr, [n2], f2)
            T.op(ACT, [n2], ["tB1"], lambda: A_.activation(out=tB1[:], in_=b2[:, 256:512], func=AF.Tanh, scale=0.5))
            c0 = 15 + r4 * 128
            T.op(DVE, ["tB1", n2], [f"YT{r4}"],
                 lambda: V.scalar_tensor_tensor(out=YT[:, :, c0:c0 + 128], in0=tB1[:].rearrange("p (c t) -> p c t", c=2), scalar=1.0,
                                                in1=b2[:, 0:256].rearrange("p (c t) -> p c t", c=2), op0=ALU.add, op1=ALU.mult))
            if b in (0, 1, NE - 2, NE - 1):
                T.op(POOL, [f"YT{r4}", "vcol"], [f"YT{r4}"],
                     lambda: G.tensor_scalar(out=YT[:, :, c0:c0 + 128], in0=YT[:, :, c0:c0 + 128], scalar1=vcol[:, b:b + 1], scalar2=None, op0=ALU.mult))
            if r4 == 3:
                T.op(POOL, ["YT3"], ["YTL"], lambda: G.tensor_copy(out=YT[:, :, 0:15], in_=YT[:, :, 512:527]))
            if r4 == 0:
                T.op(POOL, ["YT0"], ["YTR"], lambda: G.tensor_copy(out=YT[:, :, 527:542], in_=YT[:, :, 15:30]))

        def p_j3():
            b3, n3 = newbank()

            def f3():
                for i in range(4):
                    ins = fm(b3, i * 128, 8 + i)
                return ins
            T.op(PE, xr, [n3], f3)
            T.op(ACT, [n3], [f"qt{s2}"], lambda: A_.activation(out=QT[s2][:], in_=b3[:, :], func=AF.Copy))

        def p_j1pe():
            b1, n1 = newbank(slow=True)
            st["b1"], st["n1"] = b1, n1

            def f1():
                for i in range(4):
                    ins = fm(b1, i * 128, i)
                return ins
            T.op(PE, xr, [n1], f1)

        def p_j1ev():
            b1, n1 = st["b1"], st["n1"]
            au = b1[:, 0:256]; ag = b1[:, 256:512]
            T.op(ACT, [n1], ["tA1"], lambda: A_.activation(out=tA1[:], in_=au, func=AF.Square, scale=math.sqrt(GA)))
            T.op(ACT, [n1], ["tA2"], lambda: A_.activation(out=tA2[:], in_=ag, func=AF.Tanh, scale=0.5))
            T.op(DVE, ["tA1", n1], ["tA1"], lambda: V.scalar_tensor_tensor(out=tA1[:], in0=tA1[:], scalar=1.0, in1=au, op0=ALU.add, op1=ALU.mult))
            T.op(DVE, ["tA2", n1], ["tA2"], lambda: V.scalar_tensor_tensor(out=tA2[:], in0=tA2[:], scalar=1.0, in1=ag, op0=ALU.add, op1=ALU.mult))
            T.op(ACT, ["tA1"], ["tA1"], lambda: A_.activation(out=tA1[:], in_=tA1[:], func=AF.Tanh, scale=GC))
            T.op(DVE, ["tA1", n1], ["tA1"], lambda: V.scalar_tensor_tensor(out=tA1[:], in0=tA1[:], scalar=1.0, in1=au, op0=ALU.add, op1=ALU.mult))
            T.op(POOL, ["tA1", "tA2"], [f"gus{s2}"], lambda: G.tensor_tensor(out=GUS[s2][:], in0=tA1[:], in1=tA2[:], op=ALU.mult))

        def p_av():
            b4, n4 = st["b4"], st["n4"]
            av = b4[:, 256:512]
            T.op(ACT, [n4], ["tA3"], lambda: A_.activation(out=tA3[:], in_=av, func=AF.Square, scale=math.sqrt(GA)))
            T.op(DVE, ["tA3", n4], ["tA3"], lambda: V.scalar_tensor_tensor(out=tA3[:], in0=tA3[:], scalar=1.0, in1=av, op0=ALU.add, op1=ALU.mult))
            T.op(ACT, ["tA3"], ["tA3"], lambda: A_.activation(out=tA3[:], in_=tA3[:], func=AF.Tanh, scale=GC))
            T.op(DVE, ["tA3", n4], ["tA4"], lambda: V.scalar_tensor_tensor(out=tA4[:], in0=tA3[:], scalar=1.0, in1=av, op0=ALU.add, op1=ALU.mult))
            ln_a("lngm", [tA4[:]], ["tA4"])
            rs, nm, lb = ln_b("lngm", 4.0 * EPS)
            T.op(POOL, ["tA4"] + lb, [f"vh{s2}"],
                 lambda: G.tensor_scalar(out=VH[s2][:], in0=tA4[:], scalar1=rs[:], scalar2=nm[:], op0=ALU.mult, op1=ALU.add))

        def p_j5():
            b5, n5 = newbank()
            T.op(PE, xr, [n5], lambda: tm(b5, 0, T_BG, 256))
            T.op(ACT, [n5], ["tB2"], lambda: A_.activation(out=tB2[:], in_=b5[:, 0:256], func=AF.Tanh, scale=0.5))
            T.op(DVE, ["tB2", n5], [f"sg2b{s2}"],
                 lambda: V.scalar_tensor_tensor(out=SG2B[s2][:], in0=tB2[:], scalar=1.0, in1=b5[:, 0:256], op0=ALU.add, op1=ALU.mult))

        def p_j6():
            b6, n6 = newbank()
            T.op(PE, xr, [n6], lambda: tm(b6, 0, T_CG, 512))
            T.op(ACT, [n6], ["tC1"], lambda: A_.activation(out=tC1[:], in_=b6[:, :], func=AF.Tanh, scale=0.5))
            T.op(DVE, ["tC1", n6], [f"sg2c{s2}"],
                 lambda: V.scalar_tensor_tensor(out=SG2C[s2][:], in0=tC1[:], scalar=1.0, in1=b6[:, :], op0=ALU.add, op1=ALU.mult))

        def p_j1():
            p_j1pe()
            p_j1ev()

        return dict(tr=p_tr, j4=p_j4, j2=p_j2, j3=p_j3, j1=p_j1, j1pe=p_j1pe, j1ev=p_j1ev, av=p_av, j5=p_j5, j6=p_j6)

    def mix_pieces(l, e, nl_last):
        s2 = e % 3; r4 = e % 4; z2 = e % 2
        X = XR[e % NXR]; xb = f"xr{e % NXR}"
        Zt = Z[z2]; zb = f"z{z2}"
        st = {}

        def p_gconv():
            bk, bn = newbank(slow=True)
            st["bk"], st["bn"] = bk, bn

            def fg():
                for h in range(4):
                    P_.matmul(bk[(h % 2) * 64:(h % 2 + 1) * 64, (h // 2) * 128:(h // 2 + 1) * 128],
                              lhsT=VH[s2][:, h * 64:(h + 1) * 64], rhs=wsT[:, h, :], start=True, stop=True)
                for ch in range(2):
                    P_.matmul(bk[:, 256 + ch * 128:256 + (ch + 1) * 128], lhsT=ones_b[0:1, 0:128], rhs=cbrow[0:1, ch * 128:(ch + 1) * 128],
                              start=True, stop=False)
                    for k in range(31):
                        ins = P_.matmul(bk[:, 256 + ch * 128:256 + (ch + 1) * 128], lhsT=YT[:, ch, r4 * 128 + k:r4 * 128 + k + 128],
                                        rhs=dg[:, ch, k, :], start=False, stop=(k == 30))
                return ins
            ytn = [f"YT{(e - 1) % 4}", f"YT{r4}", f"YT{(e + 1) % 4}"] + (["YTL"] if r4 == 0 else []) + (["YTR"] if r4 == 3 else [])
            T.op(PE, [f"vh{s2}", "wsT", "ones_b", "cbrow", "dg"] + ytn, [bn], fg)

        def p_scores(kcs=(0, 1, 2)):
            for kc in kcs:
                ks = (e - 1 + kc) % 4
                bb = [newbank() for _ in range(2)]

                def fs():
                    for g in range(2):
                        P_.matmul(bb[g][0][:, :], lhsT=KT[ks][g * 64:(g + 1) * 64, :], rhs=QT[s2][g * 64:(g + 1) * 64, :], start=True, stop=False)
                    for g in range(2):
                        ins = P_.matmul(bb[g][0][:, :], lhsT=ident_b[:], rhs=expb[:, kc, g * 4:(g + 1) * 4, :], start=False, stop=True)
                    return ins
                T.op(PE, [f"kt{ks}", f"qt{s2}", "ident_b", "expb"], [bb[0][1], bb[1][1]], fs)
                for g in range(2):
                    T.op(ACT, [bb[g][1]], [f"PT{g}"],
                         lambda: A_.activation(out=PT[:, kc, g * 4:(g + 1) * 4, :].rearrange("p h q -> p (h q)"), in_=bb[g][0][:, :], func=AF.Exp, scale=0.125))

        def p_conv1():
            bk, bn = st["bk"], st["bn"]
            cv = bk[:, 256:512]
            ln_a("lncv", [cv], [bn])
            rs, nm, lb = ln_b("lncv", EPS)
            T.op(ACT, [bn] + lb, ["tCN"], lambda: A_.activation(out=tCN[:], in_=cv, func=AF.Identity, scale=rs[:], bias=nm[:]))

        def p_conv2():
            T.op(DVE, ["tCN", "Gc"], ["tCN"], lambda: V.tensor_tensor(out=tCN[:], in0=tCN[:], in1=Gc[:], op=ALU.mult))
            T.op(DVE, ["tCN", "Bc"], ["tCN"], lambda: V.tensor_tensor(out=tCN[:], in0=tCN[:], in1=Bc[:], op=ALU.add))
            T.op(ACT, ["tCN"], ["tCT"], lambda: A_.activation(out=tCT[:], in_=tCN[:], func=AF.Tanh, scale=0.5))

        def p_conv3():
            T.op(DVE, ["tCT", "tCN"], ["tCT"], lambda: V.scalar_tensor_tensor(out=tCT[:], in0=tCT[:], scalar=1.0, in1=tCN[:], op0=ALU.add, op1=ALU.mult))
            T.op(DVE, ["tCT", f"sg2b{s2}"], ["YB"], lambda: V.scalar_tensor_tensor(out=YB[:], in0=tCT[:], scalar=0.25, in1=SG2B[s2][:], op0=ALU.mult, op1=ALU.mult))

        def p_gmlp():
            bk, bn = st["bk"], st["bn"]
            for ch in range(2):
                T.op(DVE, [bn, "gmg", "Cst"], ["tSG"],
                     lambda: V.scalar_tensor_tensor(out=tSG[:, ch * 128:(ch + 1) * 128], in0=bk[:, ch * 128:(ch + 1) * 128], scalar=gmg[:, ch:ch + 1],
                                                    in1=Cst[:, ch, :], op0=ALU.mult, op1=ALU.add))
            T.op(POOL, ["tSG", f"gus{s2}"], ["mixA"], lambda: G.tensor_tensor(out=mixT[:, 0:256], in0=tSG[:], in1=GUS[s2][:], op=ALU.mult))

        def p_pv():
            for g in range(2):
                bo, bon = newbank()
                po = bo[:, 0:260].rearrange("p (h c) -> p h c", h=4)

                def fpv():
                    for hh in range(4):
                        for kc in range(3):
                            ks = (e - 1 + kc) % 4
                            ins = P_.matmul(po[:, hh, :], lhsT=PT[:, kc, g * 4 + hh, :], rhs=VA[ks][:, g, :], start=(kc == 0), stop=(kc == 2))
                    return ins
                T.op(PE, [f"PT{g}"] + [f"va{(e - 1 + kc) % 4}" for kc in range(3)], [bon], fpv)
                T.op(DVE, [bon, "esink"], [f"den{g}"],
                     lambda: V.tensor_tensor(out=den[:, g * 4:(g + 1) * 4].unsqueeze(2), in0=po[:, :, 64:65],
                                             in1=esink[:, g * 4:(g + 1) * 4].unsqueeze(2), op=ALU.add))
                T.op(DVE, [f"den{g}"], [f"rc{g}"], lambda: V.reciprocal(out=rc[:, g * 4:(g + 1) * 4], in_=den[:, g * 4:(g + 1) * 4]))
                T.op(DVE, [bon, f"rc{g}"], [f"On{g}"],
                     lambda: V.tensor_tensor(out=On[:, g * 256:(g + 1) * 256].rearrange("p (h d) -> p h d", h=4), in0=po[:, :, 0:64],
                                             in1=rc[:, g * 4:(g + 1) * 4].unsqueeze(2).to_broadcast([128, 4, 64]), op=ALU.mult))
                T.op(POOL, [f"On{g}", f"sg2c{s2}"], [f"YC{g}"],
                     lambda: G.tensor_tensor(out=YC[:, g * 256:(g + 1) * 256], in0=On[:, g * 256:(g + 1) * 256],
                                             in1=SG2C[s2][:, g * 256:(g + 1) * 256], op=ALU.mult))

        def p_mixtr():
            bt, btn = newbank()
            btb = bt[:].bitcast(BF16)

            def ftr():
                for c in range(2):
                    P_.transpose(out=btb[:, c * 128:(c + 1) * 128], in_=YB[:, c * 128:(c + 1) * 128], identity=ident_b[:])
                for c in range(4):
                    ins = P_.transpose(out=btb[:, (2 + c) * 128:(3 + c) * 128], in_=YC[:, c * 128:(c + 1) * 128], identity=ident_b[:])
                return ins
            T.op(PE, ["YB", "YC0", "YC1", "ident_b"], [btn], ftr)
            T.op(ACT, [btn], ["mixB"], lambda: A_.activation(out=mixT[:, 256:512], in_=btb[:, 0:256], func=AF.Copy))
            T.op(ACT, [btn], ["mixC"], lambda: A_.activation(out=mixT[:, 512:1024], in_=btb[:, 256:768], func=AF.Copy, scale=0.5))

        def p_outproj():
            for n in range(2):
                by, byn = newbank()

                def fo():
                    for kc in range(8):
                        ins = P_.matmul(by[:, :], lhsT=mixT[:, kc * 128:(kc + 1) * 128], rhs=wout[:, kc, n * 512:(n + 1) * 512],
                                        start=(kc == 0), stop=(kc == 7))
                    return ins
                T.op(PE, ["mixA", "mixB", "mixC"] + WOUT_NAMES, [byn], fo)
                T.op(DVE, [byn, xb], [zb + f"h{n}"],
                     lambda: V.scalar_tensor_tensor(out=Zt[:, n * 512:(n + 1) * 512], in0=X[:, n * 512:(n + 1) * 512], scalar=ALPHA, in1=by[:, :],
                                                    op0=ALU.mult, op1=ALU.add))

        def p_postln_a():
            ln_a("lnpo", [Zt[:, 0:512], Zt[:, 512:1024]], [zb + "h0", zb + "h1"])

        def p_postln_b():
            rs, nm, lb = ln_b("lnpo", EPS)
            T.op(ACT, [zb + "h0", zb + "h1"] + lb, [zb], lambda: A_.activation(out=Zt[:], in_=Zt[:], func=AF.Identity, scale=rs[:], bias=nm[:]))

        def p_postln_c():
            T.op(DVE, [zb, "Gp"], [zb], lambda: V.tensor_tensor(out=Zt[:], in0=Zt[:], in1=Gp[:], op=ALU.mult))

        def p_postln_d():
            T.op(POOL, [zb, "Bp"], [zb], lambda: G.tensor_tensor(out=Zt[:], in0=Zt[:], in1=Bp[:], op=ALU.add))
            if not nl_last:
                T.dma(SP, ST[z2], x1s_d.ap()[e * 128:(e + 1) * 128, :], Zt[:], [zb], [f"x1s{e}", zb + "h0", zb + "h1"])
            else:
                r0 = (e - 2) * 128
                T.dma(SP, ST[z2], out_d.ap()[r0:r0 + 128, :], Zt[:], [zb], [zb + "h0", zb + "h1"])

        return dict(gconv=p_gconv, scores=p_scores, sc0=lambda: p_scores((0,)), sc1=lambda: p_scores((1,)), sc2=lambda: p_scores((2,)), conv1=p_conv1, conv2=p_conv2, conv3=p_conv3, gmlp=p_gmlp, pv=p_pv,
                    mixtr=p_mixtr, outproj=p_outproj, postln=[p_postln_a, p_postln_b, p_postln_c, p_postln_d])

    IN_ALL = ["tr", "j4", "j2", "j3", "j1", "av", "j5", "j6"]
    STEP = STEP_ORDER.split()

    def run_input(l, b):
        p = input_pieces(l, b)
        for k in IN_ALL:
            p[k]()
            pump(1)

    for l in range(nlayers):
        first_b, last_b = l, NE - 1 - l
        nl_last = (l == nlayers - 1 and nlayers == DEPTH)
        for b in range(first_b, first_b + NXR):
            issue_load(l, b)
        for b in range(first_b, first_b + NXR):
            lnin_stage(l, b)
        if l == 0:
            setup_bias()
            queue_win(0)
            pump(16)
            cast_mode[0] = "alt"
        queue_wout(l)
        run_input(l, first_b)
        load_layer(l)
        issue_load(l, first_b + NXR)
        for b in range(first_b + 1, first_b + 3):
            run_input(l, b)
        pump(len(pending))
        deferred = [[], [], [], []]
        for e in range(l + 1, NE - 1 - l):
            pi = input_pieces(l, e + 2) if e + 2 <= last_b else None
            pm = mix_pieces(l, e, nl_last)
            for tok in STEP:
                pump(1)
                kind, name = tok.split(".")
                if kind == "m":
                    pm[name]()
                elif kind == "i":
                    if pi is not None:
                        pi[name]()
                elif kind == "d":
                    k = int(name)
                    for fn in deferred[k]:
                        fn()
                    deferred[k] = []
            for k in range(4):
                deferred[k].append(pm["postln"][k])
            if e + 4 <= last_b and e + 4 > first_b + 4:
                lp = lnin_pieces(l, e + 4)
                for k in range(4):
                    deferred[k].append(lp[k])
            if e + NXR <= last_b and e + NXR > first_b + NXR:
                issue_load(l, e + NXR)
            if e + 2 == last_b and l + 1 < nlayers:
                queue_win(l + 1)
        for k in range(4):
            for fn in deferred[k]:
                fn()
        pump(len(pending))
    T.wait_all(SP, ST)
    return nc


_PROG = {}


def t5_bucket(rel):
    nb = 16
    max_exact = 8
    ret = jnp.where(rel > 0, nb, 0)
    n = jnp.abs(rel)
    nf = jnp.maximum(n, 1).astype(jnp.float32)
    large = max_exact + (jnp.log(nf / max_exact) / math.log(128 / max_exact) * (nb - max_exact)).astype(jnp.int32)
    large = jnp.minimum(large, nb - 1)
    return ret + jnp.where(n < max_exact, n, large)


def host_consts():
    rel = 255 - np.arange(512)
    with jax.default_device(jax.devices("cpu")[0]):
        bucket = np.asarray(t5_bucket(jnp.asarray(rel, dtype=jnp.int32)))
    oh = np.zeros((33, 512), np.float32)
    oh[bucket, np.arange(512)] = 1.0
    oh[32] = np.where(np.abs(rel) <= 128, 0.0, -30000.0)
    return oh


def make_in_maps(inputs):
    f = lambda a: np.ascontiguousarray(np.asarray(a, dtype=np.float32))
    x = f(inputs["x"])
    w_in = np.ascontiguousarray(f(inputs["w_in"])[:, :, PERM])
    common = dict(
        w_in=w_in, w_out=f(inputs["w_out"]),
        ln_in_g=f(inputs["ln_in_g"]).reshape(1, D), ln_in_b=f(inputs["ln_in_b"]).reshape(1, D),
        post_g=f(inputs["post_ln_g"]), post_b=f(inputs["post_ln_b"]),
        gm_g=np.ascontiguousarray(f(inputs["gmlp_ln_g"]).reshape(DEPTH, 2, 128).transpose(0, 2, 1)),
        gm_b=f(inputs["gmlp_ln_b"]),
        wsT=np.ascontiguousarray(f(inputs["w_spatial"]).transpose(0, 3, 1, 2)),
        b_sp=f(inputs["b_spatial"]),
        conv_wT=np.ascontiguousarray(f(inputs["conv_w"]).reshape(DEPTH, 31, 2, 128).transpose(0, 3, 2, 1)),
        conv_b=f(inputs["conv_b"]), conv_g=f(inputs["conv_ln_g"]), conv_bb=f(inputs["conv_ln_b"]),
        sink=f(inputs["attn_sink"]), rel_bias=f(inputs["rel_bias"]), oh=host_consts(),
        ident=np.eye(128, dtype=np.float32), jflip=np.ascontiguousarray(np.eye(128, dtype=np.float32)[::-1]),
    )
    maps = []
    for c in range(NCORE):
        bi, sg = c // 4, c % 4
        t0 = sg * TOK_CORE - 256
        xs = np.zeros((NE * 128, D), np.float32)
        lo, hi = max(t0, 0), min(t0 + NE * 128, SEQ)
        xs[lo - t0:hi - t0] = x[bi, lo:hi]
        valid = np.zeros((1, NE), np.float32)
        for e in range(NE):
            tb = t0 + e * 128
            valid[0, e] = 1.0 if (0 <= tb < SEQ) else 0.0
        m = dict(common)
        m["x"] = xs
        m["valid"] = valid
        maps.append(m)
    return maps


def kernel(**inputs):
    if "nc" not in _PROG:
        _PROG["nc"] = build_program()
    nc = _PROG["nc"]
    maps = make_in_maps(inputs)
    res = run_bass_kernel_spmd(nc, maps, core_ids=list(range(NCORE)))
    _PROG["res"] = res
    out = np.zeros((2, SEQ, D), np.float32)
    for c in range(NCORE):
        bi, sg = c // 4, c % 4
        out[bi, sg * TOK_CORE:(sg + 1) * TOK_CORE] = res.results[c]["out"]
    return out
```

```python
import math
from contextlib import ExitStack

import numpy as np
import jax
import jax.numpy as jnp
import concourse.bass as bass
import concourse.mybir as mybir
from concourse.bass_utils import run_bass_kernel_spmd

F32 = mybir.dt.float32
BF16 = mybir.dt.bfloat16
AF = mybir.ActivationFunctionType
ALU = mybir.AluOpType

D = 1024
SEQ = 16384
NCORE = 8
TOK_CORE = 4096
NE = 36
DIN = 2816
ALPHA = 4 ** 0.25
EPS = 1e-5
GA = 0.044715
GC = 0.7978845608028654
DEPTH = 2
import os
SS = os.environ.get("KSS", "act,dve,pool").split(",")
ORDER = os.environ.get("KORDER", "mimimiml")
NSLOW = int(os.environ.get("KNSLOW", "3"))
NXR = 5
STEP_ORDER = os.environ.get("KSTEP", "m.gconv m.sc0 m.sc1 i.tr m.sc2 m.conv1 d.0 i.j4 d.1 m.conv2 m.pv d.2 m.conv3 m.gmlp i.j2 d.3 i.j3 i.j1pe i.j5 m.mixtr i.j6 m.outproj i.j1ev i.av")

_o = dict(au=0, av=256, ag=512, ba=768, bb=1024, bg=1280, cq=1536, ck=2048, cv=2176, cg=2304)
_perm = []
_perm += list(range(_o['au'], _o['au'] + 256))
_perm += list(range(_o['ag'], _o['ag'] + 256))
_perm += list(range(_o['ba'], _o['ba'] + 256))
_perm += list(range(_o['bb'], _o['bb'] + 256))
for _c in range(4):
    _perm += list(range(_o['cq'] + _c * 64, _o['cq'] + _c * 64 + 64))
    _perm += list(range(_o['cq'] + (_c + 4) * 64, _o['cq'] + (_c + 4) * 64 + 64))
_perm += list(range(_o['ck'], _o['ck'] + 128))
T_AV = len(_perm); _perm += list(range(_o['av'], _o['av'] + 256))
T_BG = len(_perm); _perm += list(range(_o['bg'], _o['bg'] + 256))
T_CV = len(_perm); _perm += list(range(_o['cv'], _o['cv'] + 128))
T_CG = len(_perm); _perm += list(range(_o['cg'], _o['cg'] + 512))
PERM = np.array(_perm)
assert len(PERM) == DIN and len(set(_perm)) == DIN


class Eng:
    def __init__(self, name, h, sem, inc=1, selfsync=False):
        self.name, self.h, self.sem, self.inc, self.selfsync = name, h, sem, inc, selfsync
        self.count = 0
        self.waited = {}


class Buf:
    __slots__ = ("w", "r")

    def __init__(self):
        self.w = None
        self.r = {}


class Tracker:
    def __init__(self):
        self.bufs = {}

    def buf(self, name):
        b = self.bufs.get(name)
        if b is None:
            b = self.bufs[name] = Buf()
        return b

    def _deps(self, reads, writes):
        need = {}
        for n in reads:
            ev = self.buf(n).w
            if ev is not None and need.get(ev[0], 0) < ev[1]:
                need[ev[0]] = ev[1]
        for n in writes:
            b = self.buf(n)
            if b.w is not None and need.get(b.w[0], 0) < b.w[1]:
                need[b.w[0]] = b.w[1]
            for e, t in b.r.items():
                if need.get(e, 0) < t:
                    need[e] = t
        return need

    def _wait(self, eng, need):
        for e, t in need.items():
            if e is eng and not eng.selfsync:
                continue
            if eng.waited.get(e, 0) >= t:
                continue
            assert t <= e.count, f"{eng.name} needs unissued tick {t} of {e.name} ({e.count})"
            eng.h.wait_ge(e.sem, t * e.inc)
            eng.waited[e] = t
            if e.inc == 16 and t > getattr(e, "max_wait", 0):
                e.max_wait = t

    def _record(self, ev_eng, tick, reads, writes):
        for n in reads:
            b = self.buf(n)
            if b.r.get(ev_eng, 0) < tick:
                b.r[ev_eng] = tick
        for n in writes:
            b = self.buf(n)
            b.w = (ev_eng, tick)
            b.r = {}

    def op(self, eng, reads, writes, fn):
        self._wait(eng, self._deps(reads, writes))
        inst = fn()
        eng.count += 1
        inst.then_inc(eng.sem, 1)
        self._record(eng, eng.count, reads, writes)

    def dma(self, q, stream, out, in_, reads, writes, **kw):
        self._wait(q, self._deps(reads, writes))
        mw = getattr(stream, "max_wait", 0)
        if mw > q.waited.get(stream, 0):
            q.h.wait_ge(stream.sem, mw * stream.inc)
            q.waited[stream] = mw
        inst = q.h.dma_start(out=out, in_=in_, **kw)
        stream.count += 1
        inst.then_inc(stream.sem, 16)
        self._record(stream, stream.count, reads, writes)

    def set_writer_latest(self, names, stream):
        for n in names:
            self.buf(n).w = (stream, stream.count)

    def wait_all(self, eng, streams):
        for s in streams:
            if s.count > 0 and eng.waited.get(s, 0) < s.count:
                eng.h.wait_ge(s.sem, s.count * s.inc)
                eng.waited[s] = s.count


def build_program(nlayers=DEPTH, debug=False):
    nc = bass.Bass("TRN2", target_bir_lowering=False)
    es = ExitStack()

    def sb(name, shape, dt=F32):
        return es.enter_context(nc.sbuf_tensor(name, shape, dt))

    def sem(name):
        return es.enter_context(nc.semaphore(name))

    def din(name, shape):
        return nc.dram_tensor(name, shape, F32, kind="ExternalInput")

    T = Tracker()
    PE = Eng("pe", nc.tensor, sem("s_pe"))
    ACT = Eng("act", nc.scalar, sem("s_act"), selfsync=("act" in SS))
    DVE = Eng("dve", nc.vector, sem("s_dve"), selfsync=("dve" in SS))
    POOL = Eng("pool", nc.gpsimd, sem("s_pool"), selfsync=("pool" in SS))
    SP = Eng("sp", nc.sync, sem("s_sp"))
    LD = [Eng(f"ld{i}", None, sem(f"s_ld{i}"), inc=16) for i in range(NXR)]
    ST = [Eng(f"st{i}", None, sem(f"s_st{i}"), inc=16) for i in range(2)]
    LDW = Eng("ldw", None, sem("s_ldw"), inc=16)
    LDS = [Eng(f"lds{i}", None, sem(f"s_lds{i}"), inc=16) for i in range(2)]
    LDC = Eng("ldc", None, sem("s_ldc"), inc=16)
    LDT = Eng("ldt", None, sem("s_ldt"), inc=16)

    x_d = din("x", [NE * 128, D])
    valid_d = din("valid", [1, NE])
    w_in_d = din("w_in", [DEPTH, D, DIN])
    w_out_d = din("w_out", [DEPTH, D, D])
    ln_in_g_d = din("ln_in_g", [1, D]); ln_in_b_d = din("ln_in_b", [1, D])
    post_g_d = din("post_g", [DEPTH, D]); post_b_d = din("post_b", [DEPTH, D])
    gm_g_d = din("gm_g", [DEPTH, 128, 2]); gm_b_d = din("gm_b", [DEPTH, 256])
    wsT_d = din("wsT", [DEPTH, 128, 4, 128]); b_sp_d = din("b_sp", [DEPTH, 4, 128])
    conv_wT_d = din("conv_wT", [DEPTH, 128, 2, 31]); conv_b_d = din("conv_b", [DEPTH, 256])
    conv_g_d = din("conv_g", [DEPTH, 256]); conv_bb_d = din("conv_bb", [DEPTH, 256])
    sink_d = din("sink", [DEPTH, 8]); relb_d = din("rel_bias", [32, 8]); oh_d = din("oh", [33, 512])
    ident_d = din("ident", [128, 128]); jflip_d = din("jflip", [128, 128])
    out_d = nc.dram_tensor("out", [TOK_CORE, D], F32, kind="ExternalOutput")
    x1s_d = nc.dram_tensor("x1s", [NE * 128, D], F32, kind="ExternalOutput") if debug else nc.dram_tensor("x1s", [NE * 128, D], F32)
    fv_d = nc.dram_tensor("fvs", [8, 512], F32)

    win = sb("win", [128, 8, DIN], BF16)
    wout = sb("wout", [128, 8, D], BF16)
    dg = sb("dg", [128, 2, 31, 128], BF16)
    XR = [sb(f"xr{i}", [128, D]) for i in range(NXR)]
    Z = [sb(f"z{i}", [128, D]) for i in range(2)]
    xT = sb("xT", [128, D], BF16)
    mixT = sb("mixT", [128, D], BF16)
    Gin = sb("Gin", [128, D]); Bin = sb("Bin", [128, D]); Gp = sb("Gp", [128, D]); Bp = sb("Bp", [128, D])
    expb = sb("expb", [128, 3, 8, 128], BF16)
    PT = sb("PT", [128, 3, 8, 128], BF16)
    GUS = [sb(f"gus{i}", [128, 256]) for i in range(3)]
    VH = [sb(f"vh{i}", [128, 256], BF16) for i in range(3)]
    SG2B = [sb(f"sg2b{i}", [128, 256]) for i in range(3)]
    QT = [sb(f"qt{i}", [128, 512], BF16) for i in range(3)]
    SG2C = [sb(f"sg2c{i}", [128, 512]) for i in range(3)]
    KT = [sb(f"kt{i}", [128, 128], BF16) for i in range(4)]
    VA = [sb(f"va{i}", [128, 2, 65], BF16) for i in range(4)]
    YT = sb("YT", [128, 2, 4 * 128 + 30], BF16)
    tA1 = sb("tA1", [128, 256]); tA2 = sb("tA2", [128, 256]); tA3 = sb("tA3", [128, 256]); tA4 = sb("tA4", [128, 256])
    tB1 = sb("tB1", [128, 256]); tB2 = sb("tB2", [128, 256]); tC1 = sb("tC1", [128, 512])
    tSG = sb("tSG", [128, 256]); tCN = sb("tCN", [128, 256]); tCT = sb("tCT", [128, 256])
    On = sb("On", [128, 512]); YB = sb("YB", [128, 256], BF16); YC = sb("YC", [128, 512], BF16)
    ident_f = sb("ident_f", [128, 128]); ident_b = sb("ident_b", [128, 128], BF16); jflip = sb("jflip_sb", [128, 128])
    cwT = sb("cwT", [128, 2, 31]); wsT = sb("wsT_sb", [128, 4, 128], BF16); LBb = sb("LBb", [128, 256], BF16)
    bsT = sb("bsT", [128, 2, 128]); Cst = sb("Cst", [128, 2, 128]); gmg = sb("gmg", [128, 2])
    Gc = sb("Gc", [128, 256]); Bc = sb("Bc", [128, 256]); cbrow = sb("cbrow", [1, 256], BF16); ones_b = sb("ones_b", [1, 128], BF16)
    esink = sb("esink", [128, 8]); vcol = sb("vcol", [128, NE]); onescol = sb("onescol", [128, 2, 1])
    RB = sb("RB", [33, 8]); OHt = sb("OHt", [33, 512]); fvs = sb("fvs_sb", [8, 512]); hank = sb("hank", [128, 8, 128])
    mhalf = sb("mhalf", [128, 1])
    stg = [sb(f"stg{i}", [128, 1408]) for i in range(2)]
    den = sb("den", [128, 8]); rc = sb("rc", [128, 8])
    lnt = {}
    for tag in ("lnin", "lngm", "lncv", "lnpo"):
        lnt[tag] = dict(st=sb(tag + "_st", [128, 12]), mv=sb(tag + "_mv", [128, 2]), ve=sb(tag + "_ve", [128, 1]),
                        rs=sb(tag + "_rs", [128, 1]), nm=sb(tag + "_nm", [128, 1]))

    banks = [es.enter_context(nc.psum_tensor(f"bank{i}", [128, 512], F32)) for i in range(8)]
    bstate = [0]

    sstate = [0]

    def newbank(slow=False):
        if slow:
            i = sstate[0] % NSLOW
            sstate[0] += 1
        else:
            i = NSLOW + bstate[0] % (8 - NSLOW)
            bstate[0] += 1
        return banks[i], f"bank{i}"

    V, G, A_, P_ = nc.vector, nc.gpsimd, nc.scalar, nc.tensor

    def bc_rows(dram, row, n):
        return bass.AP(dram, row * n, [[0, 128], [1, n]])

    T.dma(SP, LDC, ident_f[:], ident_d.ap()[:, :], [], ["ident_f"])
    T.dma(SP, LDC, vcol[:], bc_rows(valid_d, 0, NE), [], ["vcol"])
    T.dma(SP, LDC, Gin[:], bc_rows(ln_in_g_d, 0, D), [], ["Gin"])
    T.dma(SP, LDC, Bin[:], bc_rows(ln_in_b_d, 0, D), [], ["Bin"])
    T.set_writer_latest(["ident_f", "vcol", "Gin", "Bin"], LDC)
    T.op(DVE, ["ident_f"], ["ident_b"], lambda: V.tensor_copy(out=ident_b[:], in_=ident_f[:]))
    T.op(DVE, [], ["mhalf"], lambda: V.memset(mhalf[:], -0.5))
    T.op(DVE, [], ["onescol"], lambda: V.memset(onescol[:], 1.0))
    T.op(DVE, [], ["ones_b"], lambda: V.memset(ones_b[:], 1.0))
    T.op(POOL, [], ["YT"], lambda: G.memset(YT[:], 0.0))

    def setup_bias():
        T.dma(SP, LDC, jflip[:], jflip_d.ap()[:, :], [], ["jflip"])
        T.dma(SP, LDC, OHt[:], oh_d.ap()[:, :], [], ["OHt"])
        T.op(DVE, [], ["RB"], lambda: V.memset(RB[:], 1.0))
        T.dma(SP, LDC, RB[0:32, :], relb_d.ap()[:, :], [], ["RB"])
        T.set_writer_latest(["jflip", "OHt", "RB"], LDC)
        bk, bn = newbank()
        T.op(PE, ["RB", "OHt"], [bn], lambda: P_.matmul(bk[0:8, :], lhsT=RB[0:33, 0:8], rhs=OHt[0:33, :], start=True, stop=True))
        T.op(DVE, [bn], ["fvs"], lambda: V.tensor_copy(out=fvs[:], in_=bk[0:8, :]))
        T.dma(POOL, LDT, fv_d.ap()[:, :], fvs[:], ["fvs"], ["fv_d"])
        for kc in range(3):
            T.dma(POOL, LDT, hank[:], bass.AP(fv_d, 256 - kc * 128, [[1, 128], [512, 8], [1, 128]]), ["fv_d"], ["hank"])
            for hh in range(2):
                bk, bn = newbank()
                T.op(PE, ["hank", "jflip"], [bn],
                     lambda: P_.matmul(bk[:, :], lhsT=jflip[:], rhs=hank[:, hh * 4:(hh + 1) * 4, :], start=True, stop=True))
                T.op(ACT, [bn], ["expb"],
                     lambda: A_.activation(out=expb[:, kc, hh * 4:(hh + 1) * 4, :].rearrange("p h q -> p (h q)"), in_=bk[:, :], func=AF.Copy, scale=8.0))


    WIN_NAMES = [f"win{k}" for k in range(16)]
    WOUT_NAMES = [f"wout{k}" for k in range(8)]
    stg_i = [0]
    cast_mode = ["dve"]

    def staged_cast(dst_ap, src_ap, dstname, n):
        i = stg_i[0] % 2
        stg_i[0] += 1
        T.dma(SP, LDS[i], stg[i][:, 0:n], src_ap, [], [f"stg{i}"])
        if i == 0 and cast_mode[0] == "alt":
            T.op(ACT, [f"stg{i}"], [dstname], lambda: A_.activation(out=dst_ap, in_=stg[i][:, 0:n], func=AF.Copy))
        else:
            T.op(DVE, [f"stg{i}"], [dstname], lambda: V.tensor_copy(out=dst_ap, in_=stg[i][:, 0:n]))

    pending = []

    def pump(n=1):
        for _ in range(n):
            if pending:
                pending.pop(0)()

    def queue_win(l):
        for kc in range(8):
            for hf in range(2):
                pending.append(lambda kc=kc, hf=hf: staged_cast(
                    win[:, kc, hf * 1408:(hf + 1) * 1408], w_in_d.ap()[l, kc * 128:(kc + 1) * 128, hf * 1408:(hf + 1) * 1408],
                    WIN_NAMES[kc * 2 + hf], 1408))

    def queue_wout(l):
        for kc in range(8):
            pending.append(lambda kc=kc: staged_cast(wout[:, kc, :], w_out_d.ap()[l, kc * 128:(kc + 1) * 128, :], WOUT_NAMES[kc], 1024))

    def load_layer(l):
        T.dma(POOL, LDW, wsT[:], wsT_d.ap()[l, :, :, :], [], ["wsT"])
        T.dma(POOL, LDW, LBb[:], bc_rows(gm_b_d, l, 256), [], ["LBb"])
        T.dma(POOL, LDW, cbrow[:], conv_b_d.ap()[l:l + 1, :], [], ["cbrow"])
        T.set_writer_latest(["wsT", "LBb", "cbrow"], LDW)
        T.dma(SP, LDC, cwT[:], conv_wT_d.ap()[l, :, :, :], [], ["cwT"])
        T.dma(SP, LDC, gmg[:], gm_g_d.ap()[l, :, :], [], ["gmg"])
        for h in range(4):
            T.dma(SP, LDC, bsT[(h % 2) * 64:(h % 2 + 1) * 64, h // 2, :],
                  bass.AP(b_sp_d, (l * 4 + h) * 128, [[0, 64], [1, 128]]), [], ["bsT"])
        T.dma(SP, LDC, Gc[:], bc_rows(conv_g_d, l, 256), [], ["Gc"])
        T.dma(SP, LDC, Bc[:], bc_rows(conv_bb_d, l, 256), [], ["Bc"])
        T.dma(SP, LDC, esink[:], bc_rows(sink_d, l, 8), [], ["esink"])
        T.dma(SP, LDC, Gp[:], bc_rows(post_g_d, l, D), [], ["Gp"])
        T.dma(SP, LDC, Bp[:], bc_rows(post_b_d, l, D), [], ["Bp"])
        T.set_writer_latest(["cwT", "gmg", "bsT", "Gc", "Bc", "esink", "Gp", "Bp"], LDC)
        T.op(ACT, ["esink"], ["esink"], lambda: A_.activation(out=esink[:], in_=esink[:], func=AF.Exp))
        T.op(DVE, ["gmg"], ["gmg"], lambda: V.tensor_scalar(out=gmg[:], in0=gmg[:], scalar1=0.25, scalar2=None, op0=ALU.mult))
        def dg_piece(ch, k0, k1):
            for k in range(k0, k1):
                T.op(POOL, ["ident_f", "cwT"], ["dg"],
                     lambda: G.tensor_scalar(out=dg[:, ch, k, :], in0=ident_f[:], scalar1=cwT[:, ch, k:k + 1], scalar2=0.5,
                                             op0=ALU.mult, op1=ALU.mult))
        for ch in range(2):
            for k0 in range(0, 31, 8):
                pending.append(lambda ch=ch, k0=k0: dg_piece(ch, k0, min(k0 + 8, 31)))
        bk, bn = newbank()

        def f():
            for h in range(4):
                ins = P_.matmul(bk[(h % 2) * 64:(h % 2 + 1) * 64, (h // 2) * 128:(h // 2 + 1) * 128],
                                lhsT=LBb[:, h * 64:(h + 1) * 64], rhs=wsT[:, h, :], start=True, stop=True)
            return ins
        T.op(PE, ["LBb", "wsT"], [bn], f)
        T.op(DVE, [bn, "bsT"], ["Cst"], lambda: V.tensor_tensor(out=Cst[:].rearrange("p c i -> p (c i)"), in0=bk[:, 0:256],
                                                               in1=bsT[:].rearrange("p c i -> p (c i)"), op=ALU.add))
        T.op(DVE, ["Cst"], ["Cst"], lambda: V.tensor_scalar(out=Cst[:], in0=Cst[:], scalar1=0.25, scalar2=None, op0=ALU.mult))

    def ln_a(tag, aps, srcbufs):
        t = lnt[tag]
        for i, ap in enumerate(aps):
            T.op(DVE, srcbufs, [tag + "st"], lambda: V.bn_stats(out=t["st"][:, 6 * i:6 * i + 6], in_=ap))
        k = len(aps)
        T.op(DVE, [tag + "st"], [tag + "mv"], lambda: V.bn_aggr(out=t["mv"][:, 0:2], in_=t["st"][:, 0:6 * k]))

    def ln_b(tag, eps):
        t = lnt[tag]
        T.op(POOL, [tag + "mv"], [tag + "ve"], lambda: G.tensor_scalar(out=t["ve"][:], in0=t["mv"][:, 1:2], scalar1=eps, scalar2=None, op0=ALU.add))
        T.op(POOL, [tag + "ve", "mhalf"], [tag + "rs"], lambda: G.tensor_tensor(out=t["rs"][:], in0=t["ve"][:], in1=mhalf[:], op=ALU.pow))
        T.op(POOL, [tag + "mv", tag + "rs"], [tag + "nm"],
             lambda: G.tensor_scalar(out=t["nm"][:], in0=t["mv"][:, 0:1], scalar1=t["rs"][:], scalar2=-1.0, op0=ALU.mult, op1=ALU.mult))
        return t["rs"], t["nm"], [tag + "rs", tag + "nm"]

    def issue_load(l, b):
        X = XR[b % NXR]
        if l == 0:
            T.dma(SP, LD[b % NXR], X[:], x_d.ap()[b * 128:(b + 1) * 128, :], [], [f"xr{b % NXR}"])
        else:
            T.dma(SP, LD[b % NXR], X[:], x1s_d.ap()[b * 128:(b + 1) * 128, :], [f"x1s{b}"], [f"xr{b % NXR}"])

    def lnin_pieces(l, b):
        X = XR[b % NXR]; xb = f"xr{b % NXR}"
        st = {}

        def a():
            ln_a("lnin", [X[:, 0:512], X[:, 512:1024]], [xb])

        def bb():
            rs, nm, lb = ln_b("lnin", EPS)
            T.op(ACT, [xb] + lb, [xb], lambda: A_.activation(out=X[:], in_=X[:], func=AF.Identity, scale=rs[:], bias=nm[:]))

        def c():
            T.op(DVE, [xb, "Gin"], [xb], lambda: V.tensor_tensor(out=X[:], in0=X[:], in1=Gin[:], op=ALU.mult))

        def d():
            T.op(POOL, [xb, "Bin"], [xb], lambda: G.tensor_tensor(out=X[:], in0=X[:], in1=Bin[:], op=ALU.add))
        if l != 0:
            return [lambda: None] * 4
        return [a, bb, c, d]

    def lnin_stage(l, b):
        for fn in lnin_pieces(l, b):
            fn()

    def input_pieces(l, b):
        X = XR[b % NXR]; xb = f"xr{b % NXR}"
        s2 = b % 3; r4 = b % 4
        xr = WIN_NAMES + ["xT0", "xT1"]
        st = {}

        def fm(bk, col, ci):
            for kc in range(8):
                ins = P_.matmul(bk[:, col:col + 128], lhsT=win[:, kc, ci * 128:(ci + 1) * 128], rhs=xT[:, kc * 128:(kc + 1) * 128],
                                start=(kc == 0), stop=(kc == 7))
            return ins

        def tm(bk, col, w0, n):
            for kc in range(8):
                ins = P_.matmul(bk[:, col:col + n], lhsT=xT[:, kc * 128:(kc + 1) * 128], rhs=win[:, kc, w0:w0 + n],
                                start=(kc == 0), stop=(kc == 7))
            return ins

        def p_tr():
            for half in range(2):
                bk, bn = newbank()

                def f():
                    for i in range(4):
                        kc = half * 4 + i
                        ins = P_.transpose(out=bk[:, i * 128:(i + 1) * 128], in_=X[:, kc * 128:(kc + 1) * 128], identity=ident_f[:])
                    return ins
                T.op(PE, [xb, "ident_f"], [bn], f)
                T.op(ACT, [bn], [f"xT{half}"], lambda: A_.activation(out=xT[:, half * 512:(half + 1) * 512], in_=bk[:, :], func=AF.Copy))

        def p_j4():
            b4, n4 = newbank(slow=True)
            st["b4"], st["n4"] = b4, n4

            def f4():
                fm(b4, 0, 12)
                tm(b4, 128, T_CV, 128)
                return tm(b4, 256, T_AV, 256)
            T.op(PE, xr, [n4], f4)
            T.op(DVE, [n4], [f"kt{r4}"], lambda: V.tensor_copy(out=KT[r4][:], in_=b4[:, 0:128]))
            T.op(DVE, [n4, "vcol"], [f"va{r4}"],
                 lambda: V.tensor_scalar(out=VA[r4][:, :, 0:64], in0=b4[:, 128:256].rearrange("p (g d) -> p g d", g=2),
                                         scalar1=vcol[:, b:b + 1], scalar2=None, op0=ALU.mult))
            T.op(DVE, ["onescol", "vcol"], [f"va{r4}"],
                 lambda: V.tensor_scalar(out=VA[r4][:, :, 64:65], in0=onescol[:], scalar1=vcol[:, b:b + 1], scalar2=None, op0=ALU.mult))

        def p_j2():
            b2, n2 = newbank()

            def f2():
                for i in range(4):
                    ins = fm(b2, i * 128, 4 + i)
                return ins
            T.op(PE, xr, [n2], f2)
            T.op(ACT, [n2], ["tB1"], lambda: A_.activation(out=tB1[:], in_=b2[:, 256:512], func=AF.Tanh, scale=0.5))
            c0 = 15 + r4 * 128
            T.op(DVE, ["tB1", n2], [f"YT{r4}"],
                 lambda: V.scalar_tensor_tensor(out=YT[:, :, c0:c0 + 128], in0=tB1[:].rearrange("p (c t) -> p c t", c=2), scalar=1.0,
                                                in1=b2[:, 0:256].rearrange("p (c t) -> p c t", c=2), op0=ALU.add, op1=ALU.mult))
            if b in (0, 1, NE - 2, NE - 1):
                T.op(POOL, [f"YT{r4}", "vcol"], [f"YT{r4}"],
                     lambda: G.tensor_scalar(out=YT[:, :, c0:c0 + 128], in0=YT[:, :, c0:c0 + 128], scalar1=vcol[:, b:b + 1], scalar2=None, op0=ALU.mult))
            if r4 == 3:
                T.op(POOL, ["YT3"], ["YTL"], lambda: G.tensor_copy(out=YT[:, :, 0:15], in_=YT[:, :, 512:527]))
            if r4 == 0:
                T.op(POOL, ["YT0"], ["YTR"], lambda: G.tensor_copy(out=YT[:, :, 527:542], in_=YT[:, :, 15:30]))

        def p_j3():
            b3, n3 = newbank()

            def f3():
                for i in range(4):
                    ins = fm(b3, i * 128, 8 + i)
                return ins
            T.op(PE, xr, [n3], f3)
            T.op(ACT, [n3], [f"qt{s2}"], lambda: A_.activation(out=QT[s2][:], in_=b3[:, :], func=AF.Copy))

        def p_j1pe():
            b1, n1 = newbank(slow=True)
            st["b1"], st["n1"] = b1, n1

            def f1():
                for i in range(4):
                    ins = fm(b1, i * 128, i)
                return ins
            T.op(PE, xr, [n1], f1)

        def p_j1ev():
            b1, n1 = st["b1"], st["n1"]
            au = b1[:, 0:256]; ag = b1[:, 256:512]
            T.op(ACT, [n1], ["tA1"], lambda: A_.activation(out=tA1[:], in_=au, func=AF.Square, scale=math.sqrt(GA)))
            T.op(ACT, [n1], ["tA2"], lambda: A_.activation(out=tA2[:], in_=ag, func=AF.Tanh, scale=0.5))
            T.op(DVE, ["tA1", n1], ["tA1"], lambda: V.scalar_tensor_tensor(out=tA1[:], in0=tA1[:], scalar=1.0, in1=au, op0=ALU.add, op1=ALU.mult))
            T.op(DVE, ["tA2", n1], ["tA2"], lambda: V.scalar_tensor_tensor(out=tA2[:], in0=tA2[:], scalar=1.0, in1=ag, op0=ALU.add, op1=ALU.mult))
            T.op(ACT, ["tA1"], ["tA1"], lambda: A_.activation(out=tA1[:], in_=tA1[:], func=AF.Tanh, scale=GC))
            T.op(DVE, ["tA1", n1], ["tA1"], lambda: V.scalar_tensor_tensor(out=tA1[:], in0=tA1[:], scalar=1.0, in1=au, op0=ALU.add, op1=ALU.mult))
            T.op(POOL, ["tA1", "tA2"], [f"gus{s2}"], lambda: G.tensor_tensor(out=GUS[s2][:], in0=tA1[:], in1=tA2[:], op=ALU.mult))

        def p_av():
            b4, n4 = st["b4"], st["n4"]
            av = b4[:, 256:512]
            T.op(ACT, [n4], ["tA3"], lambda: A_.activation(out=tA3[:], in_=av, func=AF.Square, scale=math.sqrt(GA)))
            T.op(DVE, ["tA3", n4], ["tA3"], lambda: V.scalar_tensor_tensor(out=tA3[:], in0=tA3[:], scalar=1.0, in1=av, op0=ALU.add, op1=ALU.mult))
            T.op(ACT, ["tA3"], ["tA3"], lambda: A_.activation(out=tA3[:], in_=tA3[:], func=AF.Tanh, scale=GC))
            T.op(DVE, ["tA3", n4], ["tA4"], lambda: V.scalar_tensor_tensor(out=tA4[:], in0=tA3[:], scalar=1.0, in1=av, op0=ALU.add, op1=ALU.mult))
            ln_a("lngm", [tA4[:]], ["tA4"])
            rs, nm, lb = ln_b("lngm", 4.0 * EPS)
            T.op(POOL, ["tA4"] + lb, [f"vh{s2}"],
                 lambda: G.tensor_scalar(out=VH[s2][:], in0=tA4[:], scalar1=rs[:], scalar2=nm[:], op0=ALU.mult, op1=ALU.add))

        def p_j5():
            b5, n5 = newbank()
            T.op(PE, xr, [n5], lambda: tm(b5, 0, T_BG, 256))
            T.op(ACT, [n5], ["tB2"], lambda: A_.activation(out=tB2[:], in_=b5[:, 0:256], func=AF.Tanh, scale=0.5))
            T.op(DVE, ["tB2", n5], [f"sg2b{s2}"],
                 lambda: V.scalar_tensor_tensor(out=SG2B[s2][:], in0=tB2[:], scalar=1.0, in1=b5[:, 0:256], op0=ALU.add, op1=ALU.mult))

        def p_j6():
            b6, n6 = newbank()
            T.op(PE, xr, [n6], lambda: tm(b6, 0, T_CG, 512))
            T.op(ACT, [n6], ["tC1"], lambda: A_.activation(out=tC1[:], in_=b6[:, :], func=AF.Tanh, scale=0.5))
            T.op(DVE, ["tC1", n6], [f"sg2c{s2}"],
                 lambda: V.scalar_tensor_tensor(out=SG2C[s2][:], in0=tC1[:], scalar=1.0, in1=b6[:, :], op0=ALU.add, op1=ALU.mult))

        def p_j1():
            p_j1pe()
            p_j1ev()

        return dict(tr=p_tr, j4=p_j4, j2=p_j2, j3=p_j3, j1=p_j1, j1pe=p_j1pe, j1ev=p_j1ev, av=p_av, j5=p_j5, j6=p_j6)

    def mix_pieces(l, e, nl_last):
        s2 = e % 3; r4 = e % 4; z2 = e % 2
        X = XR[e % NXR]; xb = f"xr{e % NXR}"
        Zt = Z[z2]; zb = f"z{z2}"
        st = {}

        def p_gconv():
            bk, bn = newbank(slow=True)
            st["bk"], st["bn"] = bk, bn

            def fg():
                for h in range(4):
                    P_.matmul(bk[(h % 2) * 64:(h % 2 + 1) * 64, (h // 2) * 128:(h // 2 + 1) * 128],
                              lhsT=VH[s2][:, h * 64:(h + 1) * 64], rhs=wsT[:, h, :], start=True, stop=True)
                for ch in range(2):
                    P_.matmul(bk[:, 256 + ch * 128:256 + (ch + 1) * 128], lhsT=ones_b[0:1, 0:128], rhs=cbrow[0:1, ch * 128:(ch + 1) * 128],
                              start=True, stop=False)
                    for k in range(31):
                        ins = P_.matmul(bk[:, 256 + ch * 128:256 + (ch + 1) * 128], lhsT=YT[:, ch, r4 * 128 + k:r4 * 128 + k + 128],
                                        rhs=dg[:, ch, k, :], start=False, stop=(k == 30))
                return ins
            ytn = [f"YT{(e - 1) % 4}", f"YT{r4}", f"YT{(e + 1) % 4}"] + (["YTL"] if r4 == 0 else []) + (["YTR"] if r4 == 3 else [])
            T.op(PE, [f"vh{s2}", "wsT", "ones_b", "cbrow", "dg"] + ytn, [bn], fg)

        def p_scores(kcs=(0, 1, 2)):
            for kc in kcs:
                ks = (e - 1 + kc) % 4
                bb = [newbank() for _ in range(2)]

                def fs():
                    for g in range(2):
                        P_.matmul(bb[g][0][:, :], lhsT=KT[ks][g * 64:(g + 1) * 64, :], rhs=QT[s2][g * 64:(g + 1) * 64, :], start=True, stop=False)
                    for g in range(2):
                        ins = P_.matmul(bb[g][0][:, :], lhsT=ident_b[:], rhs=expb[:, kc, g * 4:(g + 1) * 4, :], start=False, stop=True)
                    return ins
                T.op(PE, [f"kt{ks}", f"qt{s2}", "ident_b", "expb"], [bb[0][1], bb[1][1]], fs)
                for g in range(2):
                    T.op(ACT, [bb[g][1]], [f"PT{g}"],
                         lambda: A_.activation(out=PT[:, kc, g * 4:(g + 1) * 4, :].rearrange("p h q -> p (h q)"), in_=bb[g][0][:, :], func=AF.Exp, scale=0.125))

        def p_conv1():
            bk, bn = st["bk"], st["bn"]
            cv = bk[:, 256:512]
            ln_a("lncv", [cv], [bn])
            rs, nm, lb = ln_b("lncv", EPS)
            T.op(ACT, [bn] + lb, ["tCN"], lambda: A_.activation(out=tCN[:], in_=cv, func=AF.Identity, scale=rs[:], bias=nm[:]))

        def p_conv2():
            T.op(DVE, ["tCN", "Gc"], ["tCN"], lambda: V.tensor_tensor(out=tCN[:], in0=tCN[:], in1=Gc[:], op=ALU.mult))
            T.op(DVE, ["tCN", "Bc"], ["tCN"], lambda: V.tensor_tensor(out=tCN[:], in0=tCN[:], in1=Bc[:], op=ALU.add))
            T.op(ACT, ["tCN"], ["tCT"], lambda: A_.activation(out=tCT[:], in_=tCN[:], func=AF.Tanh, scale=0.5))

        def p_conv3():
            T.op(DVE, ["tCT", "tCN"], ["tCT"], lambda: V.scalar_tensor_tensor(out=tCT[:], in0=tCT[:], scalar=1.0, in1=tCN[:], op0=ALU.add, op1=ALU.mult))
            T.op(DVE, ["tCT", f"sg2b{s2}"], ["YB"], lambda: V.scalar_tensor_tensor(out=YB[:], in0=tCT[:], scalar=0.25, in1=SG2B[s2][:], op0=ALU.mult, op1=ALU.mult))

        def p_gmlp():
            bk, bn = st["bk"], st["bn"]
            for ch in range(2):
                T.op(DVE, [bn, "gmg", "Cst"], ["tSG"],
                     lambda: V.scalar_tensor_tensor(out=tSG[:, ch * 128:(ch + 1) * 128], in0=bk[:, ch * 128:(ch + 1) * 128], scalar=gmg[:, ch:ch + 1],
                                                    in1=Cst[:, ch, :], op0=ALU.mult, op1=ALU.add))
            T.op(POOL, ["tSG", f"gus{s2}"], ["mixA"], lambda: G.tensor_tensor(out=mixT[:, 0:256], in0=tSG[:], in1=GUS[s2][:], op=ALU.mult))

        def p_pv():
            for g in range(2):
                bo, bon = newbank()
                po = bo[:, 0:260].rearrange("p (h c) -> p h c", h=4)

                def fpv():
                    for hh in range(4):
                        for kc in range(3):
                            ks = (e - 1 + kc) % 4
                            ins = P_.matmul(po[:, hh, :], lhsT=PT[:, kc, g * 4 + hh, :], rhs=VA[ks][:, g, :], start=(kc == 0), stop=(kc == 2))
                    return ins
                T.op(PE, [f"PT{g}"] + [f"va{(e - 1 + kc) % 4}" for kc in range(3)], [bon], fpv)
                T.op(DVE, [bon, "esink"], [f"den{g}"],
                     lambda: V.tensor_tensor(out=den[:, g * 4:(g + 1) * 4].unsqueeze(2), in0=po[:, :, 64:65],
                                             in1=esink[:, g * 4:(g + 1) * 4].unsqueeze(2), op=ALU.add))
                T.op(DVE, [f"den{g}"], [f"rc{g}"], lambda: V.reciprocal(out=rc[:, g * 4:(g + 1) * 4], in_=den[:, g * 4:(g + 1) * 4]))
                T.op(DVE, [bon, f"rc{g}"], [f"On{g}"],
                     lambda: V.tensor_tensor(out=On[:, g * 256:(g + 1) * 256].rearrange("p (h d) -> p h d", h=4), in0=po[:, :, 0:64],
                                             in1=rc[:, g * 4:(g + 1) * 4].unsqueeze(2).to_broadcast([128, 4, 64]), op=ALU.mult))
                T.op(POOL, [f"On{g}", f"sg2c{s2}"], [f"YC{g}"],
                     lambda: G.tensor_tensor(out=YC[:, g * 256:(g + 1) * 256], in0=On[:, g * 256:(g + 1) * 256],
                                             in1=SG2C[s2][:, g * 256:(g + 1) * 256], op=ALU.mult))

        def p_mixtr():
            bt, btn = newbank()
            btb = bt[:].bitcast(BF16)

            def ftr():
                for c in range(2):
                    P_.transpose(out=btb[:, c * 128:(c + 1) * 128], in_=YB[:, c * 128:(c + 1) * 128], identity=ident_b[:])
                for c in range(4):
                    ins = P_.transpose(out=btb[:, (2 + c) * 128:(3 + c) * 128], in_=YC[:, c * 128:(c + 1) * 128], identity=ident_b[:])
                return ins
            T.op(PE, ["YB", "YC0", "YC1", "ident_b"], [btn], ftr)
            T.op(ACT, [btn], ["mixB"], lambda: A_.activation(out=mixT[:, 256:512], in_=btb[:, 0:256], func=AF.Copy))
            T.op(ACT, [btn], ["mixC"], lambda: A_.activation(out=mixT[:, 512:1024], in_=btb[:, 256:768], func=AF.Copy, scale=0.5))

        def p_outproj():
            for n in range(2):
                by, byn = newbank()

                def fo():
                    for kc in range(8):
                        ins = P_.matmul(by[:, :], lhsT=mixT[:, kc * 128:(kc + 1) * 128], rhs=wout[:, kc, n * 512:(n + 1) * 512],
                                        start=(kc == 0), stop=(kc == 7))
                    return ins
                T.op(PE, ["mixA", "mixB", "mixC"] + WOUT_NAMES, [byn], fo)
                T.op(DVE, [byn, xb], [zb + f"h{n}"],
                     lambda: V.scalar_tensor_tensor(out=Zt[:, n * 512:(n + 1) * 512], in0=X[:, n * 512:(n + 1) * 512], scalar=ALPHA, in1=by[:, :],
                                                    op0=ALU.mult, op1=ALU.add))

        def p_postln_a():
            ln_a("lnpo", [Zt[:, 0:512], Zt[:, 512:1024]], [zb + "h0", zb + "h1"])

        def p_postln_b():
            rs, nm, lb = ln_b("lnpo", EPS)
            T.op(ACT, [zb + "h0", zb + "h1"] + lb, [zb], lambda: A_.activation(out=Zt[:], in_=Zt[:], func=AF.Identity, scale=rs[:], bias=nm[:]))

        def p_postln_c():
            T.op(DVE, [zb, "Gp"], [zb], lambda: V.tensor_tensor(out=Zt[:], in0=Zt[:], in1=Gp[:], op=ALU.mult))

        def p_postln_d():
            T.op(POOL, [zb, "Bp"], [zb], lambda: G.tensor_tensor(out=Zt[:], in0=Zt[:], in1=Bp[:], op=ALU.add))
            if not nl_last:
                T.dma(SP, ST[z2], x1s_d.ap()[e * 128:(e + 1) * 128, :], Zt[:], [zb], [f"x1s{e}", zb + "h0", zb + "h1"])
            else:
                r0 = (e - 2) * 128
                T.dma(SP, ST[z2], out_d.ap()[r0:r0 + 128, :], Zt[:], [zb], [zb + "h0", zb + "h1"])

        return dict(gconv=p_gconv, scores=p_scores, sc0=lambda: p_scores((0,)), sc1=lambda: p_scores((1,)), sc2=lambda: p_scores((2,)), conv1=p_conv1, conv2=p_conv2, conv3=p_conv3, gmlp=p_gmlp, pv=p_pv,
                    mixtr=p_mixtr, outproj=p_outproj, postln=[p_postln_a, p_postln_b, p_postln_c, p_postln_d])

    IN_ALL = ["tr", "j4", "j2", "j3", "j1", "av", "j5", "j6"]
    STEP = STEP_ORDER.split()

    def run_input(l, b):
        p = input_pieces(l, b)
        for k in IN_ALL:
            p[k]()
            pump(1)

    for l in range(nlayers):
        first_b, last_b = l, NE - 1 - l
        nl_last = (l == nlayers - 1 and nlayers == DEPTH)
        if l == 0:
            setup_bias()
        for b in range(first_b, first_b + NXR):
            issue_load(l, b)
        if l == 0:
            queue_win(0)
            pump(16)
            cast_mode[0] = "alt"
        for b in range(first_b, first_b + NXR):
            lnin_stage(l, b)
        queue_wout(l)
        run_input(l, first_b)
        load_layer(l)
        issue_load(l, first_b + NXR)
        for b in range(first_b + 1, first_b + 3):
            run_input(l, b)
        pump(len(pending))
        deferred = [[], [], [], []]
        for e in range(l + 1, NE - 1 - l):
            pi = input_pieces(l, e + 2) if e + 2 <= last_b else None
            pm = mix_pieces(l, e, nl_last)
            for tok in STEP:
                pump(1)
                kind, name = tok.split(".")
                if kind == "m":
                    pm[name]()
                elif kind == "i":
                    if pi is not None:
                        pi[name]()
                elif kind == "d":
                    k = int(name)
                    for fn in deferred[k]:
                        fn()
                    deferred[k] = []
            for k in range(4):
                deferred[k].append(pm["postln"][k])
            if e + 4 <= last_b and e + 4 > first_b + 4:
                lp = lnin_pieces(l, e + 4)
                for k in range(4):
                    deferred[k].append(lp[k])
            if e + NXR <= last_b and e + NXR > first_b + NXR:
                issue_load(l, e + NXR)
            if e + 2 == last_b and l + 1 < nlayers:
                queue_win(l + 1)
        for k in range(4):
            for fn in deferred[k]:
                fn()
        pump(len(pending))
    T.wait_all(SP, ST)
    return nc


_PROG = {}


def t5_bucket(rel):
    nb = 16
    max_exact = 8
    ret = jnp.where(rel > 0, nb, 0)
    n = jnp.abs(rel)
    nf = jnp.maximum(n, 1).astype(jnp.float32)
    large = max_exact + (jnp.log(nf / max_exact) / math.log(128 / max_exact) * (nb - max_exact)).astype(jnp.int32)
    large = jnp.minimum(large, nb - 1)
    return ret + jnp.where(n < max_exact, n, large)


def host_consts():
    rel = 255 - np.arange(512)
    with jax.default_device(jax.devices("cpu")[0]):
        bucket = np.asarray(t5_bucket(jnp.asarray(rel, dtype=jnp.int32)))
    oh = np.zeros((33, 512), np.float32)
    oh[bucket, np.arange(512)] = 1.0
    oh[32] = np.where(np.abs(rel) <= 128, 0.0, -30000.0)
    return oh


def make_in_maps(inputs):
    f = lambda a: np.ascontiguousarray(np.asarray(a, dtype=np.float32))
    x = f(inputs["x"])
    w_in = np.ascontiguousarray(f(inputs["w_in"])[:, :, PERM])
    common = dict(
        w_in=w_in, w_out=f(inputs["w_out"]),
        ln_in_g=f(inputs["ln_in_g"]).reshape(1, D), ln_in_b=f(inputs["ln_in_b"]).reshape(1, D),
        post_g=f(inputs["post_ln_g"]), post_b=f(inputs["post_ln_b"]),
        gm_g=np.ascontiguousarray(f(inputs["gmlp_ln_g"]).reshape(DEPTH, 2, 128).transpose(0, 2, 1)),
        gm_b=f(inputs["gmlp_ln_b"]),
        wsT=np.ascontiguousarray(f(inputs["w_spatial"]).transpose(0, 3, 1, 2)),
        b_sp=f(inputs["b_spatial"]),
        conv_wT=np.ascontiguousarray(f(inputs["conv_w"]).reshape(DEPTH, 31, 2, 128).transpose(0, 3, 2, 1)),
        conv_b=f(inputs["conv_b"]), conv_g=f(inputs["conv_ln_g"]), conv_bb=f(inputs["conv_ln_b"]),
        sink=f(inputs["attn_sink"]), rel_bias=f(inputs["rel_bias"]), oh=host_consts(),
        ident=np.eye(128, dtype=np.float32), jflip=np.ascontiguousarray(np.eye(128, dtype=np.float32)[::-1]),
    )
    maps = []
    for c in range(NCORE):
        bi, sg = c // 4, c % 4
        t0 = sg * TOK_CORE - 256
        xs = np.zeros((NE * 128, D), np.float32)
        lo, hi = max(t0, 0), min(t0 + NE * 128, SEQ)
        xs[lo - t0:hi - t0] = x[bi, lo:hi]
        valid = np.zeros((1, NE), np.float32)
        for e in range(NE):
            tb = t0 + e * 128
            valid[0, e] = 1.0 if (0 <= tb < SEQ) else 0.0
        m = dict(common)
        m["x"] = xs
        m["valid"] = valid
        maps.append(m)
    return maps


def kernel(**inputs):
    if "nc" not in _PROG:
        _PROG["nc"] = build_program()
    nc = _PROG["nc"]
    maps = make_in_maps(inputs)
    res = run_bass_kernel_spmd(nc, maps, core_ids=list(range(NCORE)))
    _PROG["res"] = res
    out = np.zeros((2, SEQ, D), np.float32)
    for c in range(NCORE):
        bi, sg = c // 4, c % 4
        out[bi, sg * TOK_CORE:(sg + 1) * TOK_CORE] = res.results[c]["out"]
    return out
```

```python
import math
from contextlib import ExitStack

import numpy as np
import jax
import jax.numpy as jnp
import concourse.bass as bass
import concourse.mybir as mybir
from concourse.bass_utils import run_bass_kernel_spmd

F32 = mybir.dt.float32
BF16 = mybir.dt.bfloat16
AF = mybir.ActivationFunctionType
ALU = mybir.AluOpType

D = 1024
SEQ = 16384
NCORE = 8
TOK_CORE = 4096
NE = 36
DIN = 2816
ALPHA = 4 ** 0.25
EPS = 1e-5
GA = 0.044715
GC = 0.7978845608028654
DEPTH = 2
import os
SS = os.environ.get("KSS", "act,dve,pool").split(",")
ORDER = os.environ.get("KORDER", "mimimiml")
NSLOW = int(os.environ.get("KNSLOW", "3"))
NXR = 5
STEP_ORDER = os.environ.get("KSTEP", "m.gconv i.tr m.sc0 m.sc1 m.sc2 m.conv1 d.0 i.j4 d.1 m.conv2 m.pv d.2 m.conv3 m.gmlp i.j2 d.3 i.j3 i.j1pe i.j5 m.mixtr i.j6 m.outproj i.j1ev i.av")

_o = dict(au=0, av=256, ag=512, ba=768, bb=1024, bg=1280, cq=1536, ck=2048, cv=2176, cg=2304)
_perm = []
_perm += list(range(_o['au'], _o['au'] + 256))
_perm += list(range(_o['ag'], _o['ag'] + 256))
_perm += list(range(_o['ba'], _o['ba'] + 256))
_perm += list(range(_o['bb'], _o['bb'] + 256))
for _c in range(4):
    _perm += list(range(_o['cq'] + _c * 64, _o['cq'] + _c * 64 + 64))
    _perm += list(range(_o['cq'] + (_c + 4) * 64, _o['cq'] + (_c + 4) * 64 + 64))
_perm += list(range(_o['ck'], _o['ck'] + 128))
T_AV = len(_perm); _perm += list(range(_o['av'], _o['av'] + 256))
T_BG = len(_perm); _perm += list(range(_o['bg'], _o['bg'] + 256))
T_CV = len(_perm); _perm += list(range(_o['cv'], _o['cv'] + 128))
T_CG = len(_perm); _perm += list(range(_o['cg'], _o['cg'] + 512))
PERM = np.array(_perm)
assert len(PERM) == DIN and len(set(_perm)) == DIN


class Eng:
    def __init__(self, name, h, sem, inc=1, selfsync=False):
        self.name, self.h, self.sem, self.inc, self.selfsync = name, h, sem, inc, selfsync
        self.count = 0
        self.waited = {}


class Buf:
    __slots__ = ("w", "r")

    def __init__(self):
        self.w = None
        self.r = {}


class Tracker:
    def __init__(self):
        self.bufs = {}

    def buf(self, name):
        b = self.bufs.get(name)
        if b is None:
            b = self.bufs[name] = Buf()
        return b

    def _deps(self, reads, writes):
        need = {}
        for n in reads:
            ev = self.buf(n).w
            if ev is not None and need.get(ev[0], 0) < ev[1]:
                need[ev[0]] = ev[1]
        for n in writes:
            b = self.buf(n)
            if b.w is not None and need.get(b.w[0], 0) < b.w[1]:
                need[b.w[0]] = b.w[1]
            for e, t in b.r.items():
                if need.get(e, 0) < t:
                    need[e] = t
        return need

    def _wait(self, eng, need):
        for e, t in need.items():
            if e is eng and not eng.selfsync:
                continue
            if eng.waited.get(e, 0) >= t:
                continue
            assert t <= e.count, f"{eng.name} needs unissued tick {t} of {e.name} ({e.count})"
            eng.h.wait_ge(e.sem, t * e.inc)
            eng.waited[e] = t
            if e.inc == 16 and t > getattr(e, "max_wait", 0):
                e.max_wait = t

    def _record(self, ev_eng, tick, reads, writes):
        for n in reads:
            b = self.buf(n)
            if b.r.get(ev_eng, 0) < tick:
                b.r[ev_eng] = tick
        for n in writes:
            b = self.buf(n)
            b.w = (ev_eng, tick)
            b.r = {}

    def op(self, eng, reads, writes, fn):
        self._wait(eng, self._deps(reads, writes))
        inst = fn()
        eng.count += 1
        inst.then_inc(eng.sem, 1)
        self._record(eng, eng.count, reads, writes)

    def dma(self, q, stream, out, in_, reads, writes, **kw):
        self._wait(q, self._deps(reads, writes))
        mw = getattr(stream, "max_wait", 0)
        if mw > q.waited.get(stream, 0):
            q.h.wait_ge(stream.sem, mw * stream.inc)
            q.waited[stream] = mw
        inst = q.h.dma_start(out=out, in_=in_, **kw)
        stream.count += 1
        inst.then_inc(stream.sem, 16)
        self._record(stream, stream.count, reads, writes)

    def set_writer_latest(self, names, stream):
        for n in names:
            self.buf(n).w = (stream, stream.count)

    def wait_all(self, eng, streams):
        for s in streams:
            if s.count > 0 and eng.waited.get(s, 0) < s.count:
                eng.h.wait_ge(s.sem, s.count * s.inc)
                eng.waited[s] = s.count


def build_program(nlayers=DEPTH, debug=False):
    nc = bass.Bass("TRN2", target_bir_lowering=False)
    es = ExitStack()

    def sb(name, shape, dt=F32):
        return es.enter_context(nc.sbuf_tensor(name, shape, dt))

    def sem(name):
        return es.enter_context(nc.semaphore(name))

    def din(name, shape):
        return nc.dram_tensor(name, shape, F32, kind="ExternalInput")

    T = Tracker()
    PE = Eng("pe", nc.tensor, sem("s_pe"))
    ACT = Eng("act", nc.scalar, sem("s_act"), selfsync=("act" in SS))
    DVE = Eng("dve", nc.vector, sem("s_dve"), selfsync=("dve" in SS))
    POOL = Eng("pool", nc.gpsimd, sem("s_pool"), selfsync=("pool" in SS))
    SP = Eng("sp", nc.sync, sem("s_sp"))
    LD = [Eng(f"ld{i}", None, sem(f"s_ld{i}"), inc=16) for i in range(NXR)]
    ST = [Eng(f"st{i}", None, sem(f"s_st{i}"), inc=16) for i in range(2)]
    LDW = Eng("ldw", None, sem("s_ldw"), inc=16)
    LDS = [Eng(f"lds{i}", None, sem(f"s_lds{i}"), inc=16) for i in range(2)]
    LDC = Eng("ldc", None, sem("s_ldc"), inc=16)
    LDT = Eng("ldt", None, sem("s_ldt"), inc=16)

    x_d = din("x", [NE * 128, D])
    valid_d = din("valid", [1, NE])
    w_in_d = din("w_in", [DEPTH, D, DIN])
    w_out_d = din("w_out", [DEPTH, D, D])
    ln_in_g_d = din("ln_in_g", [1, D]); ln_in_b_d = din("ln_in_b", [1, D])
    post_g_d = din("post_g", [DEPTH, D]); post_b_d = din("post_b", [DEPTH, D])
    gm_g_d = din("gm_g", [DEPTH, 128, 2]); gm_b_d = din("gm_b", [DEPTH, 256])
    wsT_d = din("wsT", [DEPTH, 128, 4, 128]); b_sp_d = din("b_sp", [DEPTH, 4, 128])
    conv_wT_d = din("conv_wT", [DEPTH, 128, 2, 31]); conv_b_d = din("conv_b", [DEPTH, 256])
    conv_g_d = din("conv_g", [DEPTH, 256]); conv_bb_d = din("conv_bb", [DEPTH, 256])
    sink_d = din("sink", [DEPTH, 8]); relb_d = din("rel_bias", [32, 8]); oh_d = din("oh", [33, 512])
    ident_d = din("ident", [128, 128]); jflip_d = din("jflip", [128, 128])
    out_d = nc.dram_tensor("out", [TOK_CORE, D], F32, kind="ExternalOutput")
    x1s_d = nc.dram_tensor("x1s", [NE * 128, D], F32, kind="ExternalOutput") if debug else nc.dram_tensor("x1s", [NE * 128, D], F32)
    fv_d = nc.dram_tensor("fvs", [8, 512], F32)

    win = sb("win", [128, 8, DIN], BF16)
    wout = sb("wout", [128, 8, D], BF16)
    dg = sb("dg", [128, 2, 31, 128], BF16)
    XR = [sb(f"xr{i}", [128, D]) for i in range(NXR)]
    Z = [sb(f"z{i}", [128, D]) for i in range(2)]
    xT = sb("xT", [128, D], BF16)
    mixT = sb("mixT", [128, D], BF16)
    Gin = sb("Gin", [128, D]); Bin = sb("Bin", [128, D]); Gp = sb("Gp", [128, D]); Bp = sb("Bp", [128, D])
    expb = sb("expb", [128, 3, 8, 128], BF16)
    PT = sb("PT", [128, 3, 8, 128], BF16)
    GUS = [sb(f"gus{i}", [128, 256]) for i in range(3)]
    VH = [sb(f"vh{i}", [128, 256], BF16) for i in range(3)]
    SG2B = [sb(f"sg2b{i}", [128, 256]) for i in range(3)]
    QT = [sb(f"qt{i}", [128, 512], BF16) for i in range(3)]
    SG2C = [sb(f"sg2c{i}", [128, 512]) for i in range(3)]
    KT = [sb(f"kt{i}", [128, 128], BF16) for i in range(4)]
    VA = [sb(f"va{i}", [128, 2, 65], BF16) for i in range(4)]
    YT = sb("YT", [128, 2, 4 * 128 + 30], BF16)
    tA1 = sb("tA1", [128, 256]); tA2 = sb("tA2", [128, 256]); tA3 = sb("tA3", [128, 256]); tA4 = sb("tA4", [128, 256])
    tB1 = sb("tB1", [128, 256]); tB2 = sb("tB2", [128, 256]); tC1 = sb("tC1", [128, 512])
    tSG = sb("tSG", [128, 256]); tCN = sb("tCN", [128, 256]); tCT = sb("tCT", [128, 256])
    On = sb("On", [128, 512]); YB = sb("YB", [128, 256], BF16); YC = sb("YC", [128, 512], BF16)
    ident_f = sb("ident_f", [128, 128]); ident_b = sb("ident_b", [128, 128], BF16); jflip = sb("jflip_sb", [128, 128])
    cwT = sb("cwT", [128, 2, 31]); wsT = sb("wsT_sb", [128, 4, 128], BF16); LBb = sb("LBb", [128, 256], BF16)
    bsT = sb("bsT", [128, 2, 128]); Cst = sb("Cst", [128, 2, 128]); gmg = sb("gmg", [128, 2])
    Gc = sb("Gc", [128, 256]); Bc = sb("Bc", [128, 256]); cbrow = sb("cbrow", [1, 256], BF16); ones_b = sb("ones_b", [1, 128], BF16)
    esink = sb("esink", [128, 8]); vcol = sb("vcol", [128, NE]); onescol = sb("onescol", [128, 2, 1])
    RB = sb("RB", [33, 8]); OHt = sb("OHt", [33, 512]); fvs = sb("fvs_sb", [8, 512]); hank = sb("hank", [128, 8, 128])
    mhalf = sb("mhalf", [128, 1])
    stg = [sb(f"stg{i}", [128, 1408]) for i in range(2)]
    den = sb("den", [128, 8]); rc = sb("rc", [128, 8])
    lnt = {}
    for tag in ("lnin", "lngm", "lncv", "lnpo"):
        lnt[tag] = dict(st=sb(tag + "_st", [128, 12]), mv=sb(tag + "_mv", [128, 2]), ve=sb(tag + "_ve", [128, 1]),
                        rs=sb(tag + "_rs", [128, 1]), nm=sb(tag + "_nm", [128, 1]))

    banks = [es.enter_context(nc.psum_tensor(f"bank{i}", [128, 512], F32)) for i in range(8)]
    bstate = [0]

    sstate = [0]

    def newbank(slow=False):
        if slow:
            i = sstate[0] % NSLOW
            sstate[0] += 1
        else:
            i = NSLOW + bstate[0] % (8 - NSLOW)
            bstate[0] += 1
        return banks[i], f"bank{i}"

    V, G, A_, P_ = nc.vector, nc.gpsimd, nc.scalar, nc.tensor

    def bc_rows(dram, row, n):
        return bass.AP(dram, row * n, [[0, 128], [1, n]])

    T.dma(SP, LDC, ident_f[:], ident_d.ap()[:, :], [], ["ident_f"])
    T.dma(SP, LDC, vcol[:], bc_rows(valid_d, 0, NE), [], ["vcol"])
    T.dma(SP, LDC, Gin[:], bc_rows(ln_in_g_d, 0, D), [], ["Gin"])
    T.dma(SP, LDC, Bin[:], bc_rows(ln_in_b_d, 0, D), [], ["Bin"])
    T.set_writer_latest(["ident_f", "vcol", "Gin", "Bin"], LDC)
    T.op(DVE, ["ident_f"], ["ident_b"], lambda: V.tensor_copy(out=ident_b[:], in_=ident_f[:]))
    T.op(DVE, [], ["mhalf"], lambda: V.memset(mhalf[:], -0.5))
    T.op(DVE, [], ["onescol"], lambda: V.memset(onescol[:], 1.0))
    T.op(DVE, [], ["ones_b"], lambda: V.memset(ones_b[:], 1.0))
    T.op(POOL, [], ["YT"], lambda: G.memset(YT[:], 0.0))

    def setup_bias():
        T.dma(SP, LDC, jflip[:], jflip_d.ap()[:, :], [], ["jflip"])
        T.dma(SP, LDC, OHt[:], oh_d.ap()[:, :], [], ["OHt"])
        T.op(DVE, [], ["RB"], lambda: V.memset(RB[:], 1.0))
        T.dma(SP, LDC, RB[0:32, :], relb_d.ap()[:, :], [], ["RB"])
        T.set_writer_latest(["jflip", "OHt", "RB"], LDC)
        bk, bn = newbank()
        T.op(PE, ["RB", "OHt"], [bn], lambda: P_.matmul(bk[0:8, :], lhsT=RB[0:33, 0:8], rhs=OHt[0:33, :], start=True, stop=True))
        T.op(DVE, [bn], ["fvs"], lambda: V.tensor_copy(out=fvs[:], in_=bk[0:8, :]))
        T.dma(POOL, LDT, fv_d.ap()[:, :], fvs[:], ["fvs"], ["fv_d"])
        for kc in range(3):
            T.dma(POOL, LDT, hank[:], bass.AP(fv_d, 256 - kc * 128, [[1, 128], [512, 8], [1, 128]]), ["fv_d"], ["hank"])
            for hh in range(2):
                bk, bn = newbank()
                T.op(PE, ["hank", "jflip"], [bn],
                     lambda: P_.matmul(bk[:, :], lhsT=jflip[:], rhs=hank[:, hh * 4:(hh + 1) * 4, :], start=True, stop=True))
                T.op(ACT, [bn], ["expb"],
                     lambda: A_.activation(out=expb[:, kc, hh * 4:(hh + 1) * 4, :].rearrange("p h q -> p (h q)"), in_=bk[:, :], func=AF.Copy, scale=8.0))


    WIN_NAMES = [f"win{k}" for k in range(16)]
    WOUT_NAMES = [f"wout{k}" for k in range(8)]
    stg_i = [0]
    cast_mode = ["dve"]

    def staged_cast(dst_ap, src_ap, dstname, n):
        i = stg_i[0] % 2
        stg_i[0] += 1
        T.dma(SP, LDS[i], stg[i][:, 0:n], src_ap, [], [f"stg{i}"])
        if i == 0 and cast_mode[0] == "alt":
            T.op(ACT, [f"stg{i}"], [dstname], lambda: A_.activation(out=dst_ap, in_=stg[i][:, 0:n], func=AF.Copy))
        else:
            T.op(DVE, [f"stg{i}"], [dstname], lambda: V.tensor_copy(out=dst_ap, in_=stg[i][:, 0:n]))

    pending = []

    def pump(n=1):
        for _ in range(n):
            if pending:
                pending.pop(0)()

    def queue_win(l):
        for kc in range(8):
            for hf in range(2):
                pending.append(lambda kc=kc, hf=hf: staged_cast(
                    win[:, kc, hf * 1408:(hf + 1) * 1408], w_in_d.ap()[l, kc * 128:(kc + 1) * 128, hf * 1408:(hf + 1) * 1408],
                    WIN_NAMES[kc * 2 + hf], 1408))

    def queue_wout(l):
        for kc in range(8):
            pending.append(lambda kc=kc: staged_cast(wout[:, kc, :], w_out_d.ap()[l, kc * 128:(kc + 1) * 128, :], WOUT_NAMES[kc], 1024))

    def load_layer(l):
        T.dma(POOL, LDW, wsT[:], wsT_d.ap()[l, :, :, :], [], ["wsT"])
        T.dma(POOL, LDW, LBb[:], bc_rows(gm_b_d, l, 256), [], ["LBb"])
        T.dma(POOL, LDW, cbrow[:], conv_b_d.ap()[l:l + 1, :], [], ["cbrow"])
        T.set_writer_latest(["wsT", "LBb", "cbrow"], LDW)
        T.dma(SP, LDC, cwT[:], conv_wT_d.ap()[l, :, :, :], [], ["cwT"])
        T.dma(SP, LDC, gmg[:], gm_g_d.ap()[l, :, :], [], ["gmg"])
        for h in range(4):
            T.dma(SP, LDC, bsT[(h % 2) * 64:(h % 2 + 1) * 64, h // 2, :],
                  bass.AP(b_sp_d, (l * 4 + h) * 128, [[0, 64], [1, 128]]), [], ["bsT"])
        T.dma(SP, LDC, Gc[:], bc_rows(conv_g_d, l, 256), [], ["Gc"])
        T.dma(SP, LDC, Bc[:], bc_rows(conv_bb_d, l, 256), [], ["Bc"])
        T.dma(SP, LDC, esink[:], bc_rows(sink_d, l, 8), [], ["esink"])
        T.dma(SP, LDC, Gp[:], bc_rows(post_g_d, l, D), [], ["Gp"])
        T.dma(SP, LDC, Bp[:], bc_rows(post_b_d, l, D), [], ["Bp"])
        T.set_writer_latest(["cwT", "gmg", "bsT", "Gc", "Bc", "esink", "Gp", "Bp"], LDC)
        T.op(ACT, ["esink"], ["esink"], lambda: A_.activation(out=esink[:], in_=esink[:], func=AF.Exp))
        T.op(DVE, ["gmg"], ["gmg"], lambda: V.tensor_scalar(out=gmg[:], in0=gmg[:], scalar1=0.25, scalar2=None, op0=ALU.mult))
        def dg_piece(ch, k0, k1):
            for k in range(k0, k1):
                T.op(POOL, ["ident_f", "cwT"], ["dg"],
                     lambda: G.tensor_scalar(out=dg[:, ch, k, :], in0=ident_f[:], scalar1=cwT[:, ch, k:k + 1], scalar2=0.5,
                                             op0=ALU.mult, op1=ALU.mult))
        for ch in range(2):
            for k0 in range(0, 31, 8):
                pending.append(lambda ch=ch, k0=k0: dg_piece(ch, k0, min(k0 + 8, 31)))
        bk, bn = newbank()

        def f():
            for h in range(4):
                ins = P_.matmul(bk[(h % 2) * 64:(h % 2 + 1) * 64, (h // 2) * 128:(h // 2 + 1) * 128],
                                lhsT=LBb[:, h * 64:(h + 1) * 64], rhs=wsT[:, h, :], start=True, stop=True)
            return ins
        T.op(PE, ["LBb", "wsT"], [bn], f)
        T.op(DVE, [bn, "bsT"], ["Cst"], lambda: V.tensor_tensor(out=Cst[:].rearrange("p c i -> p (c i)"), in0=bk[:, 0:256],
                                                               in1=bsT[:].rearrange("p c i -> p (c i)"), op=ALU.add))
        T.op(DVE, ["Cst"], ["Cst"], lambda: V.tensor_scalar(out=Cst[:], in0=Cst[:], scalar1=0.25, scalar2=None, op0=ALU.mult))

    def ln_a(tag, aps, srcbufs):
        t = lnt[tag]
        for i, ap in enumerate(aps):
            T.op(DVE, srcbufs, [tag + "st"], lambda: V.bn_stats(out=t["st"][:, 6 * i:6 * i + 6], in_=ap))
        k = len(aps)
        T.op(DVE, [tag + "st"], [tag + "mv"], lambda: V.bn_aggr(out=t["mv"][:, 0:2], in_=t["st"][:, 0:6 * k]))

    def ln_b(tag, eps):
        t = lnt[tag]
        T.op(POOL, [tag + "mv"], [tag + "ve"], lambda: G.tensor_scalar(out=t["ve"][:], in0=t["mv"][:, 1:2], scalar1=eps, scalar2=None, op0=ALU.add))
        T.op(POOL, [tag + "ve", "mhalf"], [tag + "rs"], lambda: G.tensor_tensor(out=t["rs"][:], in0=t["ve"][:], in1=mhalf[:], op=ALU.pow))
        T.op(POOL, [tag + "mv", tag + "rs"], [tag + "nm"],
             lambda: G.tensor_scalar(out=t["nm"][:], in0=t["mv"][:, 0:1], scalar1=t["rs"][:], scalar2=-1.0, op0=ALU.mult, op1=ALU.mult))
        return t["rs"], t["nm"], [tag + "rs", tag + "nm"]

    def issue_load(l, b):
        X = XR[b % NXR]
        if l == 0:
            T.dma(SP, LD[b % NXR], X[:], x_d.ap()[b * 128:(b + 1) * 128, :], [], [f"xr{b % NXR}"])
        else:
            T.dma(SP, LD[b % NXR], X[:], x1s_d.ap()[b * 128:(b + 1) * 128, :], [f"x1s{b}"], [f"xr{b % NXR}"])

    def lnin_pieces(l, b):
        X = XR[b % NXR]; xb = f"xr{b % NXR}"
        st = {}

        def a():
            ln_a("lnin", [X[:, 0:512], X[:, 512:1024]], [xb])

        def bb():
            rs, nm, lb = ln_b("lnin", EPS)
            T.op(ACT, [xb] + lb, [xb], lambda: A_.activation(out=X[:], in_=X[:], func=AF.Identity, scale=rs[:], bias=nm[:]))

        def c():
            T.op(DVE, [xb, "Gin"], [xb], lambda: V.tensor_tensor(out=X[:], in0=X[:], in1=Gin[:], op=ALU.mult))

        def d():
            T.op(POOL, [xb, "Bin"], [xb], lambda: G.tensor_tensor(out=X[:], in0=X[:], in1=Bin[:], op=ALU.add))
        if l != 0:
            return [lambda: None] * 4
        return [a, bb, c, d]

    def lnin_stage(l, b):
        for fn in lnin_pieces(l, b):
            fn()

    def input_pieces(l, b):
        X = XR[b % NXR]; xb = f"xr{b % NXR}"
        s2 = b % 3; r4 = b % 4
        xr = WIN_NAMES + ["xT0", "xT1"]
        st = {}

        def fm(bk, col, ci):
            for kc in range(8):
                ins = P_.matmul(bk[:, col:col + 128], lhsT=win[:, kc, ci * 128:(ci + 1) * 128], rhs=xT[:, kc * 128:(kc + 1) * 128],
                                start=(kc == 0), stop=(kc == 7))
            return ins

        def tm(bk, col, w0, n):
            for kc in range(8):
                ins = P_.matmul(bk[:, col:col + n], lhsT=xT[:, kc * 128:(kc + 1) * 128], rhs=win[:, kc, w0:w0 + n],
                                start=(kc == 0), stop=(kc == 7))
            return ins

        def p_tr():
            for half in range(2):
                bk, bn = newbank()

                def f():
                    for i in range(4):
                        kc = half * 4 + i
                        ins = P_.transpose(out=bk[:, i * 128:(i + 1) * 128], in_=X[:, kc * 128:(kc + 1) * 128], identity=ident_f[:])
                    return ins
                T.op(PE, [xb, "ident_f"], [bn], f)
                T.op(ACT, [bn], [f"xT{half}"], lambda: A_.activation(out=xT[:, half * 512:(half + 1) * 512], in_=bk[:, :], func=AF.Copy))

        def p_j4():
            b4, n4 = newbank(slow=True)
            st["b4"], st["n4"] = b4, n4

            def f4():
                fm(b4, 0, 12)
                tm(b4, 128, T_CV, 128)
                return tm(b4, 256, T_AV, 256)
            T.op(PE, xr, [n4], f4)
            T.op(DVE, [n4], [f"kt{r4}"], lambda: V.tensor_copy(out=KT[r4][:], in_=b4[:, 0:128]))
            T.op(DVE, [n4, "vcol"], [f"va{r4}"],
                 lambda: V.tensor_scalar(out=VA[r4][:, :, 0:64], in0=b4[:, 128:256].rearrange("p (g d) -> p g d", g=2),
                                         scalar1=vcol[:, b:b + 1], scalar2=None, op0=ALU.mult))
            T.op(DVE, ["onescol", "vcol"], [f"va{r4}"],
                 lambda: V.tensor_scalar(out=VA[r4][:, :, 64:65], in0=onescol[:], scalar1=vcol[:, b:b + 1], scalar2=None, op0=ALU.mult))

        def p_j2():
            b2, n2 = newbank()

            def f2():
                for i in range(4):
                    ins = fm(b2, i * 128, 4 + i)
                return ins
            T.op(PE, xr, [n2], f2)
            T.op(ACT, [n2], ["tB1"], lambda: A_.activation(out=tB1[:], in_=b2[:, 256:512], func=AF.Tanh, scale=0.5))
            c0 = 15 + r4 * 128
            T.op(DVE, ["tB1", n2], [f"YT{r4}"],
                 lambda: V.scalar_tensor_tensor(out=YT[:, :, c0:c0 + 128], in0=tB1[:].rearrange("p (c t) -> p c t", c=2), scalar=1.0,
                                                in1=b2[:, 0:256].rearrange("p (c t) -> p c t", c=2), op0=ALU.add, op1=ALU.mult))
            if b in (0, 1, NE - 2, NE - 1):
                T.op(POOL, [f"YT{r4}", "vcol"], [f"YT{r4}"],
                     lambda: G.tensor_scalar(out=YT[:, :, c0:c0 + 128], in0=YT[:, :, c0:c0 + 128], scalar1=vcol[:, b:b + 1], scalar2=None, op0=ALU.mult))
            if r4 == 3:
                T.op(POOL, ["YT3"], ["YTL"], lambda: G.tensor_copy(out=YT[:, :, 0:15], in_=YT[:, :, 512:527]))
            if r4 == 0:
                T.op(POOL, ["YT0"], ["YTR"], lambda: G.tensor_copy(out=YT[:, :, 527:542], in_=YT[:, :, 15:30]))

        def p_j3():
            b3, n3 = newbank()

            def f3():
                for i in range(4):
                    ins = fm(b3, i * 128, 8 + i)
                return ins
            T.op(PE, xr, [n3], f3)
            T.op(ACT, [n3], [f"qt{s2}"], lambda: A_.activation(out=QT[s2][:], in_=b3[:, :], func=AF.Copy))

        def p_j1pe():
            b1, n1 = newbank(slow=True)
            st["b1"], st["n1"] = b1, n1

            def f1():
                for i in range(4):
                    ins = fm(b1, i * 128, i)
                return ins
            T.op(PE, xr, [n1], f1)

        def p_j1ev():
            b1, n1 = st["b1"], st["n1"]
            au = b1[:, 0:256]; ag = b1[:, 256:512]
            T.op(ACT, [n1], ["tA1"], lambda: A_.activation(out=tA1[:], in_=au, func=AF.Square, scale=math.sqrt(GA)))
            T.op(ACT, [n1], ["tA2"], lambda: A_.activation(out=tA2[:], in_=ag, func=AF.Tanh, scale=0.5))
            T.op(DVE, ["tA1", n1], ["tA1"], lambda: V.scalar_tensor_tensor(out=tA1[:], in0=tA1[:], scalar=1.0, in1=au, op0=ALU.add, op1=ALU.mult))
            T.op(DVE, ["tA2", n1], ["tA2"], lambda: V.scalar_tensor_tensor(out=tA2[:], in0=tA2[:], scalar=1.0, in1=ag, op0=ALU.add, op1=ALU.mult))
            T.op(ACT, ["tA1"], ["tA1"], lambda: A_.activation(out=tA1[:], in_=tA1[:], func=AF.Tanh, scale=GC))
            T.op(DVE, ["tA1", n1], ["tA1"], lambda: V.scalar_tensor_tensor(out=tA1[:], in0=tA1[:], scalar=1.0, in1=au, op0=ALU.add, op1=ALU.mult))
            T.op(POOL, ["tA1", "tA2"], [f"gus{s2}"], lambda: G.tensor_tensor(out=GUS[s2][:], in0=tA1[:], in1=tA2[:], op=ALU.mult))

        def p_av():
            b4, n4 = st["b4"], st["n4"]
            av = b4[:, 256:512]
            T.op(ACT, [n4], ["tA3"], lambda: A_.activation(out=tA3[:], in_=av, func=AF.Square, scale=math.sqrt(GA)))
            T.op(DVE, ["tA3", n4], ["tA3"], lambda: V.scalar_tensor_tensor(out=tA3[:], in0=tA3[:], scalar=1.0, in1=av, op0=ALU.add, op1=ALU.mult))
            T.op(ACT, ["tA3"], ["tA3"], lambda: A_.activation(out=tA3[:], in_=tA3[:], func=AF.Tanh, scale=GC))
            T.op(DVE, ["tA3", n4], ["tA4"], lambda: V.scalar_tensor_tensor(out=tA4[:], in0=tA3[:], scalar=1.0, in1=av, op0=ALU.add, op1=ALU.mult))
            ln_a("lngm", [tA4[:]], ["tA4"])
            rs, nm, lb = ln_b("lngm", 4.0 * EPS)
            T.op(POOL, ["tA4"] + lb, [f"vh{s2}"],
                 lambda: G.tensor_scalar(out=VH[s2][:], in0=tA4[:], scalar1=rs[:], scalar2=nm[:], op0=ALU.mult, op1=ALU.add))

        def p_j5():
            b5, n5 = newbank()
            T.op(PE, xr, [n5], lambda: tm(b5, 0, T_BG, 256))
            T.op(ACT, [n5], ["tB2"], lambda: A_.activation(out=tB2[:], in_=b5[:, 0:256], func=AF.Tanh, scale=0.5))
            T.op(DVE, ["tB2", n5], [f"sg2b{s2}"],
                 lambda: V.scalar_tensor_tensor(out=SG2B[s2][:], in0=tB2[:], scalar=1.0, in1=b5[:, 0:256], op0=ALU.add, op1=ALU.mult))

        def p_j6():
            b6, n6 = newbank()
            T.op(PE, xr, [n6], lambda: tm(b6, 0, T_CG, 512))
            T.op(ACT, [n6], ["tC1"], lambda: A_.activation(out=tC1[:], in_=b6[:, :], func=AF.Tanh, scale=0.5))
            T.op(DVE, ["tC1", n6], [f"sg2c{s2}"],
                 lambda: V.scalar_tensor_tensor(out=SG2C[s2][:], in0=tC1[:], scalar=1.0, in1=b6[:, :], op0=ALU.add, op1=ALU.mult))

        def p_j1():
            p_j1pe()
            p_j1ev()

        return dict(tr=p_tr, j4=p_j4, j2=p_j2, j3=p_j3, j1=p_j1, j1pe=p_j1pe, j1ev=p_j1ev, av=p_av, j5=p_j5, j6=p_j6)

    def mix_pieces(l, e, nl_last):
        s2 = e % 3; r4 = e % 4; z2 = e % 2
        X = XR[e % NXR]; xb = f"xr{e % NXR}"
        Zt = Z[z2]; zb = f"z{z2}"
        st = {}

        def p_gconv():
            bk, bn = newbank(slow=True)
            st["bk"], st["bn"] = bk, bn

            def fg():
                for h in range(4):
                    P_.matmul(bk[(h % 2) * 64:(h % 2 + 1) * 64, (h // 2) * 128:(h // 2 + 1) * 128],
                              lhsT=VH[s2][:, h * 64:(h + 1) * 64], rhs=wsT[:, h, :], start=True, stop=True)
                for ch in range(2):
                    P_.matmul(bk[:, 256 + ch * 128:256 + (ch + 1) * 128], lhsT=ones_b[0:1, 0:128], rhs=cbrow[0:1, ch * 128:(ch + 1) * 128],
                              start=True, stop=False)
                    for k in range(31):
                        ins = P_.matmul(bk[:, 256 + ch * 128:256 + (ch + 1) * 128], lhsT=YT[:, ch, r4 * 128 + k:r4 * 128 + k + 128],
                                        rhs=dg[:, ch, k, :], start=False, stop=(k == 30))
                return ins
            ytn = [f"YT{(e - 1) % 4}", f"YT{r4}", f"YT{(e + 1) % 4}"] + (["YTL"] if r4 == 0 else []) + (["YTR"] if r4 == 3 else [])
            T.op(PE, [f"vh{s2}", "wsT", "ones_b", "cbrow", "dg"] + ytn, [bn], fg)

        def p_scores(kcs=(0, 1, 2)):
            for kc in kcs:
                ks = (e - 1 + kc) % 4
                bb = [newbank() for _ in range(2)]

                def fs():
                    for g in range(2):
                        P_.matmul(bb[g][0][:, :], lhsT=KT[ks][g * 64:(g + 1) * 64, :], rhs=QT[s2][g * 64:(g + 1) * 64, :], start=True, stop=False)
                    for g in range(2):
                        ins = P_.matmul(bb[g][0][:, :], lhsT=ident_b[:], rhs=expb[:, kc, g * 4:(g + 1) * 4, :], start=False, stop=True)
                    return ins
                T.op(PE, [f"kt{ks}", f"qt{s2}", "ident_b", "expb"], [bb[0][1], bb[1][1]], fs)
                for g in range(2):
                    T.op(ACT, [bb[g][1]], [f"PT{g}"],
                         lambda: A_.activation(out=PT[:, kc, g * 4:(g + 1) * 4, :].rearrange("p h q -> p (h q)"), in_=bb[g][0][:, :], func=AF.Exp, scale=0.125))

        def p_conv1():
            bk, bn = st["bk"], st["bn"]
            cv = bk[:, 256:512]
            ln_a("lncv", [cv], [bn])
            rs, nm, lb = ln_b("lncv", EPS)
            T.op(ACT, [bn] + lb, ["tCN"], lambda: A_.activation(out=tCN[:], in_=cv, func=AF.Identity, scale=rs[:], bias=nm[:]))

        def p_conv2():
            T.op(DVE, ["tCN", "Gc"], ["tCN"], lambda: V.tensor_tensor(out=tCN[:], in0=tCN[:], in1=Gc[:], op=ALU.mult))
            T.op(DVE, ["tCN", "Bc"], ["tCN"], lambda: V.tensor_tensor(out=tCN[:], in0=tCN[:], in1=Bc[:], op=ALU.add))
            T.op(ACT, ["tCN"], ["tCT"], lambda: A_.activation(out=tCT[:], in_=tCN[:], func=AF.Tanh, scale=0.5))

        def p_conv3():
            T.op(DVE, ["tCT", "tCN"], ["tCT"], lambda: V.scalar_tensor_tensor(out=tCT[:], in0=tCT[:], scalar=1.0, in1=tCN[:], op0=ALU.add, op1=ALU.mult))
            T.op(DVE, ["tCT", f"sg2b{s2}"], ["YB"], lambda: V.scalar_tensor_tensor(out=YB[:], in0=tCT[:], scalar=0.25, in1=SG2B[s2][:], op0=ALU.mult, op1=ALU.mult))

        def p_gmlp():
            bk, bn = st["bk"], st["bn"]
            for ch in range(2):
                T.op(DVE, [bn, "gmg", "Cst"], ["tSG"],
                     lambda: V.scalar_tensor_tensor(out=tSG[:, ch * 128:(ch + 1) * 128], in0=bk[:, ch * 128:(ch + 1) * 128], scalar=gmg[:, ch:ch + 1],
                                                    in1=Cst[:, ch, :], op0=ALU.mult, op1=ALU.add))
            T.op(POOL, ["tSG", f"gus{s2}"], ["mixA"], lambda: G.tensor_tensor(out=mixT[:, 0:256], in0=tSG[:], in1=GUS[s2][:], op=ALU.mult))

        def p_pv():
            for g in range(2):
                bo, bon = newbank()
                po = bo[:, 0:260].rearrange("p (h c) -> p h c", h=4)

                def fpv():
                    for hh in range(4):
                        for kc in range(3):
                            ks = (e - 1 + kc) % 4
                            ins = P_.matmul(po[:, hh, :], lhsT=PT[:, kc, g * 4 + hh, :], rhs=VA[ks][:, g, :], start=(kc == 0), stop=(kc == 2))
                    return ins
                T.op(PE, [f"PT{g}"] + [f"va{(e - 1 + kc) % 4}" for kc in range(3)], [bon], fpv)
                T.op(DVE, [bon, "esink"], [f"den{g}"],
                     lambda: V.tensor_tensor(out=den[:, g * 4:(g + 1) * 4].unsqueeze(2), in0=po[:, :, 64:65],
                                             in1=esink[:, g * 4:(g + 1) * 4].unsqueeze(2), op=ALU.add))
                T.op(DVE, [f"den{g}"], [f"rc{g}"], lambda: V.reciprocal(out=rc[:, g * 4:(g + 1) * 4], in_=den[:, g * 4:(g + 1) * 4]))
                T.op(DVE, [bon, f"rc{g}"], [f"On{g}"],
                     lambda: V.tensor_tensor(out=On[:, g * 256:(g + 1) * 256].rearrange("p (h d) -> p h d", h=4), in0=po[:, :, 0:64],
                                             in1=rc[:, g * 4:(g + 1) * 4].unsqueeze(2).to_broadcast([128, 4, 64]), op=ALU.mult))
                T.op(POOL, [f"On{g}", f"sg2c{s2}"], [f"YC{g}"],
                     lambda: G.tensor_tensor(out=YC[:, g * 256:(g + 1) * 256], in0=On[:, g * 256:(g + 1) * 256],
                                             in1=SG2C[s2][:, g * 256:(g + 1) * 256], op=ALU.mult))

        def p_mixtr():
            bt, btn = newbank()
            btb = bt[:].bitcast(BF16)

            def ftr():
                for c in range(2):
                    P_.transpose(out=btb[:, c * 128:(c + 1) * 128], in_=YB[:, c * 128:(c + 1) * 128], identity=ident_b[:])
                for c in range(4):
                    ins = P_.transpose(out=btb[:, (2 + c) * 128:(3 + c) * 128], in_=YC[:, c * 128:(c + 1) * 128], identity=ident_b[:])
                return ins
            T.op(PE, ["YB", "YC0", "YC1", "ident_b"], [btn], ftr)
            T.op(ACT, [btn], ["mixB"], lambda: A_.activation(out=mixT[:, 256:512], in_=btb[:, 0:256], func=AF.Copy))
            T.op(ACT, [btn], ["mixC"], lambda: A_.activation(out=mixT[:, 512:1024], in_=btb[:, 256:768], func=AF.Copy, scale=0.5))

        def p_outproj():
            for n in range(2):
                by, byn = newbank()

                def fo():
                    for kc in range(8):
                        ins = P_.matmul(by[:, :], lhsT=mixT[:, kc * 128:(kc + 1) * 128], rhs=wout[:, kc, n * 512:(n + 1) * 512],
                                        start=(kc == 0), stop=(kc == 7))
                    return ins
                T.op(PE, ["mixA", "mixB", "mixC"] + WOUT_NAMES, [byn], fo)
                T.op(DVE, [byn, xb], [zb + f"h{n}"],
                     lambda: V.scalar_tensor_tensor(out=Zt[:, n * 512:(n + 1) * 512], in0=X[:, n * 512:(n + 1) * 512], scalar=ALPHA, in1=by[:, :],
                                                    op0=ALU.mult, op1=ALU.add))

        def p_postln_a():
            ln_a("lnpo", [Zt[:, 0:512], Zt[:, 512:1024]], [zb + "h0", zb + "h1"])

        def p_postln_b():
            rs, nm, lb = ln_b("lnpo", EPS)
            T.op(ACT, [zb + "h0", zb + "h1"] + lb, [zb], lambda: A_.activation(out=Zt[:], in_=Zt[:], func=AF.Identity, scale=rs[:], bias=nm[:]))

        def p_postln_c():
            T.op(DVE, [zb, "Gp"], [zb], lambda: V.tensor_tensor(out=Zt[:], in0=Zt[:], in1=Gp[:], op=ALU.mult))

        def p_postln_d():
            T.op(POOL, [zb, "Bp"], [zb], lambda: G.tensor_tensor(out=Zt[:], in0=Zt[:], in1=Bp[:], op=ALU.add))
            if not nl_last:
                T.dma(SP, ST[z2], x1s_d.ap()[e * 128:(e + 1) * 128, :], Zt[:], [zb], [f"x1s{e}", zb + "h0", zb + "h1"])
            else:
                r0 = (e - 2) * 128
                T.dma(SP, ST[z2], out_d.ap()[r0:r0 + 128, :], Zt[:], [zb], [zb + "h0", zb + "h1"])

        return dict(gconv=p_gconv, scores=p_scores, sc0=lambda: p_scores((0,)), sc1=lambda: p_scores((1,)), sc2=lambda: p_scores((2,)), conv1=p_conv1, conv2=p_conv2, conv3=p_conv3, gmlp=p_gmlp, pv=p_pv,
                    mixtr=p_mixtr, outproj=p_outproj, postln=[p_postln_a, p_postln_b, p_postln_c, p_postln_d])

    IN_ALL = ["tr", "j4", "j2", "j3", "j1", "av", "j5", "j6"]
    STEP = STEP_ORDER.split()

    def run_input(l, b):
        p = input_pieces(l, b)
        for k in IN_ALL:
            p[k]()
            pump(1)

    for l in range(nlayers):
        first_b, last_b = l, NE - 1 - l
        nl_last = (l == nlayers - 1 and nlayers == DEPTH)
        if l == 0:
            setup_bias()
        for b in range(first_b, first_b + NXR):
            issue_load(l, b)
        if l == 0:
            queue_win(0)
            pump(16)
            cast_mode[0] = "alt"
        for b in range(first_b, first_b + NXR):
            lnin_stage(l, b)
        queue_wout(l)
        run_input(l, first_b)
        load_layer(l)
        issue_load(l, first_b + NXR)
        for b in range(first_b + 1, first_b + 3):
            run_input(l, b)
        pump(len(pending))
        deferred = [[], [], [], []]
        for e in range(l + 1, NE - 1 - l):
            pi = input_pieces(l, e + 2) if e + 2 <= last_b else None
            pm = mix_pieces(l, e, nl_last)
            for tok in STEP:
                pump(1)
                kind, name = tok.split(".")
                if kind == "m":
                    pm[name]()
                elif kind == "i":
                    if pi is not None:
                        pi[name]()
                elif kind == "d":
                    k = int(name)
                    for fn in deferred[k]:
                        fn()
                    deferred[k] = []
            for k in range(4):
                deferred[k].append(pm["postln"][k])
            if e + 4 <= last_b and e + 4 > first_b + 4:
                lp = lnin_pieces(l, e + 4)
                for k in range(4):
                    deferred[k].append(lp[k])
            if e + NXR <= last_b and e + NXR > first_b + NXR:
                issue_load(l, e + NXR)
            if e + 2 == last_b and l + 1 < nlayers:
                queue_win(l + 1)
        for k in range(4):
            for fn in deferred[k]:
                fn()
        pump(len(pending))
    T.wait_all(SP, ST)
    return nc


_PROG = {}


def t5_bucket(rel):
    nb = 16
    max_exact = 8
    ret = jnp.where(rel > 0, nb, 0)
    n = jnp.abs(rel)
    nf = jnp.maximum(n, 1).astype(jnp.float32)
    large = max_exact + (jnp.log(nf / max_exact) / math.log(128 / max_exact) * (nb - max_exact)).astype(jnp.int32)
    large = jnp.minimum(large, nb - 1)
    return ret + jnp.where(n < max_exact, n, large)


def host_consts():
    rel = 255 - np.arange(512)
    with jax.default_device(jax.devices("cpu")[0]):
        bucket = np.asarray(t5_bucket(jnp.asarray(rel, dtype=jnp.int32)))
    oh = np.zeros((33, 512), np.float32)
    oh[bucket, np.arange(512)] = 1.0
    oh[32] = np.where(np.abs(rel) <= 128, 0.0, -30000.0)
    return oh


def make_in_maps(inputs):
    f = lambda a: np.ascontiguousarray(np.asarray(a, dtype=np.float32))
    x = f(inputs["x"])
    w_in = np.ascontiguousarray(f(inputs["w_in"])[:, :, PERM])
    common = dict(
        w_in=w_in, w_out=f(inputs["w_out"]),
        ln_in_g=f(inputs["ln_in_g"]).reshape(1, D), ln_in_b=f(inputs["ln_in_b"]).reshape(1, D),
        post_g=f(inputs["post_ln_g"]), post_b=f(inputs["post_ln_b"]),
        gm_g=np.ascontiguousarray(f(inputs["gmlp_ln_g"]).reshape(DEPTH, 2, 128).transpose(0, 2, 1)),
        gm_b=f(inputs["gmlp_ln_b"]),
        wsT=np.ascontiguousarray(f(inputs["w_spatial"]).transpose(0, 3, 1, 2)),
        b_sp=f(inputs["b_spatial"]),
        conv_wT=np.ascontiguousarray(f(inputs["conv_w"]).reshape(DEPTH, 31, 2, 128).transpose(0, 3, 2, 1)),
        conv_b=f(inputs["conv_b"]), conv_g=f(inputs["conv_ln_g"]), conv_bb=f(inputs["conv_ln_b"]),
        sink=f(inputs["attn_sink"]), rel_bias=f(inputs["rel_bias"]), oh=host_consts(),
        ident=np.eye(128, dtype=np.float32), jflip=np.ascontiguousarray(np.eye(128, dtype=np.float32)[::-1]),
    )
    maps = []
    for c in range(NCORE):
        bi, sg = c // 4, c % 4
        t0 = sg * TOK_CORE - 256
        xs = np.zeros((NE * 128, D), np.float32)
        lo, hi = max(t0, 0), min(t0 + NE * 128, SEQ)
        xs[lo - t0:hi - t0] = x[bi, lo:hi]
        valid = np.zeros((1, NE), np.float32)
        for e in range(NE):
            tb = t0 + e * 128
            valid[0, e] = 1.0 if (0 <= tb < SEQ) else 0.0
        m = dict(common)
        m["x"] = xs
        m["valid"] = valid
        maps.append(m)
    return maps


def kernel(**inputs):
    if "nc" not in _PROG:
        _PROG["nc"] = build_program()
    nc = _PROG["nc"]
    maps = make_in_maps(inputs)
    res = run_bass_kernel_spmd(nc, maps, core_ids=list(range(NCORE)))
    _PROG["res"] = res
    out = np.zeros((2, SEQ, D), np.float32)
    for c in range(NCORE):
        bi, sg = c // 4, c % 4
        out[bi, sg * TOK_CORE:(sg + 1) * TOK_CORE] = res.results[c]["out"]
    return out
```
